# Optimizing a Trainium2 kernel written in Bass

```python
import math
import jax, jax.numpy as jnp
from jax import lax
import numpy as np

D_MODEL = 1024
BATCH = 8
SEQ = 2048
DEPTH = 4

N_A = DEPTH // 2
N_B = DEPTH - N_A
D_FF = 4 * D_MODEL
CONV_WIDTH = 31
HEAD_DIM = 64
N_HEADS = D_MODEL // (2 * HEAD_DIM)
QK_DIM = 2 * N_HEADS * HEAD_DIM
V_DIM = N_HEADS * 2 * HEAD_DIM
NUM_BUCKETS = 32
MAX_DISTANCE = 128
PLE_DIM = 256
Q_BLOCK = 128
RMS_EPS = 1e-6
LN_EPS = 1e-5
NEG_INF = -1e30

kernel_name = "yoco_conformer_diffattn_macaron"


def rms_norm(x, g, eps=RMS_EPS):
    xf = x.astype(jnp.float32)
    y = xf * lax.rsqrt(jnp.mean(xf * xf, axis=-1, keepdims=True) + eps)
    return (y * g.astype(jnp.float32)).astype(x.dtype)


def layer_norm(x, g, b, eps=LN_EPS):
    xf = x.astype(jnp.float32)
    mu = jnp.mean(xf, axis=-1, keepdims=True)
    var = jnp.mean(jnp.square(xf - mu), axis=-1, keepdims=True)
    y = (xf - mu) * lax.rsqrt(var + eps) * g.astype(jnp.float32) + b.astype(jnp.float32)
    return y.astype(x.dtype)


def swiglu_ffn(h, w_in, w_out):
    g, u = jnp.split(h @ w_in, 2, axis=-1)
    return (jax.nn.silu(g) * u) @ w_out


def conformer_conv(h, w_in, b_in, w_dw, b_dw, ln_g, ln_b, w_out, b_out):
    a, gate = jnp.split(h @ w_in + b_in, 2, axis=-1)
    u = a * jax.nn.sigmoid(gate)
    u = lax.conv_general_dilated(
        u, w_dw[:, None, :].astype(u.dtype), window_strides=(1,),
        padding=((CONV_WIDTH - 1, 0),),
        dimension_numbers=("NWC", "WIO", "NWC"),
        feature_group_count=D_MODEL) + b_dw
    u = jax.nn.silu(layer_norm(u, ln_g, ln_b))
    return u @ w_out + b_out


def rel_bucket(q_pos, k_pos):
    n = jnp.maximum(q_pos[:, None] - k_pos[None, :], 0)
    max_exact = NUM_BUCKETS // 2
    nf = jnp.maximum(n, 1).astype(jnp.float32)
    large = max_exact + (jnp.log(nf / max_exact) / math.log(MAX_DISTANCE / max_exact)
                         * (NUM_BUCKETS - max_exact)).astype(jnp.int32)
    large = jnp.minimum(large, NUM_BUCKETS - 1)
    return jnp.where(n < max_exact, n, large)


def diff_attention(h, k, v, w_q, lq1, lk1, lq2, lk2, subln, w_o, rel_bias, lambda_init):
    B, S, _ = h.shape
    nb = S // Q_BLOCK
    scale = HEAD_DIM ** -0.5
    q = (h @ w_q).reshape(B, nb, Q_BLOCK, 2, N_HEADS, HEAD_DIM).transpose(1, 0, 2, 3, 4, 5)
    k1, k2 = k[:, :, 0], k[:, :, 1]
    f32 = jnp.float32
    lam = (jnp.exp(jnp.sum(lq1.astype(f32) * lk1.astype(f32)))
           - jnp.exp(jnp.sum(lq2.astype(f32) * lk2.astype(f32))) + lambda_init)
    k_pos = jnp.arange(S)

    def block(args):
        i, qblk = args
        q_pos = i * Q_BLOCK + jnp.arange(Q_BLOCK)
        bias = rel_bias[rel_bucket(q_pos, k_pos)].astype(f32).transpose(2, 0, 1)
        mask = k_pos[None, :] <= q_pos[:, None]

        def probs(qh, kh):
            s = jnp.einsum("bqhd,bkhd->bhqk", qh, kh).astype(f32) * scale + bias
            return jax.nn.softmax(jnp.where(mask, s, NEG_INF), axis=-1)

        a = probs(qblk[:, :, 0], k1) - lam * probs(qblk[:, :, 1], k2)
        return jnp.einsum("bhqk,bkhe->bqhe", a, v)

    o = lax.map(block, (jnp.arange(nb), q))
    o = o.transpose(1, 0, 2, 3, 4).reshape(B, S, N_HEADS, 2 * HEAD_DIM)
    o = rms_norm(o, subln) * (1.0 - lambda_init)
    return o.reshape(B, S, V_DIM).astype(h.dtype) @ w_o


def setup_inputs(seed: int = 0) -> dict:
    key = jax.random.key(seed)
    ks = iter(jax.random.split(key, 48))
    f32 = jnp.float32

    def nrm(shape, fan_in):
        return jax.random.normal(next(ks), shape, f32) * (fan_in ** -0.5)

    def gain(shape):
        return 1.0 + 0.02 * jax.random.normal(next(ks), shape, f32)

    def small(shape, s=0.02):
        return s * jax.random.normal(next(ks), shape, f32)

    return {
        "x": jax.random.normal(next(ks), (BATCH, SEQ, D_MODEL), f32),
        "p": jax.random.normal(next(ks), (DEPTH, BATCH, SEQ, PLE_DIM), f32),
        "ffn1_norm": gain((DEPTH, D_MODEL)),
        "ffn1_w_in": nrm((DEPTH, D_MODEL, 2 * D_FF), D_MODEL),
        "ffn1_w_out": nrm((DEPTH, D_FF, D_MODEL), D_FF),
        "mix_norm": gain((DEPTH, D_MODEL)),
        "ffn2_norm": gain((DEPTH, D_MODEL)),
        "ffn2_w_in": nrm((DEPTH, D_MODEL, 2 * D_FF), D_MODEL),
        "ffn2_w_out": nrm((DEPTH, D_FF, D_MODEL), D_FF),
        "ple_norm": gain((DEPTH, D_MODEL)),
        "ple_w_gate": nrm((DEPTH, D_MODEL, D_MODEL), D_MODEL),
        "ple_w_proj": nrm((DEPTH, PLE_DIM, D_MODEL), PLE_DIM),
        "conv_w_in": nrm((N_A, D_MODEL, 2 * D_MODEL), D_MODEL),
        "conv_b_in": small((N_A, 2 * D_MODEL)),
        "conv_w_dw": nrm((N_A, CONV_WIDTH, D_MODEL), CONV_WIDTH),
        "conv_b_dw": small((N_A, D_MODEL)),
        "conv_ln_g": gain((N_A, D_MODEL)),
        "conv_ln_b": small((N_A, D_MODEL)),
        "conv_w_out": nrm((N_A, D_MODEL, D_MODEL), D_MODEL),
        "conv_b_out": small((N_A, D_MODEL)),
        "kv_norm": gain((D_MODEL,)),
        "w_kv": nrm((D_MODEL, QK_DIM + V_DIM), D_MODEL),
        "attn_w_q": nrm((N_B, D_MODEL, QK_DIM), D_MODEL),
        "attn_lq1": small((N_B, HEAD_DIM), 0.1),
        "attn_lk1": small((N_B, HEAD_DIM), 0.1),
        "attn_lq2": small((N_B, HEAD_DIM), 0.1),
        "attn_lk2": small((N_B, HEAD_DIM), 0.1),
        "attn_subln": gain((N_B, 2 * HEAD_DIM)),
        "attn_w_o": nrm((N_B, V_DIM, D_MODEL), V_DIM),
        "rel_bias": small((NUM_BUCKETS, N_HEADS), 0.5),
        "final_norm": gain((D_MODEL,)),
    }


def reference(x, p, ffn1_norm, ffn1_w_in, ffn1_w_out, mix_norm, ffn2_norm, ffn2_w_in, ffn2_w_out,
              ple_norm, ple_w_gate, ple_w_proj, conv_w_in, conv_b_in, conv_w_dw, conv_b_dw,
              conv_ln_g, conv_ln_b, conv_w_out, conv_b_out, kv_norm, w_kv, attn_w_q,
              attn_lq1, attn_lk1, attn_lq2, attn_lk2, attn_subln, attn_w_o, rel_bias, final_norm):
    B, S, _ = x.shape
    h = x
    k_shared = None
    v_shared = None
    for i in range(DEPTH):
        if i == N_A:
            kv = rms_norm(h, kv_norm) @ w_kv
            k_shared = kv[..., :QK_DIM].reshape(B, S, 2, N_HEADS, HEAD_DIM)
            v_shared = kv[..., QK_DIM:].reshape(B, S, N_HEADS, 2 * HEAD_DIM)
        h = h + 0.5 * swiglu_ffn(rms_norm(h, ffn1_norm[i]), ffn1_w_in[i], ffn1_w_out[i])
        hn = rms_norm(h, mix_norm[i])
        if i < N_A:
            h = h + conformer_conv(hn, conv_w_in[i], conv_b_in[i], conv_w_dw[i], conv_b_dw[i],
                                   conv_ln_g[i], conv_ln_b[i], conv_w_out[i], conv_b_out[i])
        else:
            j = i - N_A
            lambda_init = 0.8 - 0.6 * math.exp(-0.3 * i)
            h = h + diff_attention(hn, k_shared, v_shared, attn_w_q[j], attn_lq1[j], attn_lk1[j],
                                   attn_lq2[j], attn_lk2[j], attn_subln[j], attn_w_o[j],
                                   rel_bias, lambda_init)
        h = h + 0.5 * swiglu_ffn(rms_norm(h, ffn2_norm[i]), ffn2_w_in[i], ffn2_w_out[i])
        gate = jax.nn.sigmoid(rms_norm(h, ple_norm[i]) @ ple_w_gate[i])
        h = h + gate * (p[i] @ ple_w_proj[i])
    return rms_norm(h, final_norm)
```

```python
import math
from contextlib import ExitStack

import numpy as np
import concourse.bass as bass
import concourse.mybir as mybir
from concourse.bass_utils import run_bass_kernel_spmd
from concourse.alu_op_type import AluOpType as ALU

F32 = mybir.dt.float32
BF16 = mybir.dt.bfloat16
AF = mybir.ActivationFunctionType

D = 1024
S = 2048
NCH = 8
TT = 512
NT = S // TT
DFF = 4096
GH = 2
NG = DFF // (128 * GH)
CONVW = 31
PAD = CONVW - 1
NHEAD = 8
RMS_EPS = 1e-6
LN_EPS = 1e-5
NV = 18 + 37 * 2
SAME_ENG_SYNC = True
NWCH = 3


class Tk:
    __slots__ = ("sem", "val", "eng")

    def __init__(self, sem, val, eng=None):
        self.sem, self.val, self.eng = sem, val, eng


class Res:
    def __init__(self, name):
        self.name = name
        self.w = None
        self.r = []
        self.sem = None
        self.cnt = 0


class Sched:
    def __init__(self, nc, stack):
        self.nc = nc
        self.stack = stack
        self.eng = {"pe": nc.tensor, "act": nc.scalar, "dve": nc.vector, "pool": nc.gpsimd, "sp": nc.sync}
        self.sem = {k: stack.enter_context(nc.semaphore("sem_" + k)) for k in self.eng}
        self.cnt = {k: 0 for k in self.eng}
        self.pending = {k: [] for k in self.eng}
        self.waited = {}
        self.nsem = len(self.eng)
        self.dma_res = []
        self.out_tks = []

    def _wait_for(self, en, tickets):
        best = {}
        for tk in tickets:
            if tk is None:
                continue
            if tk.val is None:
                assert tk.eng == en, "pending ticket consumed cross-engine"
                continue
            if tk.eng == en and (en == "pe" or not SAME_ENG_SYNC):
                continue
            key = id(tk.sem)
            if key not in best or best[key].val < tk.val:
                best[key] = tk
        for key, tk in best.items():
            if self.waited.get((en, key), 0) >= tk.val:
                continue
            self.eng[en].wait_ge(tk.sem, tk.val)
            self.waited[(en, key)] = tk.val

    def _deps(self, reads, writes):
        deps = []
        for r in reads:
            deps.append(r.w)
        for w in writes:
            deps.append(w.w)
            deps.extend(w.r)
        return deps

    def emit(self, en, fn, reads=(), writes=(), inc=True):
        self._wait_for(en, self._deps(reads, writes))
        ins = fn(self.eng[en])
        if inc:
            ins.then_inc(self.sem[en], 1)
            self.cnt[en] += 1
            tk = Tk(self.sem[en], self.cnt[en], en)
            for p in self.pending[en]:
                p.val = self.cnt[en]
            self.pending[en] = []
        else:
            tk = Tk(self.sem[en], None, en)
            self.pending[en].append(tk)
        for w in writes:
            w.w = tk
            w.r = []
        for r in reads:
            r.r.append(tk)
        return tk

    def dma(self, q, out, in_, reads, writes, semres=None):
        semres = semres or writes[0]
        if semres.sem is None:
            semres.sem = self.stack.enter_context(self.nc.semaphore("d_" + semres.name))
            self.nsem += 1
            self.dma_res.append(semres)
        self._wait_for(q, self._deps(reads, writes))
        self.eng[q].dma_start(out=out, in_=in_).then_inc(semres.sem, 16)
        semres.cnt += 16
        tk = Tk(semres.sem, semres.cnt, "dma")
        for w in writes:
            w.w = tk
            w.r = []
        for r in reads:
            r.r.append(tk)
        return tk

    def barrier(self):
        tks = [Tk(self.sem[k], self.cnt[k], k) for k in self.eng if self.cnt[k] > 0]
        dtk = [Tk(r.sem, r.cnt, "dma") for r in self.dma_res if r.cnt > 0]
        for en in self.eng:
            self._wait_for(en, [t for t in tks if t.eng != en] + dtk)


def lambda_init_of(layer):
    return 0.8 - 0.6 * math.exp(-0.3 * layer)


def build_program(stop_after=None, debug=False):
    nc = bass.Bass("TRN2", target_bir_lowering=False)

    def din(name, shape, dt=F32):
        return nc.dram_tensor(name, list(shape), dt, kind="ExternalInput").ap()

    xT = din("xT", [128, NCH, S])
    ppT = din("ppT", [4, 128, 2, S])
    vecs_d = din("vecs", [128, NV * 8])
    wgu_d = din("wgu", [8, NG, 128, 2 * 8 * 128 * GH])
    wo_d = din("wo", [8, NG, 128, GH * D])
    plg_d = din("plg", [4, 8, 128, 8 * 128])
    plp_d = din("plp", [4, 8, 128, 2 * 128])
    cwin_d = din("cwin", [2, 8, 128, 2 * 8 * 128])
    cwout_d = din("cwout", [2, 8, 128, 8 * 128])
    wk_d = din("wk", [8, 128, 8 * 128])
    wv_d = din("wv", [2, 128, 8 * 512])
    wq_d = din("wq", [2, 8, 128, 8 * 128])
    wao_d = din("wao", [2, 8, 128, D])
    lv_d = din("lv", [128, 2 * 4 * 64])
    subln_d = din("subln", [128, 2])
    relb_d = din("relb", [32, 8])
    onehot_d = din("onehot", [32, 384])
    maskrow_d = din("maskrow", [1, 384])
    outT = nc.dram_tensor("outT", [128, NCH, S], F32, kind="ExternalOutput").ap()
    dbg = nc.dram_tensor("dbg", [16, 128, NCH, 16, 32], F32, kind="ExternalOutput").ap() if debug else None
    kT_dram = nc.dram_tensor("kT_scr", [NHEAD, 128, S], BF16, kind="Internal").ap()
    v_dram = nc.dram_tensor("v_scr", [NHEAD, 128, 16, 128], BF16, kind="Internal").ap()
    g_dram = nc.dram_tensor("g_scr", [NHEAD, 128, 384], F32, kind="Internal").ap()

    with ExitStack() as stack:
        sc = Sched(nc, stack)
        E = sc.emit

        def sb(name, shape, dt):
            return stack.enter_context(nc.sbuf_tensor("s_" + name, list(shape), dt))

        h = sb("h", [128, NCH, S], F32)
        xn = sb("xn", [128, NCH, S], BF16)
        vecs = sb("vecs", [128, NV * 8], F32)
        ones_bf = sb("ones_bf", [128, 128], BF16)
        ident_bf = sb("ident_bf", [128, 128], BF16)
        wgu = [sb(f"wgu{i}", [128, 2, 8, 128 * GH], BF16) for i in range(2)]
        wo = [sb(f"wo{i}", [128, GH, D], BF16) for i in range(2)]
        actb = [sb(f"actb{i}", [128, GH, TT], BF16) for i in range(2)]
        sq = [sb(f"sq{i}", [128, NCH, TT], BF16) for i in range(2)]
        NSCR = 8
        scr = [sb(f"scr{i}", [128, TT], F32) for i in range(NSCR)]
        wch = [sb(f"wch{i}", [128, 2, 8, 128], BF16) for i in range(NWCH)]
        ps = stack.enter_context(nc.psum_tensor("ps", [128, 8, TT], F32))

        R_h = [[Res(f"h{c}_{t}") for t in range(NT)] for c in range(NCH)]
        R_xn = [Res(f"xn{t}") for t in range(NT)]
        R_vecs = Res("vecs")
        R_const = Res("const")
        R_wgu = [Res(f"wgu{i}") for i in range(2)]
        R_wo = [Res(f"wo{i}") for i in range(2)]
        R_act = [Res(f"act{i}") for i in range(2)]
        R_sq = [Res(f"sq{i}") for i in range(2)]
        R_scr = [Res(f"scr{i}") for i in range(NSCR)]
        R_wch = [Res(f"wch{i}") for i in range(NWCH)]
        R_ps = [Res(f"ps{i}") for i in range(8)]
        rot = {}

        def nxt(key, n):
            rot[key] = (rot.get(key, -1) + 1) % n
            return rot[key]

        def vcol(v, c):
            return vecs[:, v * 8 + c: v * 8 + c + 1]

        def tsl(t):
            return slice(t * TT, (t + 1) * TT)

        sc.dma("sp", vecs[:, :], vecs_d[:, :], [], [R_vecs])
        for t in range(NT):
            sc.dma("sp", h[:, :, tsl(t)], xT[:, :, tsl(t)], [], [R_h[c][t] for c in range(NCH)],
                   semres=Res(f"xload{t}"))
        E("pool", lambda e: e.memset(ones_bf[:, :], 1.0), [], [R_const])
        E("pool", lambda e: e.memset(ident_bf[:, :], 0.0), [], [R_const])
        E("pool", lambda e: e.affine_select(out=ident_bf[:, :], in_=ident_bf[:, :], pattern=[[-1, 128]],
                                            compare_op=ALU.not_equal, fill=1.0, base=0, channel_multiplier=1),
          [], [R_const])

        def stat_matmuls(bank, src_of_c, reads, nchunks=NCH):
            for c in range(nchunks):
                E("pe", lambda e, c=c: e.matmul(ps[:, bank, :], ones_bf[:, :], src_of_c(c),
                                                start=(c == 0), stop=(c == nchunks - 1)),
                  reads + [R_const], [R_ps[bank]], inc=(c == nchunks - 1))

        def rstd_from(bank_ap, bank_res, inv_n, eps, out_i):
            E("act", lambda e: e.activation(out=scr[out_i][:, :], in_=bank_ap, func=AF.Ln, bias=eps, scale=inv_n),
              [bank_res], [R_scr[out_i]])
            E("act", lambda e: e.activation(out=scr[out_i][:, :], in_=scr[out_i][:, :], func=AF.Exp, scale=-0.5),
              [R_scr[out_i]], [R_scr[out_i]])

        def rms_phase(vidx, dst_of=None):
            for t in range(NT):
                si = nxt("sq", 2)
                E("act", lambda e: e.activation(out=sq[si][:, :, :], in_=h[:, :, tsl(t)], func=AF.Square),
                  [R_h[c][t] for c in range(NCH)], [R_sq[si]])
                bank = 6 + nxt("statbank", 2)
                stat_matmuls(bank, lambda c: sq[si][:, c, :], [R_sq[si]])
                ri = nxt("rstd", 2)
                rstd_from(ps[:, bank, :], R_ps[bank], 1.0 / D, RMS_EPS, ri)
                for c in range(NCH):
                    if dst_of is None:
                        E("dve", lambda e, c=c: e.scalar_tensor_tensor(
                            out=xn[:, c, tsl(t)], in0=h[:, c, tsl(t)], scalar=vcol(vidx, c), in1=scr[ri][:, :],
                            op0=ALU.mult, op1=ALU.mult),
                          [R_h[c][t], R_scr[ri], R_vecs], [R_xn[t]])
                    else:
                        dst_of(t, c, ri)

        def ffn_phase(f, vidx):
            rms_phase(vidx)
            units = [(J, t) for J in range(NG) for t in range(NT)]
            slot_of = {}

            def load(J):
                s = nxt("ffnw", 2)
                slot_of[J] = s
                sc.dma("pool", wgu[s][:, :, :, :].rearrange("p a k c -> p (a k c)").rearrange("p (a b) -> p a b", b=2048),
                       wgu_d[f, J].rearrange("p (a b) -> p a b", b=2048), [], [R_wgu[s]])
                sc.dma("pool", wo[s][:, :, :].rearrange("p a c -> p (a c)").rearrange("p (a b) -> p a b", b=1024),
                       wo_d[f, J].rearrange("p (a b) -> p a b", b=1024), [], [R_wo[s]])

            def gu(n):
                J, t = units[n]
                s = slot_of[J]
                ab = n % 2
                for jj in range(GH):
                    gb = nxt("gbank", 2)
                    ub = 2 + nxt("ubank", 2)
                    for k in range(8):
                        E("pe", lambda e, k=k: e.matmul(ps[:, gb, :], wgu[s][:, 0, k, jj * 128:(jj + 1) * 128],
                                                        xn[:, k, tsl(t)], start=(k == 0), stop=(k == 7)),
                          [R_wgu[s], R_xn[t]], [R_ps[gb]], inc=(k == 7))
                    for k in range(8):
                        E("pe", lambda e, k=k: e.matmul(ps[:, ub, :], wgu[s][:, 1, k, jj * 128:(jj + 1) * 128],
                                                        xn[:, k, tsl(t)], start=(k == 0), stop=(k == 7)),
                          [R_wgu[s], R_xn[t]], [R_ps[ub]], inc=(k == 7))
                    si = 2 + nxt("sgb", 2)
                    E("act", lambda e: e.activation(out=scr[si][:, :], in_=ps[:, gb, :], func=AF.Silu),
                      [R_ps[gb]], [R_scr[si]])
                    E("dve", lambda e: e.tensor_tensor(out=actb[ab][:, jj, :], in0=ps[:, ub, :], in1=scr[si][:, :],
                                                       op=ALU.mult),
                      [R_ps[ub], R_scr[si]], [R_act[ab]])

            def oacc(n):
                J, t = units[n]
                s = slot_of[J]
                ab = n % 2
                for m in range(NCH):
                    ob = 4 + nxt("obank", 2)
                    for jj in range(GH):
                        E("pe", lambda e, jj=jj: e.matmul(ps[:, ob, :], wo[s][:, jj, m * 128:(m + 1) * 128],
                                                          actb[ab][:, jj, :], start=(jj == 0), stop=(jj == GH - 1)),
                          [R_wo[s], R_act[ab]], [R_ps[ob]], inc=(jj == GH - 1))
                    E("dve", lambda e: e.scalar_tensor_tensor(out=h[:, m, tsl(t)], in0=ps[:, ob, :], scalar=0.5,
                                                              in1=h[:, m, tsl(t)], op0=ALU.mult, op1=ALU.add),
                      [R_ps[ob], R_h[m][t]], [R_h[m][t]])

            load(0)
            for n in range(len(units) + 1):
                if n < len(units):
                    gu(n)
                if n >= 1:
                    oacc(n - 1)
                if n < len(units):
                    J, t = units[n]
                    if t == 0 and J + 1 < NG:
                        load(J + 1)

        def ple_phase(l):
            rms_phase(4 * l + 3)
            ppb = sq[1][:, :, :].rearrange("p a b -> p (a b)").rearrange("p (k t) -> p k t", k=2)
            R_ppb = R_sq[1]
            sc.dma("pool", sq[1][:, :, :].rearrange("p a b -> p (a b)").rearrange("p (a b) -> p a b", b=1024),
                   ppT[l].rearrange("p k t -> p (k t)").rearrange("p (a b) -> p a b", b=1024), [], [R_ppb])
            for m in range(NCH):
                wi = nxt("wch", NWCH)
                sc.dma("pool", wch[wi][:, 0, :, :].rearrange("p k c -> p (k c)"), plg_d[l, m], [], [R_wch[wi]])
                sc.dma("pool", wch[wi][:, 1, 0:2, :].rearrange("p k c -> p (k c)"), plp_d[l, m], [], [R_wch[wi]])
                for t in range(NT):
                    gb = nxt("gbank", 2)
                    ub = 2 + nxt("ubank", 2)
                    for k in range(8):
                        E("pe", lambda e, k=k: e.matmul(ps[:, gb, :], wch[wi][:, 0, k, :], xn[:, k, tsl(t)],
                                                        start=(k == 0), stop=(k == 7)),
                          [R_wch[wi], R_xn[t]], [R_ps[gb]], inc=(k == 7))
                    for k in range(2):
                        E("pe", lambda e, k=k: e.matmul(ps[:, ub, :], wch[wi][:, 1, k, :], ppb[:, k, tsl(t)],
                                                        start=(k == 0), stop=(k == 1)),
                          [R_wch[wi], R_ppb], [R_ps[ub]], inc=(k == 1))
                    si = 2 + nxt("sgb", 2)
                    E("act", lambda e: e.activation(out=scr[si][:, :], in_=ps[:, gb, :], func=AF.Sigmoid),
                      [R_ps[gb]], [R_scr[si]])
                    E("dve", lambda e: e.tensor_tensor(out=scr[si][:, :], in0=ps[:, ub, :], in1=scr[si][:, :],
                                                       op=ALU.mult),
                      [R_ps[ub], R_scr[si]], [R_scr[si]])
                    E("dve", lambda e: e.tensor_tensor(out=h[:, m, tsl(t)], in0=h[:, m, tsl(t)],
                                                       in1=scr[si][:, :], op=ALU.add),
                      [R_scr[si], R_h[m][t]], [R_h[m][t]])

        def conv_phase(l):
            vb = 18 + 37 * l
            rms_phase(4 * l + 1)
            with ExitStack() as st2:
                ubuf = st2.enter_context(nc.sbuf_tensor("s_" + f"ubuf{l}", [128, NCH, PAD + TT], BF16))
                yb = st2.enter_context(nc.sbuf_tensor("s_" + f"yb{l}", [128, NCH, TT], F32))
                dg = [st2.enter_context(nc.sbuf_tensor("s_" + f"dg{l}_{i}", [128, CONVW, 128], BF16)) for i in range(1)]
                R_ub = [Res(f"ubuf{c}") for c in range(NCH)]
                R_yb = [Res(f"yb{c}") for c in range(NCH)]
                R_dg = [Res("dg0")]
                E("pool", lambda e: e.memset(ubuf[:, :, 0:PAD], 0.0), [], R_ub)
                for t in range(NT):
                    for c in range(NCH):
                        wi = nxt("wch", NWCH)
                        sc.dma("pool", wch[wi][:, :, :, :].rearrange("p a k c -> p (a k c)"), cwin_d[l, c],
                               [], [R_wch[wi]])
                        ab_ = nxt("gbank", 2)
                        gb_ = 2 + nxt("ubank", 2)
                        for k in range(8):
                            E("pe", lambda e, k=k: e.matmul(ps[:, ab_, :], wch[wi][:, 0, k, :], xn[:, k, tsl(t)],
                                                            start=(k == 0), stop=(k == 7)),
                              [R_wch[wi], R_xn[t]], [R_ps[ab_]], inc=(k == 7))
                        for k in range(8):
                            E("pe", lambda e, k=k: e.matmul(ps[:, gb_, :], wch[wi][:, 1, k, :], xn[:, k, tsl(t)],
                                                            start=(k == 0), stop=(k == 7)),
                              [R_wch[wi], R_xn[t]], [R_ps[gb_]], inc=(k == 7))
                        si = 2 + nxt("sgb", 2)
                        E("act", lambda e: e.activation(out=scr[si][:, :], in_=ps[:, gb_, :], func=AF.Sigmoid,
                                                        bias=vcol(vb + 1, c)),
                          [R_ps[gb_], R_vecs], [R_scr[si]])
                        if t > 0:
                            E("pool", lambda e: e.tensor_copy(out=ubuf[:, c, 0:PAD], in_=ubuf[:, c, TT:TT + PAD]),
                              [R_ub[c]], [R_ub[c]])
                        E("dve", lambda e: e.scalar_tensor_tensor(out=ubuf[:, c, PAD:PAD + TT], in0=ps[:, ab_, :],
                                                                  scalar=vcol(vb, c), in1=scr[si][:, :],
                                                                  op0=ALU.add, op1=ALU.mult),
                          [R_ps[ab_], R_scr[si], R_vecs], [R_ub[c]])
                    for c in range(NCH):
                        di = 0
                        for j in range(CONVW):
                            E("dve", lambda e, j=j: e.tensor_scalar(out=dg[di][:, j, :], in0=ident_bf[:, :],
                                                                    scalar1=vcol(vb + 6 + j, c), scalar2=None,
                                                                    op0=ALU.mult),
                              [R_const, R_vecs], [R_dg[di]], inc=(j == CONVW - 1))
                        yb_ = 4 + nxt("obank", 2)
                        for j in range(CONVW):
                            E("pe", lambda e, j=j: e.matmul(ps[:, yb_, :], dg[di][:, j, :], ubuf[:, c, j:j + TT],
                                                            start=(j == 0), stop=(j == CONVW - 1)),
                              [R_dg[di], R_ub[c]], [R_ps[yb_]], inc=(j == CONVW - 1))
                        E("act", lambda e: e.activation(out=yb[:, c, :], in_=ps[:, yb_, :], func=AF.Identity,
                                                        bias=vcol(vb + 2, c)),
                          [R_ps[yb_], R_vecs], [R_yb[c]])
                    si_ = nxt("sq", 2)
                    zs = xn[:, :, tsl(t)]
                    R_zs = R_xn[t]
                    E("act", lambda e: e.activation(out=sq[si_][:, :, :], in_=yb[:, :, :], func=AF.Square),
                      R_yb, [R_sq[si_]])
                    E("dve", lambda e: e.tensor_copy(out=zs, in_=yb[:, :, :]), R_yb, [R_zs])
                    b1 = 6 + nxt("statbank", 2)
                    stat_matmuls(b1, lambda c: xn[:, c, tsl(t)], [R_zs])
                    b2 = 6 + nxt("statbank", 2)
                    stat_matmuls(b2, lambda c: sq[si_][:, c, :], [R_sq[si_]])
                    MU, MSQ, RS, NM, Z = 4, 5, 6, 7, 0
                    E("dve", lambda e: e.tensor_scalar(out=scr[MU][:, :], in0=ps[:, b1, :], scalar1=1.0 / D,
                                                       scalar2=None, op0=ALU.mult), [R_ps[b1]], [R_scr[MU]])
                    E("dve", lambda e: e.tensor_tensor(out=scr[MSQ][:, :], in0=scr[MU][:, :], in1=scr[MU][:, :],
                                                       op=ALU.mult), [R_scr[MU]], [R_scr[MSQ]])
                    E("dve", lambda e: e.scalar_tensor_tensor(out=scr[MSQ][:, :], in0=ps[:, b2, :], scalar=1.0 / D,
                                                              in1=scr[MSQ][:, :], op0=ALU.mult, op1=ALU.subtract),
                      [R_ps[b2], R_scr[MSQ]], [R_scr[MSQ]])
                    rstd_from(scr[MSQ][:, :], R_scr[MSQ], 1.0, LN_EPS, RS)
                    E("dve", lambda e: e.scalar_tensor_tensor(out=scr[NM][:, :], in0=scr[MU][:, :], scalar=-1.0,
                                                              in1=scr[RS][:, :], op0=ALU.mult, op1=ALU.mult),
                      [R_scr[MU], R_scr[RS]], [R_scr[NM]])
                    for c in range(NCH):
                        zi = nxt("z", 2)
                        E("dve", lambda e: e.tensor_tensor(out=scr[zi][:, :], in0=yb[:, c, :], in1=scr[RS][:, :],
                                                           op=ALU.mult), [R_yb[c], R_scr[RS]], [R_scr[zi]])
                        E("dve", lambda e: e.tensor_tensor(out=scr[zi][:, :], in0=scr[zi][:, :], in1=scr[NM][:, :],
                                                           op=ALU.add), [R_scr[zi], R_scr[NM]], [R_scr[zi]])
                        E("act", lambda e: e.activation(out=xn[:, c, tsl(t)], in_=scr[zi][:, :], func=AF.Silu,
                                                        bias=vcol(vb + 4, c), scale=vcol(vb + 3, c)),
                          [R_scr[zi], R_vecs], [R_zs])
                    for m in range(NCH):
                        wi = nxt("wch", NWCH)
                        sc.dma("pool", wch[wi][:, 0, :, :].rearrange("p k c -> p (k c)"), cwout_d[l, m],
                               [], [R_wch[wi]])
                        ob = 4 + nxt("obank", 2)
                        for k in range(8):
                            E("pe", lambda e, k=k: e.matmul(ps[:, ob, :], wch[wi][:, 0, k, :], xn[:, k, tsl(t)],
                                                            start=(k == 0), stop=(k == 7)),
                              [R_wch[wi], R_zs], [R_ps[ob]], inc=(k == 7))
                        E("dve", lambda e: e.scalar_tensor_tensor(out=h[:, m, tsl(t)], in0=ps[:, ob, :],
                                                                  scalar=vcol(vb + 5, m), in1=h[:, m, tsl(t)],
                                                                  op0=ALU.add, op1=ALU.add),
                          [R_ps[ob], R_h[m][t], R_vecs], [R_h[m][t]])
                sc.barrier()

        def kv_phase():
            rms_phase(16)
            with ExitStack() as st2:
                kst = [st2.enter_context(nc.sbuf_tensor("s_" + f"kst{i}", [128, S], BF16)) for i in range(2)]
                vst = [st2.enter_context(nc.sbuf_tensor("s_" + f"vst{i}", [128, D], BF16)) for i in range(2)]
                wvb = [st2.enter_context(nc.sbuf_tensor("s_" + f"wvb{i}", [128, 8, 512], BF16)) for i in range(2)]
                R_kst = [Res("kst0"), Res("kst1")]
                R_vst = [Res("vst0"), Res("vst1")]
                R_wvb = [Res("wvb0"), Res("wvb1")]
                for hh in range(NHEAD):
                    wi = nxt("wch", NWCH)
                    sc.dma("pool", wch[wi][:, 0, :, :].rearrange("p k c -> p (k c)"), wk_d[hh], [], [R_wch[wi]])
                    ks = nxt("kst", 2)
                    for t in range(NT):
                        bk = nxt("gbank", 2)
                        for k in range(8):
                            E("pe", lambda e, k=k: e.matmul(ps[:, bk, :], wch[wi][:, 0, k, :], xn[:, k, tsl(t)],
                                                            start=(k == 0), stop=(k == 7)),
                              [R_wch[wi], R_xn[t]], [R_ps[bk]], inc=(k == 7))
                        E("act", lambda e: e.activation(out=kst[ks][:, tsl(t)], in_=ps[:, bk, :], func=AF.Identity),
                          [R_ps[bk]], [R_kst[ks]])
                    sc.dma("sp", kT_dram[hh], kst[ks][:, :], [R_kst[ks]], [R_kv], semres=R_kstout[ks])
                for half in range(2):
                    sc.dma("pool", wvb[half][:, :, :].rearrange("p k c -> p (k c)").rearrange("p (a b) -> p a b", b=2048),
                           wv_d[half].rearrange("p (a b) -> p a b", b=2048), [], [R_wvb[half]])
                for kt in range(16):
                    vs = nxt("vst", 2)
                    for half in range(2):
                        bk = 2 + nxt("ubank", 2)
                        for k in range(8):
                            E("pe", lambda e, k=k: e.matmul(ps[:, bk, :], xn[:, k, kt * 128:(kt + 1) * 128],
                                                            wvb[half][:, k, :], start=(k == 0), stop=(k == 7)),
                              [R_wvb[half], R_xn[kt // 4]], [R_ps[bk]], inc=(k == 7))
                        E("dve", lambda e: e.tensor_copy(out=vst[vs][:, half * 512:(half + 1) * 512], in_=ps[:, bk, :]),
                          [R_ps[bk]], [R_vst[vs]])
                    sc.dma("sp", v_dram[:, :, kt, :].rearrange("h p e -> p h e"),
                           vst[vs][:, :].rearrange("p (h e) -> p h e", e=128), [R_vst[vs]], [R_kv],
                           semres=R_vstout[vs])
                sc.barrier()

        R_kv = Res("kvdram")
        R_kstout = [Res("kstout0"), Res("kstout1")]
        R_vstout = [Res("vstout0"), Res("vstout1")]

        def attn_phase(layer, j_, A):
            lam_init = lambda_init_of(layer)
            rms_phase(4 * layer + 1)
            kTh, vh, qT, pT, wob = A["kTh"], A["vh"], A["qT"], A["pT"], A["wob"]
            R_kTh, R_vh, R_qT, R_pT, R_wob = A["R_kTh"], A["R_vh"], A["R_qT"], A["R_pT"], A["R_wob"]
            small, R_small = A["small"], A["R_small"]
            lvb = A["lvb"]
            base = j_ * 256
            for i2 in range(2):
                E("dve", lambda e, i2=i2: e.tensor_tensor(out=A["ltmp"][:, :], in0=lvb[:, base + i2 * 128: base + i2 * 128 + 64],
                                                          in1=lvb[:, base + i2 * 128 + 64: base + i2 * 128 + 128],
                                                          op=ALU.mult), [A["R_lvb"]], [A["R_ltmp"]])
                E("dve", lambda e, i2=i2: e.tensor_reduce(out=small[:, i2:i2 + 1], in_=A["ltmp"][:, :],
                                                          axis=mybir.AxisListType.X, op=ALU.add),
                  [A["R_ltmp"]], [R_small])
            E("act", lambda e: e.activation(out=small[:, 0:2], in_=small[:, 0:2], func=AF.Exp), [R_small], [R_small])
            E("dve", lambda e: e.scalar_tensor_tensor(out=small[:, 2:3], in0=small[:, 1:2], scalar=-lam_init,
                                                      in1=small[:, 0:1], op0=ALU.add, op1=ALU.subtract),
              [R_small], [R_small])
            E("dve", lambda e: e.tensor_scalar(out=small[:, 3:4], in0=A["sublnb"][:, j_:j_ + 1],
                                               scalar1=1.0 - lam_init, scalar2=None, op0=ALU.mult),
              [A["R_sublnb"]], [R_small])
            biasd, biasn, b31 = A["biasd"], A["biasn"], A["b31"]
            R_bias = A["R_bias"]
            for hh in range(NHEAD):
                hs = 0
                ws = hh % 2
                wi = nxt("wch", NWCH)
                sc.dma("pool", wch[wi][:, 0, :, :].rearrange("p k c -> p (k c)"), wq_d[j_, hh], [], [R_wch[wi]])
                sc.dma("pool", wob[ws][:, :], wao_d[j_, hh], [], [R_wob[ws]])
                sc.dma("sp", kTh[hs][:, :], kT_dram[hh], [R_kv], [R_kTh[hs]])
                sc.dma("sp", vh[hs][:, :, :], v_dram[hh], [R_kv], [R_vh[hs]])
                for t in range(NT):
                    bk = 6 + nxt("statbank", 2)
                    for k in range(8):
                        E("pe", lambda e, k=k: e.matmul(ps[:, bk, :], wch[wi][:, 0, k, :], xn[:, k, tsl(t)],
                                                        start=(k == 0), stop=(k == 7)),
                          [R_wch[wi], R_xn[t]], [R_ps[bk]], inc=(k == 7))
                    E("dve", lambda e: e.tensor_copy(out=qT[hs][:, tsl(t)], in_=ps[:, bk, :]),
                      [R_ps[bk]], [R_qT[hs]])
                for qi in range(NT):
                    nk = 4 * (qi + 1)
                    NUM = [2, 3]
                    DEN = [4, 5]
                    for kj in range(nk):
                        r = kj - 4 * qi
                        c0 = max(r, 0)
                        qs = slice(qi * TT + c0 * 128, (qi + 1) * TT)
                        cs = slice(c0 * 128, TT)
                        pi = nxt("pT", 2)
                        for mp in range(2):
                            sbk = mp
                            rows = slice(mp * 64, (mp + 1) * 64)
                            E("pe", lambda e: e.matmul(ps[:, sbk, cs], kTh[hs][rows, kj * 128:(kj + 1) * 128],
                                                       qT[hs][rows, qs], start=True, stop=True),
                              [R_kTh[hs], R_qT[hs]], [R_ps[sbk]])
                            pr = R_pT[pi][mp]
                            pt = pT[pi][mp]
                            if r <= -2:
                                E("act", lambda e: e.activation(out=pt[:, cs], in_=ps[:, sbk, cs], func=AF.Exp,
                                                                bias=b31[:, hh:hh + 1], scale=0.125),
                                  [R_ps[sbk], R_bias], [pr])
                            else:
                                for sbl in range(c0, 4):
                                    ss = slice(sbl * 128, (sbl + 1) * 128)
                                    dd = sbl - r
                                    if dd >= 2:
                                        E("act", lambda e, ss=ss: e.activation(out=pt[:, ss], in_=ps[:, sbk, ss],
                                                                               func=AF.Exp, bias=b31[:, hh:hh + 1],
                                                                               scale=0.125),
                                          [R_ps[sbk], R_bias], [pr])
                                    else:
                                        bt = biasd if dd == 0 else biasn
                                        ti = 2 + nxt("sgb", 2)
                                        E("dve", lambda e, ss=ss, bt=bt: e.scalar_tensor_tensor(
                                            out=scr[ti][:, 0:128], in0=ps[:, sbk, ss], scalar=0.125,
                                            in1=bt[:, hh, :], op0=ALU.mult, op1=ALU.add),
                                          [R_ps[sbk], R_bias], [R_scr[ti]])
                                        E("act", lambda e, ss=ss: e.activation(out=pt[:, ss], in_=scr[ti][:, 0:128],
                                                                               func=AF.Exp),
                                          [R_scr[ti]], [pr])
                        for mp in range(2):
                            pr = R_pT[pi][mp]
                            pt = pT[pi][mp]
                            E("pe", lambda e: e.matmul(ps[:, NUM[mp], cs], vh[hs][:, kj, :], pt[:, cs],
                                                       start=(kj == 0), stop=(kj == nk - 1)),
                              [R_vh[hs], pr], [R_ps[NUM[mp]]], inc=(kj == nk - 1))
                            E("pe", lambda e: e.matmul(ps[:, DEN[mp], cs], ones_bf[:, :], pt[:, cs],
                                                       start=(kj == 0), stop=(kj == nk - 1)),
                              [R_const, pr], [R_ps[DEN[mp]]], inc=True)
                    R1, R2, O1, T2 = 4, 5, 6, 7
                    for mp, ri in ((0, R1), (1, R2)):
                        E("act", lambda e, mp=mp, ri=ri: e.activation(out=scr[ri][:, :], in_=ps[:, DEN[mp], :],
                                                                      func=AF.Ln), [R_ps[DEN[mp]]], [R_scr[ri]])
                        E("act", lambda e, ri=ri: e.activation(out=scr[ri][:, :], in_=scr[ri][:, :], func=AF.Exp,
                                                               scale=-1.0), [R_scr[ri]], [R_scr[ri]])
                    E("dve", lambda e: e.tensor_tensor(out=scr[O1][:, :], in0=ps[:, NUM[0], :], in1=scr[R1][:, :],
                                                       op=ALU.mult), [R_ps[NUM[0]], R_scr[R1]], [R_scr[O1]])
                    E("dve", lambda e: e.tensor_tensor(out=scr[T2][:, :], in0=ps[:, NUM[1], :], in1=scr[R2][:, :],
                                                       op=ALU.mult), [R_ps[NUM[1]], R_scr[R2]], [R_scr[T2]])
                    E("dve", lambda e: e.scalar_tensor_tensor(out=scr[O1][:, :], in0=scr[T2][:, :],
                                                              scalar=small[:, 2:3], in1=scr[O1][:, :],
                                                              op0=ALU.mult, op1=ALU.add),
                      [R_scr[T2], R_scr[O1], R_small], [R_scr[O1]])
                    si = nxt("sq", 2)
                    E("act", lambda e: e.activation(out=sq[si][:, 0, :], in_=scr[O1][:, :], func=AF.Square),
                      [R_scr[O1]], [R_sq[si]])
                    bk = 6 + nxt("statbank", 2)
                    stat_matmuls(bk, lambda c: sq[si][:, 0, :], [R_sq[si]], nchunks=1)
                    rstd_from(ps[:, bk, :], R_ps[bk], 1.0 / 128, RMS_EPS, T2)
                    E("dve", lambda e: e.scalar_tensor_tensor(out=sq[si][:, 1, :], in0=scr[O1][:, :],
                                                              scalar=small[:, 3:4], in1=scr[T2][:, :],
                                                              op0=ALU.mult, op1=ALU.mult),
                      [R_scr[O1], R_scr[T2], R_small, R_sq[si]], [R_sq[si]])
                    for m in range(NCH):
                        ob = 6 + nxt("statbank", 2)
                        E("pe", lambda e: e.matmul(ps[:, ob, :], wob[ws][:, m * 128:(m + 1) * 128], sq[si][:, 1, :],
                                                   start=True, stop=True), [R_wob[ws], R_sq[si]], [R_ps[ob]])
                        E("dve", lambda e: e.tensor_tensor(out=h[:, m, tsl(qi)], in0=ps[:, ob, :],
                                                           in1=h[:, m, tsl(qi)], op=ALU.add),
                          [R_ps[ob], R_h[m][qi]], [R_h[m][qi]])

        def attn_setup(st2):
            A = {}
            A["kTh"] = [st2.enter_context(nc.sbuf_tensor("s_" + f"kTh{i}", [128, S], BF16)) for i in range(1)]
            A["vh"] = [st2.enter_context(nc.sbuf_tensor("s_" + f"vh{i}", [128, 16, 128], BF16)) for i in range(1)]
            A["qT"] = [st2.enter_context(nc.sbuf_tensor("s_" + f"qT{i}", [128, S], BF16)) for i in range(1)]
            A["pT"] = [[st2.enter_context(nc.sbuf_tensor("s_" + f"pT{i}_{m}", [128, TT], BF16)) for m in range(2)]
                       for i in range(2)]
            A["wob"] = [st2.enter_context(nc.sbuf_tensor("s_" + f"wob{i}", [128, D], BF16)) for i in range(2)]
            A["small"] = st2.enter_context(nc.sbuf_tensor("s_small", [128, 8], F32))
            A["ltmp"] = st2.enter_context(nc.sbuf_tensor("s_ltmp", [128, 64], F32))
            A["lvb"] = st2.enter_context(nc.sbuf_tensor("s_lvb", [128, 512], F32))
            A["sublnb"] = st2.enter_context(nc.sbuf_tensor("s_sublnb", [128, 2], F32))
            A["biasd"] = st2.enter_context(nc.sbuf_tensor("s_biasd", [128, NHEAD, 128], F32))
            A["biasn"] = st2.enter_context(nc.sbuf_tensor("s_biasn", [128, NHEAD, 128], F32))
            A["b31"] = st2.enter_context(nc.sbuf_tensor("s_b31", [128, NHEAD], F32))
            gsb = scr[0][:, 0:384]
            oneh = scr[1][0:32, 0:384]
            maskr = scr[2][0:1, 0:384]
            ones1 = scr[3][0:1, 0:128]
            ones32 = scr[4][0:32, 0:128]
            lhs = scr[5][0:32, 0:128]
            relb = scr[6][0:32, 0:8]
            for k in ("kTh", "vh", "qT", "wob"):
                A["R_" + k] = [Res(k + "0"), Res(k + "1")]
            A["R_pT"] = [[Res(f"pT{i}_{m}") for m in range(2)] for i in range(2)]
            for k in ("small", "ltmp", "lvb", "sublnb", "bias"):
                A["R_" + k] = Res(k)
            R_gsb, R_oneh, R_maskr, R_ones1, R_ones32, R_lhs, R_relb = (R_scr[i] for i in range(7))
            R_g = Res("gdram")
            sc.dma("sp", A["lvb"][:, :], lv_d[:, :], [], [A["R_lvb"]])
            sc.dma("sp", A["sublnb"][:, :], subln_d[:, :], [], [A["R_sublnb"]])
            sc.dma("sp", relb, relb_d[:, :], [], [R_relb], semres=Res("relb"))
            sc.dma("sp", oneh, onehot_d[:, :], [], [R_oneh], semres=Res("oneh"))
            sc.dma("sp", maskr, maskrow_d[:, :], [], [R_maskr], semres=Res("maskr"))
            E("pool", lambda e: e.memset(ones1, 1.0), [], [R_ones1])
            E("pool", lambda e: e.memset(ones32, 1.0), [], [R_ones32])
            for hh in range(NHEAD):
                E("dve", lambda e: e.tensor_scalar(out=lhs, in0=ones32, scalar1=relb[:, hh:hh + 1],
                                                   scalar2=None, op0=ALU.mult), [R_ones32, R_relb], [R_lhs])
                E("pe", lambda e: e.matmul(ps[:, 0, 0:384], lhs, oneh, start=True, stop=False),
                  [R_lhs, R_oneh], [R_ps[0]], inc=False)
                E("pe", lambda e: e.matmul(ps[:, 0, 0:384], ones1, maskr, start=False, stop=True),
                  [R_ones1, R_maskr], [R_ps[0]])
                E("dve", lambda e: e.tensor_copy(out=gsb, in_=ps[:, 0, 0:384]), [R_ps[0]], [R_gsb])
                E("dve", lambda e: e.tensor_copy(out=A["b31"][:, hh:hh + 1], in_=gsb[:, 383:384]),
                  [R_gsb], [A["R_bias"]])
                sc.dma("sp", g_dram[hh], gsb, [R_gsb], [R_g], semres=Res(f"gout{hh}"))
            for hh in range(NHEAD):
                for dd, bt in ((0, A["biasd"]), (128, A["biasn"])):
                    src = bass.AP(tensor=g_dram.tensor, offset=g_dram[hh].offset + 128 + dd,
                                  ap=[[383, 128], [1, 128]])
                    sc.dma("sp", bt[:, hh, :], src, [R_g], [A["R_bias"]], semres=Res(f"bt{hh}_{dd}"))
            return A

        def dump_h():
            for t in range(NT):
                sc.out_tks.append(sc.dma("sp", outT[:, :, tsl(t)], h[:, :, tsl(t)],
                                         [R_h[c][t] for c in range(NCH)], [R_out], semres=Res(f"dump{t}")))

        R_out = Res("out")
        done = [False]

        nph = [0]

        def check(name):
            if debug:
                src = h[:, :, :].rearrange("p c (a b) -> p c a b", b=128)[:, :, :, 0:32]
                sc.out_tks.append(sc.dma("sp", dbg[nph[0]], src, [R_h[c][t] for c in range(NCH) for t in range(NT)],
                                         [R_out], semres=Res(f"dbg{nph[0]}")))
                nph[0] += 1
            if stop_after == name and not done[0]:
                dump_h()
                done[0] = True
            return done[0]

        def forward():
            for l in range(2):
                ffn_phase(2 * l, 4 * l + 0)
                if check(f"ffn1_{l}"):
                    return
                conv_phase(l)
                if check(f"mix_{l}"):
                    return
                ffn_phase(2 * l + 1, 4 * l + 2)
                if check(f"ffn2_{l}"):
                    return
                ple_phase(l)
                if check(f"ple_{l}"):
                    return
            kv_phase()
            with ExitStack() as st2:
                A = attn_setup(st2)
                for l in range(2, 4):
                    ffn_phase(2 * l, 4 * l + 0)
                    if check(f"ffn1_{l}"):
                        return
                    attn_phase(l, l - 2, A)
                    if check(f"mix_{l}"):
                        return
                    ffn_phase(2 * l + 1, 4 * l + 2)
                    if check(f"ffn2_{l}"):
                        return
                    ple_phase(l)
                    if check(f"ple_{l}"):
                        return
                sc.barrier()
            with ExitStack() as st2:
                ob = [st2.enter_context(nc.sbuf_tensor("s_" + f"outb{i}", [128, NCH, TT], F32)) for i in range(2)]
                R_ob = [Res("outb0"), Res("outb1")]
                cur = {}

                def dst(t, c, ri):
                    if c == 0:
                        cur["i"] = nxt("outb", 2)
                    oi = cur["i"]
                    E("dve", lambda e: e.scalar_tensor_tensor(out=ob[oi][:, c, :], in0=h[:, c, tsl(t)],
                                                              scalar=vcol(17, c), in1=scr[ri][:, :],
                                                              op0=ALU.mult, op1=ALU.mult),
                      [R_h[c][t], R_scr[ri], R_vecs], [R_ob[oi]])
                    if c == NCH - 1:
                        sc.out_tks.append(sc.dma("sp", outT[:, :, tsl(t)], ob[oi][:, :, :], [R_ob[oi]], [R_out],
                                                 semres=Res(f"fin{t}")))

                rms_phase(17, dst_of=dst)
                sc.barrier()

        forward()
        sc._wait_for("sp", sc.out_tks)
        sc.barrier()
        print("sched counts:", sc.cnt, "nsem", sc.nsem)
    return nc


def _rel_bucket_np(n):
    n = np.maximum(n, 0)
    max_exact = 16
    nf = np.maximum(n, 1).astype(np.float32)
    large = max_exact + (np.log(nf / max_exact) / math.log(128 / max_exact) * (32 - max_exact)).astype(np.int32)
    large = np.minimum(large, 31)
    return np.where(n < max_exact, n, large)


def _bucket_table():
    return _rel_bucket_np(np.arange(0, 256))


def prep_shared(inp):
    f = np.float32
    A = {k: np.asarray(v, dtype=f) for k, v in inp.items()}
    sh = {}
    vl = []
    for l in range(4):
        vl += [A["ffn1_norm"][l], A["mix_norm"][l], A["ffn2_norm"][l], A["ple_norm"][l]]
    vl += [A["kv_norm"], A["final_norm"]]
    for l in range(2):
        vl += [A["conv_b_in"][l][:D], A["conv_b_in"][l][D:], A["conv_b_dw"][l], A["conv_ln_g"][l],
               A["conv_ln_b"][l], A["conv_b_out"][l]]
        vl += [A["conv_w_dw"][l][j] for j in range(CONVW)]
    V = np.stack(vl, 0)
    assert V.shape[0] == NV
    sh["vecs"] = np.ascontiguousarray(V.reshape(NV, 8, 128).transpose(2, 0, 1).reshape(128, NV * 8))
    wgu, wo = [], []
    for l in range(4):
        for nm in ("ffn1", "ffn2"):
            wi = A[nm + "_w_in"][l]
            wgu.append(wi.reshape(8, 128, 2, NG, 128 * GH).transpose(3, 1, 2, 0, 4).reshape(NG, 128, -1))
            wt = A[nm + "_w_out"][l]
            wo.append(wt.reshape(NG, GH, 128, D).transpose(0, 2, 1, 3).reshape(NG, 128, -1))
    sh["wgu"] = np.ascontiguousarray(np.stack(wgu, 0))
    sh["wo"] = np.ascontiguousarray(np.stack(wo, 0))

    def sq_tiles(W):
        return W.reshape(8, 128, 8, 128).transpose(2, 1, 0, 3).reshape(8, 128, 1024)

    def head_tiles(W):
        return W.reshape(8, 128, 2, 8, 64).transpose(3, 1, 0, 2, 4).reshape(8, 128, 1024)

    sh["plg"] = np.ascontiguousarray(np.stack([sq_tiles(A["ple_w_gate"][l]) for l in range(4)], 0))
    sh["plp"] = np.ascontiguousarray(np.stack(
        [A["ple_w_proj"][l].reshape(2, 128, 8, 128).transpose(2, 1, 0, 3).reshape(8, 128, 256) for l in range(4)], 0))
    sh["cwin"] = np.ascontiguousarray(np.stack(
        [A["conv_w_in"][l].reshape(8, 128, 2, 8, 128).transpose(3, 1, 2, 0, 4).reshape(8, 128, -1)
         for l in range(2)], 0))
    sh["cwout"] = np.ascontiguousarray(np.stack([sq_tiles(A["conv_w_out"][l]) for l in range(2)], 0))
    sh["wk"] = np.ascontiguousarray(head_tiles(A["w_kv"][:, :D]))
    wv = A["w_kv"][:, D:]
    sh["wv"] = np.ascontiguousarray(wv.reshape(8, 128, 2, 512).transpose(2, 1, 0, 3).reshape(2, 128, -1))
    sh["wq"] = np.ascontiguousarray(np.stack([head_tiles(A["attn_w_q"][l]) for l in range(2)], 0))
    sh["wao"] = np.ascontiguousarray(np.stack([A["attn_w_o"][l].reshape(8, 128, D) for l in range(2)], 0))
    lv = np.stack([np.concatenate([A["attn_lq1"][l], A["attn_lk1"][l], A["attn_lq2"][l], A["attn_lk2"][l]])
                   for l in range(2)], 0).reshape(1, 512)
    sh["lv"] = np.ascontiguousarray(np.broadcast_to(lv, (128, 512)))
    sh["subln"] = np.ascontiguousarray(A["attn_subln"].T)
    sh["relb"] = np.ascontiguousarray(A["rel_bias"])
    bt = _bucket_table()
    oh = np.zeros((32, 384), f)
    mr = np.zeros((1, 384), f)
    for npr in range(384):
        n = npr - 128
        if n < 0:
            mr[0, npr] = -1e30
        else:
            oh[bt[min(n, 255)], npr] = 1.0
    sh["onehot"] = oh
    sh["maskrow"] = mr
    return sh


def prep_core(inp, b):
    x = np.asarray(inp["x"], dtype=np.float32)[b]
    p = np.asarray(inp["p"], dtype=np.float32)[:, b]
    xT = np.ascontiguousarray(x.T.reshape(8, 128, S).transpose(1, 0, 2))
    ppT = np.ascontiguousarray(p.transpose(0, 2, 1).reshape(4, 2, 128, S).transpose(0, 2, 1, 3))
    return {"xT": xT, "ppT": ppT}


_PROG = {}


def run(inputs, stop_after=None, trace=False, debug=False):
    key = (stop_after, debug)
    if key not in _PROG:
        _PROG[key] = build_program(stop_after, debug)
    nc = _PROG[key]
    sh = prep_shared(inputs)
    in_maps = []
    for b in range(8):
        m = dict(sh)
        m.update(prep_core(inputs, b))
        in_maps.append(m)
    res = run_bass_kernel_spmd(nc, in_maps, core_ids=list(range(8)), **({"trace": True} if trace else {}))
    outs = []
    for b in range(8):
        o = np.asarray(res.results[b]["outT"])
        outs.append(o.transpose(1, 0, 2).reshape(D, S).T)
    return np.ascontiguousarray(np.stack(outs, 0).astype(np.float32)), res


def kernel(**inputs):
    out, _ = run(inputs)
    return out
```

```python
import math
from contextlib import ExitStack

import numpy as np
import concourse.bass as bass
import concourse.mybir as mybir
from concourse.bass_utils import run_bass_kernel_spmd
from concourse.alu_op_type import AluOpType as ALU

F32 = mybir.dt.float32
BF16 = mybir.dt.bfloat16
AF = mybir.ActivationFunctionType

D = 1024
S = 2048
NCH = 8
TT = 512
NT = S // TT
DFF = 4096
GH = 2
NG = DFF // (128 * GH)
CONVW = 31
PAD = CONVW - 1
NHEAD = 8
RMS_EPS = 1e-6
LN_EPS = 1e-5
NV = 18 + 37 * 2
SAME_ENG_SYNC = True
NWCH = 3
DEN_ENG1 = "dve"


class Tk:
    __slots__ = ("sem", "val", "eng")

    def __init__(self, sem, val, eng=None):
        self.sem, self.val, self.eng = sem, val, eng


class Res:
    def __init__(self, name):
        self.name = name
        self.w = None
        self.r = []
        self.sem = None
        self.cnt = 0


class Sched:
    def __init__(self, nc, stack):
        self.nc = nc
        self.stack = stack
        self.eng = {"pe": nc.tensor, "act": nc.scalar, "dve": nc.vector, "pool": nc.gpsimd, "sp": nc.sync}
        self.sem = {k: stack.enter_context(nc.semaphore("sem_" + k)) for k in self.eng}
        self.cnt = {k: 0 for k in self.eng}
        self.pending = {k: [] for k in self.eng}
        self.waited = {}
        self.nsem = len(self.eng)
        self.dma_res = []
        self.out_tks = []

    def _wait_for(self, en, tickets):
        best = {}
        for tk in tickets:
            if tk is None:
                continue
            if tk.val is None:
                assert tk.eng == en, "pending ticket consumed cross-engine"
                continue
            if tk.eng == en and (en == "pe" or not SAME_ENG_SYNC):
                continue
            key = id(tk.sem)
            if key not in best or best[key].val < tk.val:
                best[key] = tk
        for key, tk in best.items():
            if self.waited.get((en, key), 0) >= tk.val:
                continue
            self.eng[en].wait_ge(tk.sem, tk.val)
            self.waited[(en, key)] = tk.val

    def _deps(self, reads, writes):
        deps = []
        for r in reads:
            deps.append(r.w)
        for w in writes:
            deps.append(w.w)
            deps.extend(w.r)
        return deps

    def emit(self, en, fn, reads=(), writes=(), inc=True):
        self._wait_for(en, self._deps(reads, writes))
        ins = fn(self.eng[en])
        if inc:
            ins.then_inc(self.sem[en], 1)
            self.cnt[en] += 1
            tk = Tk(self.sem[en], self.cnt[en], en)
            for p in self.pending[en]:
                p.val = self.cnt[en]
            self.pending[en] = []
        else:
            tk = Tk(self.sem[en], None, en)
            self.pending[en].append(tk)
        for w in writes:
            w.w = tk
            w.r = []
        for r in reads:
            r.r.append(tk)
        return tk

    def dma(self, q, out, in_, reads, writes, semres=None):
        semres = semres or writes[0]
        if semres.sem is None:
            semres.sem = self.stack.enter_context(self.nc.semaphore("d_" + semres.name))
            self.nsem += 1
            self.dma_res.append(semres)
        self._wait_for(q, self._deps(reads, writes))
        self.eng[q].dma_start(out=out, in_=in_).then_inc(semres.sem, 16)
        semres.cnt += 16
        tk = Tk(semres.sem, semres.cnt, "dma")
        for w in writes:
            w.w = tk
            w.r = []
        for r in reads:
            r.r.append(tk)
        return tk

    def barrier(self):
        tks = [Tk(self.sem[k], self.cnt[k], k) for k in self.eng if self.cnt[k] > 0]
        dtk = [Tk(r.sem, r.cnt, "dma") for r in self.dma_res if r.cnt > 0]
        for en in self.eng:
            self._wait_for(en, [t for t in tks if t.eng != en] + dtk)


def lambda_init_of(layer):
    return 0.8 - 0.6 * math.exp(-0.3 * layer)


def build_program(stop_after=None, debug=False):
    nc = bass.Bass("TRN2", target_bir_lowering=False)

    def din(name, shape, dt=F32):
        return nc.dram_tensor(name, list(shape), dt, kind="ExternalInput").ap()

    xT = din("xT", [128, NCH, S])
    ppT = din("ppT", [4, 128, 2, S])
    vecs_d = din("vecs", [128, NV * 8])
    wgu_d = din("wgu", [8, NG, 128, 2 * 8 * 128 * GH])
    wo_d = din("wo", [8, NG, 128, GH * D])
    plg_d = din("plg", [4, 8, 128, 8 * 128])
    plp_d = din("plp", [4, 8, 128, 2 * 128])
    cwin_d = din("cwin", [2, 8, 128, 2 * 8 * 128])
    cwout_d = din("cwout", [2, 8, 128, 8 * 128])
    wk_d = din("wk", [8, 128, 8 * 128])
    wv_d = din("wv", [2, 128, 8 * 512])
    wq_d = din("wq", [2, 8, 128, 8 * 128])
    wao_d = din("wao", [2, 8, 128, D])
    lv_d = din("lv", [128, 2 * 4 * 64])
    subln_d = din("subln", [128, 2])
    relb_d = din("relb", [32, 8])
    onehot_d = din("onehot", [32, 384])
    maskrow_d = din("maskrow", [1, 384])
    outT = nc.dram_tensor("outT", [128, NCH, S], F32, kind="ExternalOutput").ap()
    dbg = nc.dram_tensor("dbg", [16, 128, NCH, 16, 32], F32, kind="ExternalOutput").ap() if debug else None
    kT_dram = nc.dram_tensor("kT_scr", [NHEAD, 128, S], BF16, kind="Internal").ap()
    v_dram = nc.dram_tensor("v_scr", [NHEAD, 128, 16, 128], BF16, kind="Internal").ap()
    g_dram = nc.dram_tensor("g_scr", [NHEAD, 128, 384], F32, kind="Internal").ap()

    with ExitStack() as stack:
        sc = Sched(nc, stack)
        E = sc.emit

        def sb(name, shape, dt):
            return stack.enter_context(nc.sbuf_tensor("s_" + name, list(shape), dt))

        h = sb("h", [128, NCH, S], F32)
        xn = sb("xn", [128, NCH, S], BF16)
        vecs = sb("vecs", [128, NV * 8], F32)
        ones_bf = sb("ones_bf", [128, 128], BF16)
        ident_bf = sb("ident_bf", [128, 128], BF16)
        wgu = [sb(f"wgu{i}", [128, 2, 8, 128 * GH], BF16) for i in range(2)]
        wo = [sb(f"wo{i}", [128, GH, D], BF16) for i in range(2)]
        actb = [sb(f"actb{i}", [128, GH, TT], BF16) for i in range(2)]
        sq = [sb(f"sq{i}", [128, NCH, TT], BF16) for i in range(2)]
        NSCR = 8
        scr = [sb(f"scr{i}", [128, TT], F32) for i in range(NSCR)]
        wch = [sb(f"wch{i}", [128, 2, 8, 128], BF16) for i in range(NWCH)]
        ps = stack.enter_context(nc.psum_tensor("ps", [128, 8, TT], F32))

        R_h = [[Res(f"h{c}_{t}") for t in range(NT)] for c in range(NCH)]
        R_xn = [Res(f"xn{t}") for t in range(NT)]
        R_vecs = Res("vecs")
        R_const = Res("const")
        R_wgu = [Res(f"wgu{i}") for i in range(2)]
        R_wo = [Res(f"wo{i}") for i in range(2)]
        R_act = [Res(f"act{i}") for i in range(2)]
        R_sq = [Res(f"sq{i}") for i in range(2)]
        R_scr = [Res(f"scr{i}") for i in range(NSCR)]
        R_wch = [Res(f"wch{i}") for i in range(NWCH)]
        R_ps = [Res(f"ps{i}") for i in range(8)]
        rot = {}

        def nxt(key, n):
            rot[key] = (rot.get(key, -1) + 1) % n
            return rot[key]

        def vcol(v, c):
            return vecs[:, v * 8 + c: v * 8 + c + 1]

        def tsl(t):
            return slice(t * TT, (t + 1) * TT)

        sc.dma("sp", vecs[:, :], vecs_d[:, :], [], [R_vecs])
        for t in range(NT):
            sc.dma("sp", h[:, :, tsl(t)], xT[:, :, tsl(t)], [], [R_h[c][t] for c in range(NCH)],
                   semres=Res(f"xload{t}"))
        E("pool", lambda e: e.memset(ones_bf[:, :], 1.0), [], [R_const])
        E("pool", lambda e: e.memset(ident_bf[:, :], 0.0), [], [R_const])
        E("pool", lambda e: e.affine_select(out=ident_bf[:, :], in_=ident_bf[:, :], pattern=[[-1, 128]],
                                            compare_op=ALU.not_equal, fill=1.0, base=0, channel_multiplier=1),
          [], [R_const])

        def stat_matmuls(bank, src_of_c, reads, nchunks=NCH):
            for c in range(nchunks):
                E("pe", lambda e, c=c: e.matmul(ps[:, bank, :], ones_bf[:, :], src_of_c(c),
                                                start=(c == 0), stop=(c == nchunks - 1)),
                  reads + [R_const], [R_ps[bank]], inc=(c == nchunks - 1))

        def rstd_from(bank_ap, bank_res, inv_n, eps, out_i):
            E("act", lambda e: e.activation(out=scr[out_i][:, :], in_=bank_ap, func=AF.Ln, bias=eps, scale=inv_n),
              [bank_res], [R_scr[out_i]])
            E("act", lambda e: e.activation(out=scr[out_i][:, :], in_=scr[out_i][:, :], func=AF.Exp, scale=-0.5),
              [R_scr[out_i]], [R_scr[out_i]])

        def rms_phase(vidx, dst_of=None):
            for t in range(NT):
                si = nxt("sq", 2)
                E("act", lambda e: e.activation(out=sq[si][:, :, :], in_=h[:, :, tsl(t)], func=AF.Square),
                  [R_h[c][t] for c in range(NCH)], [R_sq[si]])
                bank = 6 + nxt("statbank", 2)
                stat_matmuls(bank, lambda c: sq[si][:, c, :], [R_sq[si]])
                ri = nxt("rstd", 2)
                rstd_from(ps[:, bank, :], R_ps[bank], 1.0 / D, RMS_EPS, ri)
                for c in range(NCH):
                    if dst_of is None:
                        E("dve", lambda e, c=c: e.scalar_tensor_tensor(
                            out=xn[:, c, tsl(t)], in0=h[:, c, tsl(t)], scalar=vcol(vidx, c), in1=scr[ri][:, :],
                            op0=ALU.mult, op1=ALU.mult),
                          [R_h[c][t], R_scr[ri], R_vecs], [R_xn[t]])
                    else:
                        dst_of(t, c, ri)

        def ffn_phase(f, vidx):
            rms_phase(vidx)
            units = [(J, t) for J in range(NG) for t in range(NT)]
            slot_of = {}

            def load(J):
                s = nxt("ffnw", 2)
                slot_of[J] = s
                sc.dma("pool", wgu[s][:, :, :, :].rearrange("p a k c -> p (a k c)").rearrange("p (a b) -> p a b", b=2048),
                       wgu_d[f, J].rearrange("p (a b) -> p a b", b=2048), [], [R_wgu[s]])
                sc.dma("pool", wo[s][:, :, :].rearrange("p a c -> p (a c)").rearrange("p (a b) -> p a b", b=1024),
                       wo_d[f, J].rearrange("p (a b) -> p a b", b=1024), [], [R_wo[s]])

            def gu(n, jjs):
                J, t = units[n]
                s = slot_of[J]
                ab = n % 2
                for jj in jjs:
                    gb = nxt("gbank", 2)
                    ub = 2 + nxt("ubank", 2)
                    for k in range(8):
                        E("pe", lambda e, k=k: e.matmul(ps[:, gb, :], wgu[s][:, 0, k, jj * 128:(jj + 1) * 128],
                                                        xn[:, k, tsl(t)], start=(k == 0), stop=(k == 7)),
                          [R_wgu[s], R_xn[t]], [R_ps[gb]], inc=(k == 7))
                    for k in range(8):
                        E("pe", lambda e, k=k: e.matmul(ps[:, ub, :], wgu[s][:, 1, k, jj * 128:(jj + 1) * 128],
                                                        xn[:, k, tsl(t)], start=(k == 0), stop=(k == 7)),
                          [R_wgu[s], R_xn[t]], [R_ps[ub]], inc=(k == 7))
                    si = 2 + nxt("sgb", 2)
                    E("act", lambda e: e.activation(out=scr[si][:, :], in_=ps[:, gb, :], func=AF.Silu),
                      [R_ps[gb]], [R_scr[si]])
                    E("dve", lambda e: e.tensor_tensor(out=actb[ab][:, jj, :], in0=ps[:, ub, :], in1=scr[si][:, :],
                                                       op=ALU.mult),
                      [R_ps[ub], R_scr[si]], [R_act[ab]])

            def oacc(n, ms):
                J, t = units[n]
                s = slot_of[J]
                ab = n % 2
                for m in ms:
                    ob = 4 + nxt("obank", 2)
                    for jj in range(GH):
                        E("pe", lambda e, jj=jj: e.matmul(ps[:, ob, :], wo[s][:, jj, m * 128:(m + 1) * 128],
                                                          actb[ab][:, jj, :], start=(jj == 0), stop=(jj == GH - 1)),
                          [R_wo[s], R_act[ab]], [R_ps[ob]], inc=(jj == GH - 1))
                    E("dve", lambda e: e.scalar_tensor_tensor(out=h[:, m, tsl(t)], in0=ps[:, ob, :], scalar=0.5,
                                                              in1=h[:, m, tsl(t)], op0=ALU.mult, op1=ALU.add),
                      [R_ps[ob], R_h[m][t]], [R_h[m][t]])

            load(0)
            mper = NCH // GH
            for n in range(len(units) + 1):
                for jj in range(GH):
                    if n < len(units):
                        gu(n, [jj])
                    if n >= 1:
                        oacc(n - 1, range(jj * mper, (jj + 1) * mper))
                if n < len(units):
                    J, t = units[n]
                    if t == 0 and J + 1 < NG:
                        load(J + 1)

        def ple_phase(l):
            rms_phase(4 * l + 3)
            ppb = sq[1][:, :, :].rearrange("p a b -> p (a b)").rearrange("p (k t) -> p k t", k=2)
            R_ppb = R_sq[1]
            sc.dma("pool", sq[1][:, :, :].rearrange("p a b -> p (a b)").rearrange("p (a b) -> p a b", b=1024),
                   ppT[l].rearrange("p k t -> p (k t)").rearrange("p (a b) -> p a b", b=1024), [], [R_ppb])
            for m in range(NCH):
                wi = nxt("wch", NWCH)
                sc.dma("pool", wch[wi][:, 0, :, :].rearrange("p k c -> p (k c)"), plg_d[l, m], [], [R_wch[wi]])
                sc.dma("pool", wch[wi][:, 1, 0:2, :].rearrange("p k c -> p (k c)"), plp_d[l, m], [], [R_wch[wi]])
                for t in range(NT):
                    gb = nxt("gbank", 2)
                    ub = 2 + nxt("ubank", 2)
                    for k in range(8):
                        E("pe", lambda e, k=k: e.matmul(ps[:, gb, :], wch[wi][:, 0, k, :], xn[:, k, tsl(t)],
                                                        start=(k == 0), stop=(k == 7)),
                          [R_wch[wi], R_xn[t]], [R_ps[gb]], inc=(k == 7))
                    for k in range(2):
                        E("pe", lambda e, k=k: e.matmul(ps[:, ub, :], wch[wi][:, 1, k, :], ppb[:, k, tsl(t)],
                                                        start=(k == 0), stop=(k == 1)),
                          [R_wch[wi], R_ppb], [R_ps[ub]], inc=(k == 1))
                    si = 2 + nxt("sgb", 2)
                    E("act", lambda e: e.activation(out=scr[si][:, :], in_=ps[:, gb, :], func=AF.Sigmoid),
                      [R_ps[gb]], [R_scr[si]])
                    E("dve", lambda e: e.tensor_tensor(out=scr[si][:, :], in0=ps[:, ub, :], in1=scr[si][:, :],
                                                       op=ALU.mult),
                      [R_ps[ub], R_scr[si]], [R_scr[si]])
                    E("dve", lambda e: e.tensor_tensor(out=h[:, m, tsl(t)], in0=h[:, m, tsl(t)],
                                                       in1=scr[si][:, :], op=ALU.add),
                      [R_scr[si], R_h[m][t]], [R_h[m][t]])

        def conv_phase(l):
            vb = 18 + 37 * l
            rms_phase(4 * l + 1)
            with ExitStack() as st2:
                ubuf = st2.enter_context(nc.sbuf_tensor("s_" + f"ubuf{l}", [128, NCH, PAD + TT], BF16))
                yb = st2.enter_context(nc.sbuf_tensor("s_" + f"yb{l}", [128, NCH, TT], F32))
                dg = [st2.enter_context(nc.sbuf_tensor("s_" + f"dg{l}_{i}", [128, CONVW, 128], BF16)) for i in range(1)]
                R_ub = [Res(f"ubuf{c}") for c in range(NCH)]
                R_yb = [Res(f"yb{c}") for c in range(NCH)]
                R_dg = [Res("dg0")]
                E("pool", lambda e: e.memset(ubuf[:, :, 0:PAD], 0.0), [], R_ub)
                for t in range(NT):
                    for c in range(NCH):
                        wi = nxt("wch", NWCH)
                        sc.dma("pool", wch[wi][:, :, :, :].rearrange("p a k c -> p (a k c)"), cwin_d[l, c],
                               [], [R_wch[wi]])
                        ab_ = nxt("gbank", 2)
                        gb_ = 2 + nxt("ubank", 2)
                        for k in range(8):
                            E("pe", lambda e, k=k: e.matmul(ps[:, ab_, :], wch[wi][:, 0, k, :], xn[:, k, tsl(t)],
                                                            start=(k == 0), stop=(k == 7)),
                              [R_wch[wi], R_xn[t]], [R_ps[ab_]], inc=(k == 7))
                        for k in range(8):
                            E("pe", lambda e, k=k: e.matmul(ps[:, gb_, :], wch[wi][:, 1, k, :], xn[:, k, tsl(t)],
                                                            start=(k == 0), stop=(k == 7)),
                              [R_wch[wi], R_xn[t]], [R_ps[gb_]], inc=(k == 7))
                        si = 2 + nxt("sgb", 2)
                        E("act", lambda e: e.activation(out=scr[si][:, :], in_=ps[:, gb_, :], func=AF.Sigmoid,
                                                        bias=vcol(vb + 1, c)),
                          [R_ps[gb_], R_vecs], [R_scr[si]])
                        if t > 0:
                            E("pool", lambda e: e.tensor_copy(out=ubuf[:, c, 0:PAD], in_=ubuf[:, c, TT:TT + PAD]),
                              [R_ub[c]], [R_ub[c]])
                        E("dve", lambda e: e.scalar_tensor_tensor(out=ubuf[:, c, PAD:PAD + TT], in0=ps[:, ab_, :],
                                                                  scalar=vcol(vb, c), in1=scr[si][:, :],
                                                                  op0=ALU.add, op1=ALU.mult),
                          [R_ps[ab_], R_scr[si], R_vecs], [R_ub[c]])
                    for c in range(NCH):
                        di = 0
                        for j in range(CONVW):
                            E("dve", lambda e, j=j: e.tensor_scalar(out=dg[di][:, j, :], in0=ident_bf[:, :],
                                                                    scalar1=vcol(vb + 6 + j, c), scalar2=None,
                                                                    op0=ALU.mult),
                              [R_const, R_vecs], [R_dg[di]], inc=(j == CONVW - 1))
                        yb_ = 4 + nxt("obank", 2)
                        for j in range(CONVW):
                            E("pe", lambda e, j=j: e.matmul(ps[:, yb_, :], dg[di][:, j, :], ubuf[:, c, j:j + TT],
                                                            start=(j == 0), stop=(j == CONVW - 1)),
                              [R_dg[di], R_ub[c]], [R_ps[yb_]], inc=(j == CONVW - 1))
                        E("act", lambda e: e.activation(out=yb[:, c, :], in_=ps[:, yb_, :], func=AF.Identity,
                                                        bias=vcol(vb + 2, c)),
                          [R_ps[yb_], R_vecs], [R_yb[c]])
                    si_ = nxt("sq", 2)
                    zs = xn[:, :, tsl(t)]
                    R_zs = R_xn[t]
                    E("act", lambda e: e.activation(out=sq[si_][:, :, :], in_=yb[:, :, :], func=AF.Square),
                      R_yb, [R_sq[si_]])
                    E("dve", lambda e: e.tensor_copy(out=zs, in_=yb[:, :, :]), R_yb, [R_zs])
                    b1 = 6 + nxt("statbank", 2)
                    stat_matmuls(b1, lambda c: xn[:, c, tsl(t)], [R_zs])
                    b2 = 6 + nxt("statbank", 2)
                    stat_matmuls(b2, lambda c: sq[si_][:, c, :], [R_sq[si_]])
                    MU, MSQ, RS, NM, Z = 4, 5, 6, 7, 0
                    E("dve", lambda e: e.tensor_scalar(out=scr[MU][:, :], in0=ps[:, b1, :], scalar1=1.0 / D,
                                                       scalar2=None, op0=ALU.mult), [R_ps[b1]], [R_scr[MU]])
                    E("dve", lambda e: e.tensor_tensor(out=scr[MSQ][:, :], in0=scr[MU][:, :], in1=scr[MU][:, :],
                                                       op=ALU.mult), [R_scr[MU]], [R_scr[MSQ]])
                    E("dve", lambda e: e.scalar_tensor_tensor(out=scr[MSQ][:, :], in0=ps[:, b2, :], scalar=1.0 / D,
                                                              in1=scr[MSQ][:, :], op0=ALU.mult, op1=ALU.subtract),
                      [R_ps[b2], R_scr[MSQ]], [R_scr[MSQ]])
                    rstd_from(scr[MSQ][:, :], R_scr[MSQ], 1.0, LN_EPS, RS)
                    E("dve", lambda e: e.scalar_tensor_tensor(out=scr[NM][:, :], in0=scr[MU][:, :], scalar=-1.0,
                                                              in1=scr[RS][:, :], op0=ALU.mult, op1=ALU.mult),
                      [R_scr[MU], R_scr[RS]], [R_scr[NM]])
                    for c in range(NCH):
                        zi = nxt("z", 2)
                        E("dve", lambda e: e.tensor_tensor(out=scr[zi][:, :], in0=yb[:, c, :], in1=scr[RS][:, :],
                                                           op=ALU.mult), [R_yb[c], R_scr[RS]], [R_scr[zi]])
                        E("dve", lambda e: e.tensor_tensor(out=scr[zi][:, :], in0=scr[zi][:, :], in1=scr[NM][:, :],
                                                           op=ALU.add), [R_scr[zi], R_scr[NM]], [R_scr[zi]])
                        E("act", lambda e: e.activation(out=xn[:, c, tsl(t)], in_=scr[zi][:, :], func=AF.Silu,
                                                        bias=vcol(vb + 4, c), scale=vcol(vb + 3, c)),
                          [R_scr[zi], R_vecs], [R_zs])
                    for m in range(NCH):
                        wi = nxt("wch", NWCH)
                        sc.dma("pool", wch[wi][:, 0, :, :].rearrange("p k c -> p (k c)"), cwout_d[l, m],
                               [], [R_wch[wi]])
                        ob = 4 + nxt("obank", 2)
                        for k in range(8):
                            E("pe", lambda e, k=k: e.matmul(ps[:, ob, :], wch[wi][:, 0, k, :], xn[:, k, tsl(t)],
                                                            start=(k == 0), stop=(k == 7)),
                              [R_wch[wi], R_zs], [R_ps[ob]], inc=(k == 7))
                        E("dve", lambda e: e.scalar_tensor_tensor(out=h[:, m, tsl(t)], in0=ps[:, ob, :],
                                                                  scalar=vcol(vb + 5, m), in1=h[:, m, tsl(t)],
                                                                  op0=ALU.add, op1=ALU.add),
                          [R_ps[ob], R_h[m][t], R_vecs], [R_h[m][t]])
                sc.barrier()

        def kv_phase():
            rms_phase(16)
            with ExitStack() as st2:
                kst = [st2.enter_context(nc.sbuf_tensor("s_" + f"kst{i}", [128, S], BF16)) for i in range(2)]
                vst = [st2.enter_context(nc.sbuf_tensor("s_" + f"vst{i}", [128, D], BF16)) for i in range(2)]
                wvb = [st2.enter_context(nc.sbuf_tensor("s_" + f"wvb{i}", [128, 8, 512], BF16)) for i in range(2)]
                R_kst = [Res("kst0"), Res("kst1")]
                R_vst = [Res("vst0"), Res("vst1")]
                R_wvb = [Res("wvb0"), Res("wvb1")]
                for hh in range(NHEAD):
                    wi = nxt("wch", NWCH)
                    sc.dma("pool", wch[wi][:, 0, :, :].rearrange("p k c -> p (k c)"), wk_d[hh], [], [R_wch[wi]])
                    ks = nxt("kst", 2)
                    for t in range(NT):
                        bk = nxt("gbank", 2)
                        for k in range(8):
                            E("pe", lambda e, k=k: e.matmul(ps[:, bk, :], wch[wi][:, 0, k, :], xn[:, k, tsl(t)],
                                                            start=(k == 0), stop=(k == 7)),
                              [R_wch[wi], R_xn[t]], [R_ps[bk]], inc=(k == 7))
                        E("act", lambda e: e.activation(out=kst[ks][:, tsl(t)], in_=ps[:, bk, :], func=AF.Identity),
                          [R_ps[bk]], [R_kst[ks]])
                    sc.dma("sp", kT_dram[hh], kst[ks][:, :], [R_kst[ks]], [R_kv], semres=R_kstout[ks])
                for half in range(2):
                    sc.dma("pool", wvb[half][:, :, :].rearrange("p k c -> p (k c)").rearrange("p (a b) -> p a b", b=2048),
                           wv_d[half].rearrange("p (a b) -> p a b", b=2048), [], [R_wvb[half]])
                for kt in range(16):
                    vs = nxt("vst", 2)
                    for half in range(2):
                        bk = 2 + nxt("ubank", 2)
                        for k in range(8):
                            E("pe", lambda e, k=k: e.matmul(ps[:, bk, :], xn[:, k, kt * 128:(kt + 1) * 128],
                                                            wvb[half][:, k, :], start=(k == 0), stop=(k == 7)),
                              [R_wvb[half], R_xn[kt // 4]], [R_ps[bk]], inc=(k == 7))
                        E("dve", lambda e: e.tensor_copy(out=vst[vs][:, half * 512:(half + 1) * 512], in_=ps[:, bk, :]),
                          [R_ps[bk]], [R_vst[vs]])
                    sc.dma("sp", v_dram[:, :, kt, :].rearrange("h p e -> p h e"),
                           vst[vs][:, :].rearrange("p (h e) -> p h e", e=128), [R_vst[vs]], [R_kv],
                           semres=R_vstout[vs])
                sc.barrier()

        R_kv = Res("kvdram")
        R_kstout = [Res("kstout0"), Res("kstout1")]
        R_vstout = [Res("vstout0"), Res("vstout1")]

        def attn_phase(layer, j_, A):
            lam_init = lambda_init_of(layer)
            rms_phase(4 * layer + 1)
            kTh, vh, qT, pT, wob = A["kTh"], A["vh"], A["qT"], A["pT"], A["wob"]
            R_kTh, R_vh, R_qT, R_pT, R_wob = A["R_kTh"], A["R_vh"], A["R_qT"], A["R_pT"], A["R_wob"]
            small, R_small = A["small"], A["R_small"]
            lvb = A["lvb"]
            base = j_ * 256
            for i2 in range(2):
                E("dve", lambda e, i2=i2: e.tensor_tensor(out=A["ltmp"][:, :], in0=lvb[:, base + i2 * 128: base + i2 * 128 + 64],
                                                          in1=lvb[:, base + i2 * 128 + 64: base + i2 * 128 + 128],
                                                          op=ALU.mult), [A["R_lvb"]], [A["R_ltmp"]])
                E("dve", lambda e, i2=i2: e.tensor_reduce(out=small[:, i2:i2 + 1], in_=A["ltmp"][:, :],
                                                          axis=mybir.AxisListType.X, op=ALU.add),
                  [A["R_ltmp"]], [R_small])
            E("act", lambda e: e.activation(out=small[:, 0:2], in_=small[:, 0:2], func=AF.Exp), [R_small], [R_small])
            E("dve", lambda e: e.scalar_tensor_tensor(out=small[:, 2:3], in0=small[:, 1:2], scalar=-lam_init,
                                                      in1=small[:, 0:1], op0=ALU.add, op1=ALU.subtract),
              [R_small], [R_small])
            E("dve", lambda e: e.tensor_scalar(out=small[:, 3:4], in0=A["sublnb"][:, j_:j_ + 1],
                                               scalar1=1.0 - lam_init, scalar2=None, op0=ALU.mult),
              [A["R_sublnb"]], [R_small])
            biasd, biasn, b31 = A["biasd"], A["biasn"], A["b31"]
            R_bias = A["R_bias"]
            for hh in range(NHEAD):
                hs = 0
                ws = hh % 2
                wi = nxt("wch", NWCH)
                sc.dma("pool", wch[wi][:, 0, :, :].rearrange("p k c -> p (k c)"), wq_d[j_, hh], [], [R_wch[wi]])
                sc.dma("pool", wob[ws][:, :], wao_d[j_, hh], [], [R_wob[ws]])
                sc.dma("sp", kTh[hs][:, :], kT_dram[hh], [R_kv], [R_kTh[hs]])
                sc.dma("sp", vh[hs][:, :, :], v_dram[hh], [R_kv], [R_vh[hs]])
                for t in range(NT):
                    bk = 6 + nxt("statbank", 2)
                    for k in range(8):
                        E("pe", lambda e, k=k: e.matmul(ps[:, bk, :], wch[wi][:, 0, k, :], xn[:, k, tsl(t)],
                                                        start=(k == 0), stop=(k == 7)),
                          [R_wch[wi], R_xn[t]], [R_ps[bk]], inc=(k == 7))
                    E("dve", lambda e: e.tensor_copy(out=qT[hs][:, tsl(t)], in_=ps[:, bk, :]),
                      [R_ps[bk]], [R_qT[hs]])
                for qi in range(NT):
                    nk = 4 * (qi + 1)
                    NUM = [2, 3]
                    DEN = [4, 5]
                    for kj in range(nk):
                        r = kj - 4 * qi
                        c0 = max(r, 0)
                        qs = slice(qi * TT + c0 * 128, (qi + 1) * TT)
                        cs = slice(c0 * 128, TT)
                        pi = nxt("pT", 2)
                        for mp in range(2):
                            sbk = mp
                            rows = slice(mp * 64, (mp + 1) * 64)
                            E("pe", lambda e: e.matmul(ps[:, sbk, cs], kTh[hs][rows, kj * 128:(kj + 1) * 128],
                                                       qT[hs][rows, qs], start=True, stop=True),
                              [R_kTh[hs], R_qT[hs]], [R_ps[sbk]])
                            pr = R_pT[pi][mp]
                            pt = pT[pi][mp]
                            if r <= -2:
                                E("act", lambda e: e.activation(out=pt[:, cs], in_=ps[:, sbk, cs], func=AF.Exp,
                                                                bias=b31[:, hh:hh + 1], scale=0.125),
                                  [R_ps[sbk], R_bias], [pr])
                            else:
                                for sbl in range(c0, 4):
                                    ss = slice(sbl * 128, (sbl + 1) * 128)
                                    dd = sbl - r
                                    if dd >= 2:
                                        E("act", lambda e, ss=ss: e.activation(out=pt[:, ss], in_=ps[:, sbk, ss],
                                                                               func=AF.Exp, bias=b31[:, hh:hh + 1],
                                                                               scale=0.125),
                                          [R_ps[sbk], R_bias], [pr])
                                    else:
                                        bt = biasd if dd == 0 else biasn
                                        ti = 2 + nxt("sgb", 2)
                                        E("dve", lambda e, ss=ss, bt=bt: e.scalar_tensor_tensor(
                                            out=scr[ti][:, 0:128], in0=ps[:, sbk, ss], scalar=0.125,
                                            in1=bt[:, hh, :], op0=ALU.mult, op1=ALU.add),
                                          [R_ps[sbk], R_bias], [R_scr[ti]])
                                        E("act", lambda e, ss=ss: e.activation(out=pt[:, ss], in_=scr[ti][:, 0:128],
                                                                               func=AF.Exp),
                                          [R_scr[ti]], [pr])
                        for mp in range(2):
                            pr = R_pT[pi][mp]
                            pt = pT[pi][mp]
                            E("pe", lambda e: e.matmul(ps[:, NUM[mp], cs], vh[hs][:, kj, :], pt[:, cs],
                                                       start=(kj == 0), stop=(kj == nk - 1)),
                              [R_vh[hs], pr], [R_ps[NUM[mp]]], inc=(kj == nk - 1))
                            E("pe", lambda e: e.matmul(ps[:, DEN[mp], cs], ones_bf[:, :], pt[:, cs],
                                                       start=(kj == 0), stop=(kj == nk - 1)),
                              [R_const, pr], [R_ps[DEN[mp]]], inc=True)
                    R1, R2, O1, T2 = 4, 5, 6, 7
                    for mp, ri in ((0, R1), (1, R2)):
                        E("act", lambda e, mp=mp, ri=ri: e.activation(out=scr[ri][:, :], in_=ps[:, DEN[mp], :],
                                                                      func=AF.Ln), [R_ps[DEN[mp]]], [R_scr[ri]])
                        E("act", lambda e, ri=ri: e.activation(out=scr[ri][:, :], in_=scr[ri][:, :], func=AF.Exp,
                                                               scale=-1.0), [R_scr[ri]], [R_scr[ri]])
                    E("dve", lambda e: e.tensor_tensor(out=scr[O1][:, :], in0=ps[:, NUM[0], :], in1=scr[R1][:, :],
                                                       op=ALU.mult), [R_ps[NUM[0]], R_scr[R1]], [R_scr[O1]])
                    E("dve", lambda e: e.tensor_tensor(out=scr[T2][:, :], in0=ps[:, NUM[1], :], in1=scr[R2][:, :],
                                                       op=ALU.mult), [R_ps[NUM[1]], R_scr[R2]], [R_scr[T2]])
                    E("dve", lambda e: e.scalar_tensor_tensor(out=scr[O1][:, :], in0=scr[T2][:, :],
                                                              scalar=small[:, 2:3], in1=scr[O1][:, :],
                                                              op0=ALU.mult, op1=ALU.add),
                      [R_scr[T2], R_scr[O1], R_small], [R_scr[O1]])
                    si = nxt("sq", 2)
                    E("act", lambda e: e.activation(out=sq[si][:, 0, :], in_=scr[O1][:, :], func=AF.Square),
                      [R_scr[O1]], [R_sq[si]])
                    bk = 6 + nxt("statbank", 2)
                    stat_matmuls(bk, lambda c: sq[si][:, 0, :], [R_sq[si]], nchunks=1)
                    rstd_from(ps[:, bk, :], R_ps[bk], 1.0 / 128, RMS_EPS, T2)
                    E("dve", lambda e: e.scalar_tensor_tensor(out=sq[si][:, 1, :], in0=scr[O1][:, :],
                                                              scalar=small[:, 3:4], in1=scr[T2][:, :],
                                                              op0=ALU.mult, op1=ALU.mult),
                      [R_scr[O1], R_scr[T2], R_small, R_sq[si]], [R_sq[si]])
                    for m in range(NCH):
                        ob = 6 + nxt("statbank", 2)
                        E("pe", lambda e: e.matmul(ps[:, ob, :], wob[ws][:, m * 128:(m + 1) * 128], sq[si][:, 1, :],
                                                   start=True, stop=True), [R_wob[ws], R_sq[si]], [R_ps[ob]])
                        E("dve", lambda e: e.tensor_tensor(out=h[:, m, tsl(qi)], in0=ps[:, ob, :],
                                                           in1=h[:, m, tsl(qi)], op=ALU.add),
                          [R_ps[ob], R_h[m][qi]], [R_h[m][qi]])

        def attn_setup(st2):
            A = {}
            A["kTh"] = [st2.enter_context(nc.sbuf_tensor("s_" + f"kTh{i}", [128, S], BF16)) for i in range(1)]
            A["vh"] = [st2.enter_context(nc.sbuf_tensor("s_" + f"vh{i}", [128, 16, 128], BF16)) for i in range(1)]
            A["qT"] = [st2.enter_context(nc.sbuf_tensor("s_" + f"qT{i}", [128, S], BF16)) for i in range(1)]
            A["pT"] = [[st2.enter_context(nc.sbuf_tensor("s_" + f"pT{i}_{m}", [128, TT], BF16)) for m in range(2)]
                       for i in range(2)]
            A["wob"] = [st2.enter_context(nc.sbuf_tensor("s_" + f"wob{i}", [128, D], BF16)) for i in range(2)]
            A["small"] = st2.enter_context(nc.sbuf_tensor("s_small", [128, 8], F32))
            A["ltmp"] = st2.enter_context(nc.sbuf_tensor("s_ltmp", [128, 64], F32))
            A["lvb"] = st2.enter_context(nc.sbuf_tensor("s_lvb", [128, 512], F32))
            A["sublnb"] = st2.enter_context(nc.sbuf_tensor("s_sublnb", [128, 2], F32))
            A["biasd"] = st2.enter_context(nc.sbuf_tensor("s_biasd", [128, NHEAD, 128], F32))
            A["biasn"] = st2.enter_context(nc.sbuf_tensor("s_biasn", [128, NHEAD, 128], F32))
            A["b31"] = st2.enter_context(nc.sbuf_tensor("s_b31", [128, NHEAD], F32))
            gsb = scr[0][:, 0:384]
            oneh = scr[1][0:32, 0:384]
            maskr = scr[2][0:1, 0:384]
            ones1 = scr[3][0:1, 0:128]
            ones32 = scr[4][0:32, 0:128]
            lhs = scr[5][0:32, 0:128]
            relb = scr[6][0:32, 0:8]
            for k in ("kTh", "vh", "qT", "wob"):
                A["R_" + k] = [Res(k + "0"), Res(k + "1")]
            A["R_pT"] = [[Res(f"pT{i}_{m}") for m in range(2)] for i in range(2)]
            for k in ("small", "ltmp", "lvb", "sublnb", "bias"):
                A["R_" + k] = Res(k)
            R_gsb, R_oneh, R_maskr, R_ones1, R_ones32, R_lhs, R_relb = (R_scr[i] for i in range(7))
            R_g = Res("gdram")
            sc.dma("sp", A["lvb"][:, :], lv_d[:, :], [], [A["R_lvb"]])
            sc.dma("sp", A["sublnb"][:, :], subln_d[:, :], [], [A["R_sublnb"]])
            sc.dma("sp", relb, relb_d[:, :], [], [R_relb], semres=Res("relb"))
            sc.dma("sp", oneh, onehot_d[:, :], [], [R_oneh], semres=Res("oneh"))
            sc.dma("sp", maskr, maskrow_d[:, :], [], [R_maskr], semres=Res("maskr"))
            E("pool", lambda e: e.memset(ones1, 1.0), [], [R_ones1])
            E("pool", lambda e: e.memset(ones32, 1.0), [], [R_ones32])
            for hh in range(NHEAD):
                E("dve", lambda e: e.tensor_scalar(out=lhs, in0=ones32, scalar1=relb[:, hh:hh + 1],
                                                   scalar2=None, op0=ALU.mult), [R_ones32, R_relb], [R_lhs])
                E("pe", lambda e: e.matmul(ps[:, 0, 0:384], lhs, oneh, start=True, stop=False),
                  [R_lhs, R_oneh], [R_ps[0]], inc=False)
                E("pe", lambda e: e.matmul(ps[:, 0, 0:384], ones1, maskr, start=False, stop=True),
                  [R_ones1, R_maskr], [R_ps[0]])
                E("dve", lambda e: e.tensor_copy(out=gsb, in_=ps[:, 0, 0:384]), [R_ps[0]], [R_gsb])
                E("dve", lambda e: e.tensor_copy(out=A["b31"][:, hh:hh + 1], in_=gsb[:, 383:384]),
                  [R_gsb], [A["R_bias"]])
                sc.dma("sp", g_dram[hh], gsb, [R_gsb], [R_g], semres=Res(f"gout{hh}"))
            for hh in range(NHEAD):
                for dd, bt in ((0, A["biasd"]), (128, A["biasn"])):
                    src = bass.AP(tensor=g_dram.tensor, offset=g_dram[hh].offset + 128 + dd,
                                  ap=[[383, 128], [1, 128]])
                    sc.dma("sp", bt[:, hh, :], src, [R_g], [A["R_bias"]], semres=Res(f"bt{hh}_{dd}"))
            return A

        def dump_h():
            for t in range(NT):
                sc.out_tks.append(sc.dma("sp", outT[:, :, tsl(t)], h[:, :, tsl(t)],
                                         [R_h[c][t] for c in range(NCH)], [R_out], semres=Res(f"dump{t}")))

        R_out = Res("out")
        done = [False]

        nph = [0]

        def check(name):
            if debug:
                src = h[:, :, :].rearrange("p c (a b) -> p c a b", b=128)[:, :, :, 0:32]
                sc.out_tks.append(sc.dma("sp", dbg[nph[0]], src, [R_h[c][t] for c in range(NCH) for t in range(NT)],
                                         [R_out], semres=Res(f"dbg{nph[0]}")))
                nph[0] += 1
            if stop_after == name and not done[0]:
                dump_h()
                done[0] = True
            return done[0]

        def forward():
            for l in range(2):
                ffn_phase(2 * l, 4 * l + 0)
                if check(f"ffn1_{l}"):
                    return
                conv_phase(l)
                if check(f"mix_{l}"):
                    return
                ffn_phase(2 * l + 1, 4 * l + 2)
                if check(f"ffn2_{l}"):
                    return
                ple_phase(l)
                if check(f"ple_{l}"):
                    return
            kv_phase()
            with ExitStack() as st2:
                A = attn_setup(st2)
                for l in range(2, 4):
                    ffn_phase(2 * l, 4 * l + 0)
                    if check(f"ffn1_{l}"):
                        return
                    attn_phase(l, l - 2, A)
                    if check(f"mix_{l}"):
                        return
                    ffn_phase(2 * l + 1, 4 * l + 2)
                    if check(f"ffn2_{l}"):
                        return
                    ple_phase(l)
                    if check(f"ple_{l}"):
                        return
                sc.barrier()
            with ExitStack() as st2:
                ob = [st2.enter_context(nc.sbuf_tensor("s_" + f"outb{i}", [128, NCH, TT], F32)) for i in range(2)]
                R_ob = [Res("outb0"), Res("outb1")]
                cur = {}

                def dst(t, c, ri):
                    if c == 0:
                        cur["i"] = nxt("outb", 2)
                    oi = cur["i"]
                    E("dve", lambda e: e.scalar_tensor_tensor(out=ob[oi][:, c, :], in0=h[:, c, tsl(t)],
                                                              scalar=vcol(17, c), in1=scr[ri][:, :],
                                                              op0=ALU.mult, op1=ALU.mult),
                      [R_h[c][t], R_scr[ri], R_vecs], [R_ob[oi]])
                    if c == NCH - 1:
                        sc.out_tks.append(sc.dma("sp", outT[:, :, tsl(t)], ob[oi][:, :, :], [R_ob[oi]], [R_out],
                                                 semres=Res(f"fin{t}")))

                rms_phase(17, dst_of=dst)
                sc.barrier()

        forward()
        sc._wait_for("sp", sc.out_tks)
        sc.barrier()
        print("sched counts:", sc.cnt, "nsem", sc.nsem)
    return nc


def _rel_bucket_np(n):
    n = np.maximum(n, 0)
    max_exact = 16
    nf = np.maximum(n, 1).astype(np.float32)
    large = max_exact + (np.log(nf / max_exact) / math.log(128 / max_exact) * (32 - max_exact)).astype(np.int32)
    large = np.minimum(large, 31)
    return np.where(n < max_exact, n, large)


def _bucket_table():
    return _rel_bucket_np(np.arange(0, 256))


def prep_shared(inp):
    f = np.float32
    A = {k: np.asarray(v, dtype=f) for k, v in inp.items()}
    sh = {}
    vl = []
    for l in range(4):
        vl += [A["ffn1_norm"][l], A["mix_norm"][l], A["ffn2_norm"][l], A["ple_norm"][l]]
    vl += [A["kv_norm"], A["final_norm"]]
    for l in range(2):
        vl += [A["conv_b_in"][l][:D], A["conv_b_in"][l][D:], A["conv_b_dw"][l], A["conv_ln_g"][l],
               A["conv_ln_b"][l], A["conv_b_out"][l]]
        vl += [A["conv_w_dw"][l][j] for j in range(CONVW)]
    V = np.stack(vl, 0)
    assert V.shape[0] == NV
    sh["vecs"] = np.ascontiguousarray(V.reshape(NV, 8, 128).transpose(2, 0, 1).reshape(128, NV * 8))
    wgu, wo = [], []
    for l in range(4):
        for nm in ("ffn1", "ffn2"):
            wi = A[nm + "_w_in"][l]
            wgu.append(wi.reshape(8, 128, 2, NG, 128 * GH).transpose(3, 1, 2, 0, 4).reshape(NG, 128, -1))
            wt = A[nm + "_w_out"][l]
            wo.append(wt.reshape(NG, GH, 128, D).transpose(0, 2, 1, 3).reshape(NG, 128, -1))
    sh["wgu"] = np.ascontiguousarray(np.stack(wgu, 0))
    sh["wo"] = np.ascontiguousarray(np.stack(wo, 0))

    def sq_tiles(W):
        return W.reshape(8, 128, 8, 128).transpose(2, 1, 0, 3).reshape(8, 128, 1024)

    def head_tiles(W):
        return W.reshape(8, 128, 2, 8, 64).transpose(3, 1, 0, 2, 4).reshape(8, 128, 1024)

    sh["plg"] = np.ascontiguousarray(np.stack([sq_tiles(A["ple_w_gate"][l]) for l in range(4)], 0))
    sh["plp"] = np.ascontiguousarray(np.stack(
        [A["ple_w_proj"][l].reshape(2, 128, 8, 128).transpose(2, 1, 0, 3).reshape(8, 128, 256) for l in range(4)], 0))
    sh["cwin"] = np.ascontiguousarray(np.stack(
        [A["conv_w_in"][l].reshape(8, 128, 2, 8, 128).transpose(3, 1, 2, 0, 4).reshape(8, 128, -1)
         for l in range(2)], 0))
    sh["cwout"] = np.ascontiguousarray(np.stack([sq_tiles(A["conv_w_out"][l]) for l in range(2)], 0))
    sh["wk"] = np.ascontiguousarray(head_tiles(A["w_kv"][:, :D]))
    wv = A["w_kv"][:, D:]
    sh["wv"] = np.ascontiguousarray(wv.reshape(8, 128, 2, 512).transpose(2, 1, 0, 3).reshape(2, 128, -1))
    sh["wq"] = np.ascontiguousarray(np.stack([head_tiles(A["attn_w_q"][l]) for l in range(2)], 0))
    sh["wao"] = np.ascontiguousarray(np.stack([A["attn_w_o"][l].reshape(8, 128, D) for l in range(2)], 0))
    lv = np.stack([np.concatenate([A["attn_lq1"][l], A["attn_lk1"][l], A["attn_lq2"][l], A["attn_lk2"][l]])
                   for l in range(2)], 0).reshape(1, 512)
    sh["lv"] = np.ascontiguousarray(np.broadcast_to(lv, (128, 512)))
    sh["subln"] = np.ascontiguousarray(A["attn_subln"].T)
    sh["relb"] = np.ascontiguousarray(A["rel_bias"])
    bt = _bucket_table()
    oh = np.zeros((32, 384), f)
    mr = np.zeros((1, 384), f)
    for npr in range(384):
        n = npr - 128
        if n < 0:
            mr[0, npr] = -1e30
        else:
            oh[bt[min(n, 255)], npr] = 1.0
    sh["onehot"] = oh
    sh["maskrow"] = mr
    return sh


def prep_core(inp, b):
    x = np.asarray(inp["x"], dtype=np.float32)[b]
    p = np.asarray(inp["p"], dtype=np.float32)[:, b]
    xT = np.ascontiguousarray(x.T.reshape(8, 128, S).transpose(1, 0, 2))
    ppT = np.ascontiguousarray(p.transpose(0, 2, 1).reshape(4, 2, 128, S).transpose(0, 2, 1, 3))
    return {"xT": xT, "ppT": ppT}


_PROG = {}


def run(inputs, stop_after=None, trace=False, debug=False):
    key = (stop_after, debug)
    if key not in _PROG:
        _PROG[key] = build_program(stop_after, debug)
    nc = _PROG[key]
    sh = prep_shared(inputs)
    in_maps = []
    for b in range(8):
        m = dict(sh)
        m.update(prep_core(inputs, b))
        in_maps.append(m)
    res = run_bass_kernel_spmd(nc, in_maps, core_ids=list(range(8)), **({"trace": True} if trace else {}))
    outs = []
    for b in range(8):
        o = np.asarray(res.results[b]["outT"])
        outs.append(o.transpose(1, 0, 2).reshape(D, S).T)
    return np.ascontiguousarray(np.stack(outs, 0).astype(np.float32)), res


def kernel(**inputs):
    out, _ = run(inputs)
    return out
```

```python
import math
from contextlib import ExitStack

import numpy as np
import concourse.bass as bass
import concourse.mybir as mybir
from concourse.bass_utils import run_bass_kernel_spmd
from concourse.alu_op_type import AluOpType as ALU

F32 = mybir.dt.float32
BF16 = mybir.dt.bfloat16
AF = mybir.ActivationFunctionType

D = 1024
S = 2048
NCH = 8
TT = 512
NT = S // TT
DFF = 4096
GH = 2
NG = DFF // (128 * GH)
CONVW = 31
PAD = CONVW - 1
NHEAD = 8
RMS_EPS = 1e-6
LN_EPS = 1e-5
NV = 18 + 37 * 2
SAME_ENG_SYNC = True
NWCH = 3
DEN_ENG1 = "dve"


class Tk:
    __slots__ = ("sem", "val", "eng")

    def __init__(self, sem, val, eng=None):
        self.sem, self.val, self.eng = sem, val, eng


class Res:
    def __init__(self, name):
        self.name = name
        self.w = None
        self.r = []
        self.sem = None
        self.cnt = 0


class Sched:
    def __init__(self, nc, stack):
        self.nc = nc
        self.stack = stack
        self.eng = {"pe": nc.tensor, "act": nc.scalar, "dve": nc.vector, "pool": nc.gpsimd, "sp": nc.sync}
        self.sem = {k: stack.enter_context(nc.semaphore("sem_" + k)) for k in self.eng}
        self.cnt = {k: 0 for k in self.eng}
        self.pending = {k: [] for k in self.eng}
        self.waited = {}
        self.nsem = len(self.eng)
        self.dma_res = []
        self.out_tks = []

    def _wait_for(self, en, tickets):
        best = {}
        for tk in tickets:
            if tk is None:
                continue
            if tk.val is None:
                assert tk.eng == en, "pending ticket consumed cross-engine"
                continue
            if tk.eng == en and (en == "pe" or not SAME_ENG_SYNC):
                continue
            key = id(tk.sem)
            if key not in best or best[key].val < tk.val:
                best[key] = tk
        for key, tk in best.items():
            if self.waited.get((en, key), 0) >= tk.val:
                continue
            self.eng[en].wait_ge(tk.sem, tk.val)
            self.waited[(en, key)] = tk.val

    def _deps(self, reads, writes):
        deps = []
        for r in reads:
            deps.append(r.w)
        for w in writes:
            deps.append(w.w)
            deps.extend(w.r)
        return deps

    def emit(self, en, fn, reads=(), writes=(), inc=True):
        self._wait_for(en, self._deps(reads, writes))
        ins = fn(self.eng[en])
        if inc:
            ins.then_inc(self.sem[en], 1)
            self.cnt[en] += 1
            tk = Tk(self.sem[en], self.cnt[en], en)
            for p in self.pending[en]:
                p.val = self.cnt[en]
            self.pending[en] = []
        else:
            tk = Tk(self.sem[en], None, en)
            self.pending[en].append(tk)
        for w in writes:
            w.w = tk
            w.r = []
        for r in reads:
            r.r.append(tk)
        return tk

    def dma(self, q, out, in_, reads, writes, semres=None):
        semres = semres or writes[0]
        if semres.sem is None:
            semres.sem = self.stack.enter_context(self.nc.semaphore("d_" + semres.name))
            self.nsem += 1
            self.dma_res.append(semres)
        self._wait_for(q, self._deps(reads, writes))
        self.eng[q].dma_start(out=out, in_=in_).then_inc(semres.sem, 16)
        semres.cnt += 16
        tk = Tk(semres.sem, semres.cnt, "dma")
        for w in writes:
            w.w = tk
            w.r = []
        for r in reads:
            r.r.append(tk)
        return tk

    def barrier(self):
        tks = [Tk(self.sem[k], self.cnt[k], k) for k in self.eng if self.cnt[k] > 0]
        dtk = [Tk(r.sem, r.cnt, "dma") for r in self.dma_res if r.cnt > 0]
        for en in self.eng:
            self._wait_for(en, [t for t in tks if t.eng != en] + dtk)


def lambda_init_of(layer):
    return 0.8 - 0.6 * math.exp(-0.3 * layer)


def build_program(stop_after=None, debug=False):
    nc = bass.Bass("TRN2", target_bir_lowering=False)

    def din(name, shape, dt=F32):
        return nc.dram_tensor(name, list(shape), dt, kind="ExternalInput").ap()

    xT = din("xT", [128, NCH, S])
    ppT = din("ppT", [4, 128, 2, S])
    vecs_d = din("vecs", [128, NV * 8])
    wgu_d = din("wgu", [8, NG, 128, 2 * 8 * 128 * GH])
    wo_d = din("wo", [8, NG, 128, GH * D])
    plg_d = din("plg", [4, 8, 128, 8 * 128])
    plp_d = din("plp", [4, 8, 128, 2 * 128])
    cwin_d = din("cwin", [2, 8, 128, 2 * 8 * 128])
    cwout_d = din("cwout", [2, 8, 128, 8 * 128])
    wk_d = din("wk", [8, 128, 8 * 128])
    wv_d = din("wv", [2, 128, 8 * 512])
    wq_d = din("wq", [2, 8, 128, 8 * 128])
    wao_d = din("wao", [2, 8, 128, D])
    lv_d = din("lv", [128, 2 * 4 * 64])
    subln_d = din("subln", [128, 2])
    relb_d = din("relb", [32, 8])
    onehot_d = din("onehot", [32, 384])
    maskrow_d = din("maskrow", [1, 384])
    outT = nc.dram_tensor("outT", [128, NCH, S], F32, kind="ExternalOutput").ap()
    dbg = nc.dram_tensor("dbg", [16, 128, NCH, 16, 32], F32, kind="ExternalOutput").ap() if debug else None
    kT_dram = nc.dram_tensor("kT_scr", [NHEAD, 128, S], BF16, kind="Internal").ap()
    v_dram = nc.dram_tensor("v_scr", [NHEAD, 128, 16, 128], BF16, kind="Internal").ap()
    g_dram = nc.dram_tensor("g_scr", [NHEAD, 128, 384], F32, kind="Internal").ap()

    with ExitStack() as stack:
        sc = Sched(nc, stack)
        E = sc.emit

        def sb(name, shape, dt):
            return stack.enter_context(nc.sbuf_tensor("s_" + name, list(shape), dt))

        h = sb("h", [128, NCH, S], F32)
        xn = sb("xn", [128, NCH, S], BF16)
        vecs = sb("vecs", [128, NV * 8], F32)
        ones_bf = sb("ones_bf", [128, 128], BF16)
        ident_bf = sb("ident_bf", [128, 128], BF16)
        wgu = [sb(f"wgu{i}", [128, 2, 8, 128 * GH], BF16) for i in range(2)]
        wo = [sb(f"wo{i}", [128, GH, D], BF16) for i in range(2)]
        actb = [sb(f"actb{i}", [128, GH, TT], BF16) for i in range(2)]
        sq = [sb(f"sq{i}", [128, NCH, TT], BF16) for i in range(2)]
        NSCR = 8
        scr = [sb(f"scr{i}", [128, TT], F32) for i in range(NSCR)]
        wch = [sb(f"wch{i}", [128, 2, 8, 128], BF16) for i in range(NWCH)]
        ps = stack.enter_context(nc.psum_tensor("ps", [128, 8, TT], F32))

        R_h = [[Res(f"h{c}_{t}") for t in range(NT)] for c in range(NCH)]
        R_xn = [Res(f"xn{t}") for t in range(NT)]
        R_vecs = Res("vecs")
        R_const = Res("const")
        R_wgu = [Res(f"wgu{i}") for i in range(2)]
        R_wo = [Res(f"wo{i}") for i in range(2)]
        R_act = [Res(f"act{i}") for i in range(2)]
        R_sq = [Res(f"sq{i}") for i in range(2)]
        R_scr = [Res(f"scr{i}") for i in range(NSCR)]
        R_wch = [Res(f"wch{i}") for i in range(NWCH)]
        R_ps = [Res(f"ps{i}") for i in range(8)]
        rot = {}

        def nxt(key, n):
            rot[key] = (rot.get(key, -1) + 1) % n
            return rot[key]

        def vcol(v, c):
            return vecs[:, v * 8 + c: v * 8 + c + 1]

        def tsl(t):
            return slice(t * TT, (t + 1) * TT)

        sc.dma("sp", vecs[:, :], vecs_d[:, :], [], [R_vecs])
        for t in range(NT):
            sc.dma("sp", h[:, :, tsl(t)], xT[:, :, tsl(t)], [], [R_h[c][t] for c in range(NCH)],
                   semres=Res(f"xload{t}"))
        E("pool", lambda e: e.memset(ones_bf[:, :], 1.0), [], [R_const])
        E("pool", lambda e: e.memset(ident_bf[:, :], 0.0), [], [R_const])
        E("pool", lambda e: e.affine_select(out=ident_bf[:, :], in_=ident_bf[:, :], pattern=[[-1, 128]],
                                            compare_op=ALU.not_equal, fill=1.0, base=0, channel_multiplier=1),
          [], [R_const])

        def stat_matmuls(bank, src_of_c, reads, nchunks=NCH):
            for c in range(nchunks):
                E("pe", lambda e, c=c: e.matmul(ps[:, bank, :], ones_bf[:, :], src_of_c(c),
                                                start=(c == 0), stop=(c == nchunks - 1)),
                  reads + [R_const], [R_ps[bank]], inc=(c == nchunks - 1))

        def rstd_from(bank_ap, bank_res, inv_n, eps, out_i):
            E("act", lambda e: e.activation(out=scr[out_i][:, :], in_=bank_ap, func=AF.Ln, bias=eps, scale=inv_n),
              [bank_res], [R_scr[out_i]])
            E("act", lambda e: e.activation(out=scr[out_i][:, :], in_=scr[out_i][:, :], func=AF.Exp, scale=-0.5),
              [R_scr[out_i]], [R_scr[out_i]])

        def rms_phase(vidx, dst_of=None):
            for t in range(NT):
                si = nxt("sq", 2)
                E("act", lambda e: e.activation(out=sq[si][:, :, :], in_=h[:, :, tsl(t)], func=AF.Square),
                  [R_h[c][t] for c in range(NCH)], [R_sq[si]])
                bank = 6 + nxt("statbank", 2)
                stat_matmuls(bank, lambda c: sq[si][:, c, :], [R_sq[si]])
                ri = nxt("rstd", 2)
                rstd_from(ps[:, bank, :], R_ps[bank], 1.0 / D, RMS_EPS, ri)
                for c in range(NCH):
                    if dst_of is None:
                        E("dve", lambda e, c=c: e.scalar_tensor_tensor(
                            out=xn[:, c, tsl(t)], in0=h[:, c, tsl(t)], scalar=vcol(vidx, c), in1=scr[ri][:, :],
                            op0=ALU.mult, op1=ALU.mult),
                          [R_h[c][t], R_scr[ri], R_vecs], [R_xn[t]])
                    else:
                        dst_of(t, c, ri)

        PRE = {"vidx": None}

        def start_rms(vidx):
            if PRE["vidx"] == vidx:
                PRE["vidx"] = None
                return
            PRE["vidx"] = None
            rms_phase(vidx)

        class RmsHook:
            def __init__(self, vidx, slots=(0, 1)):
                self.vidx, self.slots, self.prev, self.n = vidx, slots, None, 0

            def A(self, t):
                si = self.slots[self.n % len(self.slots)]
                self.n += 1
                E("act", lambda e: e.activation(out=sq[si][:, :, :], in_=h[:, :, tsl(t)], func=AF.Square),
                  [R_h[c][t] for c in range(NCH)], [R_sq[si]])
                return si

            def B(self, t, si):
                bank = 6 + nxt("statbank", 2)
                stat_matmuls(bank, lambda c: sq[si][:, c, :], [R_sq[si]])
                ri = nxt("rstd", 2)
                rstd_from(ps[:, bank, :], R_ps[bank], 1.0 / D, RMS_EPS, ri)
                for c in range(NCH):
                    E("dve", lambda e, c=c: e.scalar_tensor_tensor(
                        out=xn[:, c, tsl(t)], in0=h[:, c, tsl(t)], scalar=vcol(self.vidx, c), in1=scr[ri][:, :],
                        op0=ALU.mult, op1=ALU.mult),
                      [R_h[c][t], R_scr[ri], R_vecs], [R_xn[t]])

            def __call__(self, t):
                if len(self.slots) == 1 and self.prev is not None:
                    self.B(*self.prev)
                    self.prev = None
                si = self.A(t)
                if self.prev is not None:
                    self.B(*self.prev)
                self.prev = (t, si)

            def flush(self):
                if self.prev is not None:
                    self.B(*self.prev)
                self.prev = None
                PRE["vidx"] = self.vidx

        def ffn_phase(f, vidx, hook=None):
            start_rms(vidx)
            units = [(J, t) for J in range(NG) for t in range(NT)]
            slot_of = {}

            def load(J):
                s = nxt("ffnw", 2)
                slot_of[J] = s
                sc.dma("pool", wgu[s][:, :, :, :].rearrange("p a k c -> p (a k c)").rearrange("p (a b) -> p a b", b=2048),
                       wgu_d[f, J].rearrange("p (a b) -> p a b", b=2048), [], [R_wgu[s]])
                sc.dma("pool", wo[s][:, :, :].rearrange("p a c -> p (a c)").rearrange("p (a b) -> p a b", b=1024),
                       wo_d[f, J].rearrange("p (a b) -> p a b", b=1024), [], [R_wo[s]])

            def gu(n, jjs):
                J, t = units[n]
                s = slot_of[J]
                ab = n % 2
                for jj in jjs:
                    gb = nxt("gbank", 2)
                    ub = 2 + nxt("ubank", 2)
                    for k in range(8):
                        E("pe", lambda e, k=k: e.matmul(ps[:, gb, :], wgu[s][:, 0, k, jj * 128:(jj + 1) * 128],
                                                        xn[:, k, tsl(t)], start=(k == 0), stop=(k == 7)),
                          [R_wgu[s], R_xn[t]], [R_ps[gb]], inc=(k == 7))
                    for k in range(8):
                        E("pe", lambda e, k=k: e.matmul(ps[:, ub, :], wgu[s][:, 1, k, jj * 128:(jj + 1) * 128],
                                                        xn[:, k, tsl(t)], start=(k == 0), stop=(k == 7)),
                          [R_wgu[s], R_xn[t]], [R_ps[ub]], inc=(k == 7))
                    si = 2 + nxt("sgb", 2)
                    E("act", lambda e: e.activation(out=scr[si][:, :], in_=ps[:, gb, :], func=AF.Silu),
                      [R_ps[gb]], [R_scr[si]])
                    E("dve", lambda e: e.tensor_tensor(out=actb[ab][:, jj, :], in0=ps[:, ub, :], in1=scr[si][:, :],
                                                       op=ALU.mult),
                      [R_ps[ub], R_scr[si]], [R_act[ab]])

            def oacc(n, ms):
                J, t = units[n]
                s = slot_of[J]
                ab = n % 2
                for m in ms:
                    ob = 4 + nxt("obank", 2)
                    for jj in range(GH):
                        E("pe", lambda e, jj=jj: e.matmul(ps[:, ob, :], wo[s][:, jj, m * 128:(m + 1) * 128],
                                                          actb[ab][:, jj, :], start=(jj == 0), stop=(jj == GH - 1)),
                          [R_wo[s], R_act[ab]], [R_ps[ob]], inc=(jj == GH - 1))
                    E("dve", lambda e: e.scalar_tensor_tensor(out=h[:, m, tsl(t)], in0=ps[:, ob, :], scalar=0.5,
                                                              in1=h[:, m, tsl(t)], op0=ALU.mult, op1=ALU.add),
                      [R_ps[ob], R_h[m][t]], [R_h[m][t]])

            load(0)
            mper = NCH // GH
            for n in range(len(units) + 1):
                for jj in range(GH):
                    if n < len(units):
                        gu(n, [jj])
                    if n >= 1:
                        oacc(n - 1, range(jj * mper, (jj + 1) * mper))
                        if hook is not None and jj == GH - 1 and units[n - 1][0] == NG - 1:
                            hook(units[n - 1][1])
                if n < len(units):
                    J, t = units[n]
                    if t == 0 and J + 1 < NG:
                        load(J + 1)
            if hook is not None:
                hook.flush()

        def ple_phase(l, hook=None):
            start_rms(4 * l + 3)
            ppb = sq[1][:, :, :].rearrange("p a b -> p (a b)").rearrange("p (k t) -> p k t", k=2)
            R_ppb = R_sq[1]
            sc.dma("pool", sq[1][:, :, :].rearrange("p a b -> p (a b)").rearrange("p (a b) -> p a b", b=1024),
                   ppT[l].rearrange("p k t -> p (k t)").rearrange("p (a b) -> p a b", b=1024), [], [R_ppb])
            for m in range(NCH):
                wi = nxt("wch", NWCH)
                sc.dma("pool", wch[wi][:, 0, :, :].rearrange("p k c -> p (k c)"), plg_d[l, m], [], [R_wch[wi]])
                sc.dma("pool", wch[wi][:, 1, 0:2, :].rearrange("p k c -> p (k c)"), plp_d[l, m], [], [R_wch[wi]])
                for t in range(NT):
                    gb = nxt("gbank", 2)
                    ub = 2 + nxt("ubank", 2)
                    for k in range(8):
                        E("pe", lambda e, k=k: e.matmul(ps[:, gb, :], wch[wi][:, 0, k, :], xn[:, k, tsl(t)],
                                                        start=(k == 0), stop=(k == 7)),
                          [R_wch[wi], R_xn[t]], [R_ps[gb]], inc=(k == 7))
                    for k in range(2):
                        E("pe", lambda e, k=k: e.matmul(ps[:, ub, :], wch[wi][:, 1, k, :], ppb[:, k, tsl(t)],
                                                        start=(k == 0), stop=(k == 1)),
                          [R_wch[wi], R_ppb], [R_ps[ub]], inc=(k == 1))
                    si = 2 + nxt("sgb", 2)
                    E("act", lambda e: e.activation(out=scr[si][:, :], in_=ps[:, gb, :], func=AF.Sigmoid),
                      [R_ps[gb]], [R_scr[si]])
                    E("dve", lambda e: e.tensor_tensor(out=scr[si][:, :], in0=ps[:, ub, :], in1=scr[si][:, :],
                                                       op=ALU.mult),
                      [R_ps[ub], R_scr[si]], [R_scr[si]])
                    E("dve", lambda e: e.tensor_tensor(out=h[:, m, tsl(t)], in0=h[:, m, tsl(t)],
                                                       in1=scr[si][:, :], op=ALU.add),
                      [R_scr[si], R_h[m][t]], [R_h[m][t]])
                    if hook is not None and m == NCH - 1:
                        hook(t)
            if hook is not None:
                hook.flush()

        def conv_phase(l, hook=None):
            vb = 18 + 37 * l
            start_rms(4 * l + 1)
            with ExitStack() as st2:
                ubuf = st2.enter_context(nc.sbuf_tensor("s_" + f"ubuf{l}", [128, NCH, PAD + TT], BF16))
                yb = st2.enter_context(nc.sbuf_tensor("s_" + f"yb{l}", [128, NCH, TT], F32))
                dg0 = st2.enter_context(nc.sbuf_tensor("s_" + f"dg{l}_0", [128, CONVW, 128], BF16))
                dg1 = wgu[1][:, :, :, :].rearrange("p a k c -> p (a k c)")[:, 0:CONVW * 128].rearrange(
                    "p (j q) -> p j q", q=128)
                dg = [dg0, dg1]
                R_dg = [Res("dg0"), R_wgu[1]]
                R_ub = [Res(f"ubuf{c}") for c in range(NCH)]
                R_yb = [Res(f"yb{c}") for c in range(NCH)]
                E("pool", lambda e: e.memset(ubuf[:, :, 0:PAD], 0.0), [], R_ub)
                MU, MSQ, RS, NM = 4, 5, 6, 7

                def ag_chunk(t, c):
                    wi = nxt("wch", NWCH)
                    sc.dma("pool", wch[wi][:, :, :, :].rearrange("p a k c -> p (a k c)"), cwin_d[l, c],
                           [], [R_wch[wi]])
                    ab_ = nxt("gbank", 2)
                    gb_ = 2 + nxt("ubank", 2)
                    for k in range(8):
                        E("pe", lambda e, k=k: e.matmul(ps[:, ab_, :], wch[wi][:, 0, k, :], xn[:, k, tsl(t)],
                                                        start=(k == 0), stop=(k == 7)),
                          [R_wch[wi], R_xn[t]], [R_ps[ab_]], inc=(k == 7))
                    for k in range(8):
                        E("pe", lambda e, k=k: e.matmul(ps[:, gb_, :], wch[wi][:, 1, k, :], xn[:, k, tsl(t)],
                                                        start=(k == 0), stop=(k == 7)),
                          [R_wch[wi], R_xn[t]], [R_ps[gb_]], inc=(k == 7))
                    si = 2 + nxt("sgb", 2)
                    E("act", lambda e: e.activation(out=scr[si][:, :], in_=ps[:, gb_, :], func=AF.Sigmoid,
                                                    bias=vcol(vb + 1, c)),
                      [R_ps[gb_], R_vecs], [R_scr[si]])
                    if t > 0:
                        E("pool", lambda e: e.tensor_copy(out=ubuf[:, c, 0:PAD], in_=ubuf[:, c, TT:TT + PAD]),
                          [R_ub[c]], [R_ub[c]])
                    E("dve", lambda e: e.scalar_tensor_tensor(out=ubuf[:, c, PAD:PAD + TT], in0=ps[:, ab_, :],
                                                              scalar=vcol(vb, c), in1=scr[si][:, :],
                                                              op0=ALU.add, op1=ALU.mult),
                      [R_ps[ab_], R_scr[si], R_vecs], [R_ub[c]])

                def build_dg(c):
                    di = c % 2
                    for j in range(CONVW):
                        E("dve", lambda e, j=j: e.tensor_scalar(out=dg[di][:, j, :], in0=ident_bf[:, :],
                                                                scalar1=vcol(vb + 6 + j, c), scalar2=None,
                                                                op0=ALU.mult),
                          [R_const, R_vecs], [R_dg[di]], inc=(j == CONVW - 1))

                def taps(t):
                    build_dg(0)
                    for c in range(NCH):
                        di = c % 2
                        if c + 1 < NCH:
                            build_dg(c + 1)
                        yb_ = 4 + nxt("obank", 2)
                        for j in range(CONVW):
                            E("pe", lambda e, j=j: e.matmul(ps[:, yb_, :], dg[di][:, j, :], ubuf[:, c, j:j + TT],
                                                            start=(j == 0), stop=(j == CONVW - 1)),
                              [R_dg[di], R_ub[c]], [R_ps[yb_]], inc=(j == CONVW - 1))
                        E("act", lambda e: e.activation(out=yb[:, c, :], in_=ps[:, yb_, :], func=AF.Identity,
                                                        bias=vcol(vb + 2, c)),
                          [R_ps[yb_], R_vecs], [R_yb[c]])

                def lnstats(t):
                    si_ = nxt("sq", 2)
                    E("act", lambda e: e.activation(out=sq[si_][:, :, :], in_=yb[:, :, :], func=AF.Square),
                      R_yb, [R_sq[si_]])
                    E("dve", lambda e: e.tensor_copy(out=xn[:, :, tsl(t)], in_=yb[:, :, :]), R_yb, [R_xn[t]])
                    b1 = 6 + nxt("statbank", 2)
                    stat_matmuls(b1, lambda c: xn[:, c, tsl(t)], [R_xn[t]])
                    b2 = 6 + nxt("statbank", 2)
                    stat_matmuls(b2, lambda c: sq[si_][:, c, :], [R_sq[si_]])
                    E("dve", lambda e: e.tensor_scalar(out=scr[MU][:, :], in0=ps[:, b1, :], scalar1=1.0 / D,
                                                       scalar2=None, op0=ALU.mult), [R_ps[b1]], [R_scr[MU]])
                    E("dve", lambda e: e.tensor_tensor(out=scr[MSQ][:, :], in0=scr[MU][:, :], in1=scr[MU][:, :],
                                                       op=ALU.mult), [R_scr[MU]], [R_scr[MSQ]])
                    E("dve", lambda e: e.scalar_tensor_tensor(out=scr[MSQ][:, :], in0=ps[:, b2, :], scalar=1.0 / D,
                                                              in1=scr[MSQ][:, :], op0=ALU.mult, op1=ALU.subtract),
                      [R_ps[b2], R_scr[MSQ]], [R_scr[MSQ]])
                    rstd_from(scr[MSQ][:, :], R_scr[MSQ], 1.0, LN_EPS, RS)
                    E("dve", lambda e: e.scalar_tensor_tensor(out=scr[NM][:, :], in0=scr[MU][:, :], scalar=-1.0,
                                                              in1=scr[RS][:, :], op0=ALU.mult, op1=ALU.mult),
                      [R_scr[MU], R_scr[RS]], [R_scr[NM]])

                def lnnorm(t, c):
                    zi = nxt("z", 2)
                    E("dve", lambda e: e.tensor_tensor(out=scr[zi][:, :], in0=yb[:, c, :], in1=scr[RS][:, :],
                                                       op=ALU.mult), [R_yb[c], R_scr[RS]], [R_scr[zi]])
                    E("dve", lambda e: e.tensor_tensor(out=scr[zi][:, :], in0=scr[zi][:, :], in1=scr[NM][:, :],
                                                       op=ALU.add), [R_scr[zi], R_scr[NM]], [R_scr[zi]])
                    E("act", lambda e: e.activation(out=xn[:, c, tsl(t)], in_=scr[zi][:, :], func=AF.Silu,
                                                    bias=vcol(vb + 4, c), scale=vcol(vb + 3, c)),
                      [R_scr[zi], R_vecs], [R_xn[t]])

                def outproj(t):
                    for m in range(NCH):
                        wi = nxt("wch", NWCH)
                        sc.dma("pool", wch[wi][:, 0, :, :].rearrange("p k c -> p (k c)"), cwout_d[l, m],
                               [], [R_wch[wi]])
                        ob = 4 + nxt("obank", 2)
                        for k in range(8):
                            E("pe", lambda e, k=k: e.matmul(ps[:, ob, :], wch[wi][:, 0, k, :], xn[:, k, tsl(t)],
                                                            start=(k == 0), stop=(k == 7)),
                              [R_wch[wi], R_xn[t]], [R_ps[ob]], inc=(k == 7))
                        E("dve", lambda e: e.scalar_tensor_tensor(out=h[:, m, tsl(t)], in0=ps[:, ob, :],
                                                                  scalar=vcol(vb + 5, m), in1=h[:, m, tsl(t)],
                                                                  op0=ALU.add, op1=ALU.add),
                          [R_ps[ob], R_h[m][t], R_vecs], [R_h[m][t]])

                for c in range(NCH):
                    ag_chunk(0, c)
                for t in range(NT):
                    taps(t)
                    lnstats(t)
                    for c in range(NCH):
                        lnnorm(t, c)
                        if t + 1 < NT:
                            ag_chunk(t + 1, c)
                    outproj(t)
                    if hook is not None:
                        hook(t)
                if hook is not None:
                    hook.flush()
                sc.barrier()

        def kv_phase():
            start_rms(16)
            with ExitStack() as st2:
                kst = [st2.enter_context(nc.sbuf_tensor("s_" + f"kst{i}", [128, S], BF16)) for i in range(2)]
                vst = [st2.enter_context(nc.sbuf_tensor("s_" + f"vst{i}", [128, D], BF16)) for i in range(2)]
                wvb = [st2.enter_context(nc.sbuf_tensor("s_" + f"wvb{i}", [128, 8, 512], BF16)) for i in range(2)]
                R_kst = [Res("kst0"), Res("kst1")]
                R_vst = [Res("vst0"), Res("vst1")]
                R_wvb = [Res("wvb0"), Res("wvb1")]
                for hh in range(NHEAD):
                    wi = nxt("wch", NWCH)
                    sc.dma("pool", wch[wi][:, 0, :, :].rearrange("p k c -> p (k c)"), wk_d[hh], [], [R_wch[wi]])
                    ks = nxt("kst", 2)
                    for t in range(NT):
                        bk = nxt("gbank", 2)
                        for k in range(8):
                            E("pe", lambda e, k=k: e.matmul(ps[:, bk, :], wch[wi][:, 0, k, :], xn[:, k, tsl(t)],
                                                            start=(k == 0), stop=(k == 7)),
                              [R_wch[wi], R_xn[t]], [R_ps[bk]], inc=(k == 7))
                        E("act", lambda e: e.activation(out=kst[ks][:, tsl(t)], in_=ps[:, bk, :], func=AF.Identity),
                          [R_ps[bk]], [R_kst[ks]])
                    sc.dma("sp", kT_dram[hh], kst[ks][:, :], [R_kst[ks]], [R_kv], semres=R_kstout[ks])
                for half in range(2):
                    sc.dma("pool", wvb[half][:, :, :].rearrange("p k c -> p (k c)").rearrange("p (a b) -> p a b", b=2048),
                           wv_d[half].rearrange("p (a b) -> p a b", b=2048), [], [R_wvb[half]])
                for kt in range(16):
                    vs = nxt("vst", 2)
                    for half in range(2):
                        bk = 2 + nxt("ubank", 2)
                        for k in range(8):
                            E("pe", lambda e, k=k: e.matmul(ps[:, bk, :], xn[:, k, kt * 128:(kt + 1) * 128],
                                                            wvb[half][:, k, :], start=(k == 0), stop=(k == 7)),
                              [R_wvb[half], R_xn[kt // 4]], [R_ps[bk]], inc=(k == 7))
                        E("dve", lambda e: e.tensor_copy(out=vst[vs][:, half * 512:(half + 1) * 512], in_=ps[:, bk, :]),
                          [R_ps[bk]], [R_vst[vs]])
                    sc.dma("sp", v_dram[:, :, kt, :].rearrange("h p e -> p h e"),
                           vst[vs][:, :].rearrange("p (h e) -> p h e", e=128), [R_vst[vs]], [R_kv],
                           semres=R_vstout[vs])
                sc.barrier()

        R_kv = Res("kvdram")
        R_kstout = [Res("kstout0"), Res("kstout1")]
        R_vstout = [Res("vstout0"), Res("vstout1")]

        def attn_phase(layer, j_, A, hook=None):
            lam_init = lambda_init_of(layer)
            start_rms(4 * layer + 1)
            kTh, vh, qT, pT, wob = A["kTh"], A["vh"], A["qT"], A["pT"], A["wob"]
            R_kTh, R_vh, R_qT, R_pT, R_wob = A["R_kTh"], A["R_vh"], A["R_qT"], A["R_pT"], A["R_wob"]
            small, R_small = A["small"], A["R_small"]
            lvb = A["lvb"]
            base = j_ * 256
            for i2 in range(2):
                E("dve", lambda e, i2=i2: e.tensor_tensor(out=A["ltmp"][:, :], in0=lvb[:, base + i2 * 128: base + i2 * 128 + 64],
                                                          in1=lvb[:, base + i2 * 128 + 64: base + i2 * 128 + 128],
                                                          op=ALU.mult), [A["R_lvb"]], [A["R_ltmp"]])
                E("dve", lambda e, i2=i2: e.tensor_reduce(out=small[:, i2:i2 + 1], in_=A["ltmp"][:, :],
                                                          axis=mybir.AxisListType.X, op=ALU.add),
                  [A["R_ltmp"]], [R_small])
            E("act", lambda e: e.activation(out=small[:, 0:2], in_=small[:, 0:2], func=AF.Exp), [R_small], [R_small])
            E("dve", lambda e: e.scalar_tensor_tensor(out=small[:, 2:3], in0=small[:, 1:2], scalar=-lam_init,
                                                      in1=small[:, 0:1], op0=ALU.add, op1=ALU.subtract),
              [R_small], [R_small])
            E("dve", lambda e: e.tensor_scalar(out=small[:, 3:4], in0=A["sublnb"][:, j_:j_ + 1],
                                               scalar1=1.0 - lam_init, scalar2=None, op0=ALU.mult),
              [A["R_sublnb"]], [R_small])
            biasd, biasn, b31 = A["biasd"], A["biasn"], A["b31"]
            R_bias = A["R_bias"]
            for hh in range(NHEAD):
                hs = 0
                ws = hh % 2
                wi = nxt("wch", NWCH)
                sc.dma("pool", wch[wi][:, 0, :, :].rearrange("p k c -> p (k c)"), wq_d[j_, hh], [], [R_wch[wi]])
                sc.dma("pool", wob[ws][:, :], wao_d[j_, hh], [], [R_wob[ws]])
                sc.dma("sp", kTh[hs][:, :], kT_dram[hh], [R_kv], [R_kTh[hs]])
                sc.dma("sp", vh[hs][:, :, :], v_dram[hh], [R_kv], [R_vh[hs]])
                for t in range(NT):
                    bk = 6 + nxt("statbank", 2)
                    for k in range(8):
                        E("pe", lambda e, k=k: e.matmul(ps[:, bk, :], wch[wi][:, 0, k, :], xn[:, k, tsl(t)],
                                                        start=(k == 0), stop=(k == 7)),
                          [R_wch[wi], R_xn[t]], [R_ps[bk]], inc=(k == 7))
                    E("dve", lambda e: e.tensor_copy(out=qT[hs][:, tsl(t)], in_=ps[:, bk, :]),
                      [R_ps[bk]], [R_qT[hs]])
                for qi in range(NT):
                    nk = 4 * (qi + 1)
                    NUM = [2, 3]
                    DEN = [4, 5]
                    for kj in range(nk):
                        r = kj - 4 * qi
                        c0 = max(r, 0)
                        qs = slice(qi * TT + c0 * 128, (qi + 1) * TT)
                        cs = slice(c0 * 128, TT)
                        pi = nxt("pT", 2)
                        for mp in range(2):
                            sbk = mp
                            rows = slice(mp * 64, (mp + 1) * 64)
                            E("pe", lambda e: e.matmul(ps[:, sbk, cs], kTh[hs][rows, kj * 128:(kj + 1) * 128],
                                                       qT[hs][rows, qs], start=True, stop=True),
                              [R_kTh[hs], R_qT[hs]], [R_ps[sbk]])
                            pr = R_pT[pi][mp]
                            pt = pT[pi][mp]
                            if r <= -2:
                                E("act", lambda e: e.activation(out=pt[:, cs], in_=ps[:, sbk, cs], func=AF.Exp,
                                                                bias=b31[:, hh:hh + 1], scale=0.125),
                                  [R_ps[sbk], R_bias], [pr])
                            else:
                                for sbl in range(c0, 4):
                                    ss = slice(sbl * 128, (sbl + 1) * 128)
                                    dd = sbl - r
                                    if dd >= 2:
                                        E("act", lambda e, ss=ss: e.activation(out=pt[:, ss], in_=ps[:, sbk, ss],
                                                                               func=AF.Exp, bias=b31[:, hh:hh + 1],
                                                                               scale=0.125),
                                          [R_ps[sbk], R_bias], [pr])
                                    else:
                                        bt = biasd if dd == 0 else biasn
                                        ti = 2 + nxt("sgb", 2)
                                        E("dve", lambda e, ss=ss, bt=bt: e.scalar_tensor_tensor(
                                            out=scr[ti][:, 0:128], in0=ps[:, sbk, ss], scalar=0.125,
                                            in1=bt[:, hh, :], op0=ALU.mult, op1=ALU.add),
                                          [R_ps[sbk], R_bias], [R_scr[ti]])
                                        E("act", lambda e, ss=ss: e.activation(out=pt[:, ss], in_=scr[ti][:, 0:128],
                                                                               func=AF.Exp),
                                          [R_scr[ti]], [pr])
                        for mp in range(2):
                            pr = R_pT[pi][mp]
                            pt = pT[pi][mp]
                            E("pe", lambda e: e.matmul(ps[:, NUM[mp], cs], vh[hs][:, kj, :], pt[:, cs],
                                                       start=(kj == 0), stop=(kj == nk - 1)),
                              [R_vh[hs], pr], [R_ps[NUM[mp]]], inc=(kj == nk - 1))
                            E("pe", lambda e: e.matmul(ps[:, DEN[mp], cs], ones_bf[:, :], pt[:, cs],
                                                       start=(kj == 0), stop=(kj == nk - 1)),
                              [R_const, pr], [R_ps[DEN[mp]]], inc=True)
                    R1, R2, O1, T2 = 4, 5, 6, 7
                    for mp, ri in ((0, R1), (1, R2)):
                        E("act", lambda e, mp=mp, ri=ri: e.activation(out=scr[ri][:, :], in_=ps[:, DEN[mp], :],
                                                                      func=AF.Ln), [R_ps[DEN[mp]]], [R_scr[ri]])
                        E("act", lambda e, ri=ri: e.activation(out=scr[ri][:, :], in_=scr[ri][:, :], func=AF.Exp,
                                                               scale=-1.0), [R_scr[ri]], [R_scr[ri]])
                    E("dve", lambda e: e.tensor_tensor(out=scr[O1][:, :], in0=ps[:, NUM[0], :], in1=scr[R1][:, :],
                                                       op=ALU.mult), [R_ps[NUM[0]], R_scr[R1]], [R_scr[O1]])
                    E("dve", lambda e: e.tensor_tensor(out=scr[T2][:, :], in0=ps[:, NUM[1], :], in1=scr[R2][:, :],
                                                       op=ALU.mult), [R_ps[NUM[1]], R_scr[R2]], [R_scr[T2]])
                    E("dve", lambda e: e.scalar_tensor_tensor(out=scr[O1][:, :], in0=scr[T2][:, :],
                                                              scalar=small[:, 2:3], in1=scr[O1][:, :],
                                                              op0=ALU.mult, op1=ALU.add),
                      [R_scr[T2], R_scr[O1], R_small], [R_scr[O1]])
                    si = nxt("sq", 2)
                    E("act", lambda e: e.activation(out=sq[si][:, 0, :], in_=scr[O1][:, :], func=AF.Square),
                      [R_scr[O1]], [R_sq[si]])
                    bk = 6 + nxt("statbank", 2)
                    stat_matmuls(bk, lambda c: sq[si][:, 0, :], [R_sq[si]], nchunks=1)
                    rstd_from(ps[:, bk, :], R_ps[bk], 1.0 / 128, RMS_EPS, T2)
                    E("dve", lambda e: e.scalar_tensor_tensor(out=sq[si][:, 1, :], in0=scr[O1][:, :],
                                                              scalar=small[:, 3:4], in1=scr[T2][:, :],
                                                              op0=ALU.mult, op1=ALU.mult),
                      [R_scr[O1], R_scr[T2], R_small, R_sq[si]], [R_sq[si]])
                    for m in range(NCH):
                        ob = 6 + nxt("statbank", 2)
                        E("pe", lambda e: e.matmul(ps[:, ob, :], wob[ws][:, m * 128:(m + 1) * 128], sq[si][:, 1, :],
                                                   start=True, stop=True), [R_wob[ws], R_sq[si]], [R_ps[ob]])
                        E("dve", lambda e: e.tensor_tensor(out=h[:, m, tsl(qi)], in0=ps[:, ob, :],
                                                           in1=h[:, m, tsl(qi)], op=ALU.add),
                          [R_ps[ob], R_h[m][qi]], [R_h[m][qi]])
                    if hook is not None and hh == NHEAD - 1:
                        hook(qi)
            if hook is not None:
                hook.flush()

        def attn_setup(st2):
            A = {}
            A["kTh"] = [st2.enter_context(nc.sbuf_tensor("s_" + f"kTh{i}", [128, S], BF16)) for i in range(1)]
            A["vh"] = [st2.enter_context(nc.sbuf_tensor("s_" + f"vh{i}", [128, 16, 128], BF16)) for i in range(1)]
            A["qT"] = [st2.enter_context(nc.sbuf_tensor("s_" + f"qT{i}", [128, S], BF16)) for i in range(1)]
            A["pT"] = [[st2.enter_context(nc.sbuf_tensor("s_" + f"pT{i}_{m}", [128, TT], BF16)) for m in range(2)]
                       for i in range(2)]
            A["wob"] = [st2.enter_context(nc.sbuf_tensor("s_" + f"wob{i}", [128, D], BF16)) for i in range(2)]
            A["small"] = st2.enter_context(nc.sbuf_tensor("s_small", [128, 8], F32))
            A["ltmp"] = st2.enter_context(nc.sbuf_tensor("s_ltmp", [128, 64], F32))
            A["lvb"] = st2.enter_context(nc.sbuf_tensor("s_lvb", [128, 512], F32))
            A["sublnb"] = st2.enter_context(nc.sbuf_tensor("s_sublnb", [128, 2], F32))
            A["biasd"] = st2.enter_context(nc.sbuf_tensor("s_biasd", [128, NHEAD, 128], F32))
            A["biasn"] = st2.enter_context(nc.sbuf_tensor("s_biasn", [128, NHEAD, 128], F32))
            A["b31"] = st2.enter_context(nc.sbuf_tensor("s_b31", [128, NHEAD], F32))
            gsb = scr[0][:, 0:384]
            oneh = scr[1][0:32, 0:384]
            maskr = scr[2][0:1, 0:384]
            ones1 = scr[3][0:1, 0:128]
            ones32 = scr[4][0:32, 0:128]
            lhs = scr[5][0:32, 0:128]
            relb = scr[6][0:32, 0:8]
            for k in ("kTh", "vh", "qT", "wob"):
                A["R_" + k] = [Res(k + "0"), Res(k + "1")]
            A["R_pT"] = [[Res(f"pT{i}_{m}") for m in range(2)] for i in range(2)]
            for k in ("small", "ltmp", "lvb", "sublnb", "bias"):
                A["R_" + k] = Res(k)
            R_gsb, R_oneh, R_maskr, R_ones1, R_ones32, R_lhs, R_relb = (R_scr[i] for i in range(7))
            R_g = Res("gdram")
            sc.dma("sp", A["lvb"][:, :], lv_d[:, :], [], [A["R_lvb"]])
            sc.dma("sp", A["sublnb"][:, :], subln_d[:, :], [], [A["R_sublnb"]])
            sc.dma("sp", relb, relb_d[:, :], [], [R_relb], semres=Res("relb"))
            sc.dma("sp", oneh, onehot_d[:, :], [], [R_oneh], semres=Res("oneh"))
            sc.dma("sp", maskr, maskrow_d[:, :], [], [R_maskr], semres=Res("maskr"))
            E("pool", lambda e: e.memset(ones1, 1.0), [], [R_ones1])
            E("pool", lambda e: e.memset(ones32, 1.0), [], [R_ones32])
            for hh in range(NHEAD):
                E("dve", lambda e: e.tensor_scalar(out=lhs, in0=ones32, scalar1=relb[:, hh:hh + 1],
                                                   scalar2=None, op0=ALU.mult), [R_ones32, R_relb], [R_lhs])
                E("pe", lambda e: e.matmul(ps[:, 0, 0:384], lhs, oneh, start=True, stop=False),
                  [R_lhs, R_oneh], [R_ps[0]], inc=False)
                E("pe", lambda e: e.matmul(ps[:, 0, 0:384], ones1, maskr, start=False, stop=True),
                  [R_ones1, R_maskr], [R_ps[0]])
                E("dve", lambda e: e.tensor_copy(out=gsb, in_=ps[:, 0, 0:384]), [R_ps[0]], [R_gsb])
                E("dve", lambda e: e.tensor_copy(out=A["b31"][:, hh:hh + 1], in_=gsb[:, 383:384]),
                  [R_gsb], [A["R_bias"]])
                sc.dma("sp", g_dram[hh], gsb, [R_gsb], [R_g], semres=Res(f"gout{hh}"))
            for hh in range(NHEAD):
                for dd, bt in ((0, A["biasd"]), (128, A["biasn"])):
                    src = bass.AP(tensor=g_dram.tensor, offset=g_dram[hh].offset + 128 + dd,
                                  ap=[[383, 128], [1, 128]])
                    sc.dma("sp", bt[:, hh, :], src, [R_g], [A["R_bias"]], semres=Res(f"bt{hh}_{dd}"))
            return A

        def dump_h():
            for t in range(NT):
                sc.out_tks.append(sc.dma("sp", outT[:, :, tsl(t)], h[:, :, tsl(t)],
                                         [R_h[c][t] for c in range(NCH)], [R_out], semres=Res(f"dump{t}")))

        R_out = Res("out")
        done = [False]

        nph = [0]

        def check(name):
            if debug:
                src = h[:, :, :].rearrange("p c (a b) -> p c a b", b=128)[:, :, :, 0:32]
                sc.out_tks.append(sc.dma("sp", dbg[nph[0]], src, [R_h[c][t] for c in range(NCH) for t in range(NT)],
                                         [R_out], semres=Res(f"dbg{nph[0]}")))
                nph[0] += 1
            if stop_after == name and not done[0]:
                dump_h()
                done[0] = True
            return done[0]

        def forward():
            for l in range(2):
                ffn_phase(2 * l, 4 * l + 0, RmsHook(4 * l + 1))
                if check(f"ffn1_{l}"):
                    return
                conv_phase(l, RmsHook(4 * l + 2))
                if check(f"mix_{l}"):
                    return
                ffn_phase(2 * l + 1, 4 * l + 2, RmsHook(4 * l + 3))
                if check(f"ffn2_{l}"):
                    return
                ple_phase(l, RmsHook(4 * (l + 1) if l == 0 else 16, slots=(0,)))
                if check(f"ple_{l}"):
                    return
            kv_phase()
            with ExitStack() as st2:
                A = attn_setup(st2)
                for l in range(2, 4):
                    ffn_phase(2 * l, 4 * l + 0, RmsHook(4 * l + 1))
                    if check(f"ffn1_{l}"):
                        return
                    attn_phase(l, l - 2, A, RmsHook(4 * l + 2))
                    if check(f"mix_{l}"):
                        return
                    ffn_phase(2 * l + 1, 4 * l + 2, RmsHook(4 * l + 3))
                    if check(f"ffn2_{l}"):
                        return
                    ple_phase(l, RmsHook(4 * (l + 1), slots=(0,)) if l == 2 else None)
                    if check(f"ple_{l}"):
                        return
                sc.barrier()
            with ExitStack() as st2:
                ob = [st2.enter_context(nc.sbuf_tensor("s_" + f"outb{i}", [128, NCH, TT], F32)) for i in range(2)]
                R_ob = [Res("outb0"), Res("outb1")]
                cur = {}

                def dst(t, c, ri):
                    if c == 0:
                        cur["i"] = nxt("outb", 2)
                    oi = cur["i"]
                    E("dve", lambda e: e.scalar_tensor_tensor(out=ob[oi][:, c, :], in0=h[:, c, tsl(t)],
                                                              scalar=vcol(17, c), in1=scr[ri][:, :],
                                                              op0=ALU.mult, op1=ALU.mult),
                      [R_h[c][t], R_scr[ri], R_vecs], [R_ob[oi]])
                    if c == NCH - 1:
                        sc.out_tks.append(sc.dma("sp", outT[:, :, tsl(t)], ob[oi][:, :, :], [R_ob[oi]], [R_out],
                                                 semres=Res(f"fin{t}")))

                rms_phase(17, dst_of=dst)
                sc.barrier()

        forward()
        sc._wait_for("sp", sc.out_tks)
        sc.barrier()
        print("sched counts:", sc.cnt, "nsem", sc.nsem)
    return nc


def _rel_bucket_np(n):
    n = np.maximum(n, 0)
    max_exact = 16
    nf = np.maximum(n, 1).astype(np.float32)
    large = max_exact + (np.log(nf / max_exact) / math.log(128 / max_exact) * (32 - max_exact)).astype(np.int32)
    large = np.minimum(large, 31)
    return np.where(n < max_exact, n, large)


def _bucket_table():
    return _rel_bucket_np(np.arange(0, 256))


def prep_shared(inp):
    f = np.float32
    A = {k: np.asarray(v, dtype=f) for k, v in inp.items()}
    sh = {}
    vl = []
    for l in range(4):
        vl += [A["ffn1_norm"][l], A["mix_norm"][l], A["ffn2_norm"][l], A["ple_norm"][l]]
    vl += [A["kv_norm"], A["final_norm"]]
    for l in range(2):
        vl += [A["conv_b_in"][l][:D], A["conv_b_in"][l][D:], A["conv_b_dw"][l], A["conv_ln_g"][l],
               A["conv_ln_b"][l], A["conv_b_out"][l]]
        vl += [A["conv_w_dw"][l][j] for j in range(CONVW)]
    V = np.stack(vl, 0)
    assert V.shape[0] == NV
    sh["vecs"] = np.ascontiguousarray(V.reshape(NV, 8, 128).transpose(2, 0, 1).reshape(128, NV * 8))
    wgu, wo = [], []
    for l in range(4):
        for nm in ("ffn1", "ffn2"):
            wi = A[nm + "_w_in"][l]
            wgu.append(wi.reshape(8, 128, 2, NG, 128 * GH).transpose(3, 1, 2, 0, 4).reshape(NG, 128, -1))
            wt = A[nm + "_w_out"][l]
            wo.append(wt.reshape(NG, GH, 128, D).transpose(0, 2, 1, 3).reshape(NG, 128, -1))
    sh["wgu"] = np.ascontiguousarray(np.stack(wgu, 0))
    sh["wo"] = np.ascontiguousarray(np.stack(wo, 0))

    def sq_tiles(W):
        return W.reshape(8, 128, 8, 128).transpose(2, 1, 0, 3).reshape(8, 128, 1024)

    def head_tiles(W):
        return W.reshape(8, 128, 2, 8, 64).transpose(3, 1, 0, 2, 4).reshape(8, 128, 1024)

    sh["plg"] = np.ascontiguousarray(np.stack([sq_tiles(A["ple_w_gate"][l]) for l in range(4)], 0))
    sh["plp"] = np.ascontiguousarray(np.stack(
        [A["ple_w_proj"][l].reshape(2, 128, 8, 128).transpose(2, 1, 0, 3).reshape(8, 128, 256) for l in range(4)], 0))
    sh["cwin"] = np.ascontiguousarray(np.stack(
        [A["conv_w_in"][l].reshape(8, 128, 2, 8, 128).transpose(3, 1, 2, 0, 4).reshape(8, 128, -1)
         for l in range(2)], 0))
    sh["cwout"] = np.ascontiguousarray(np.stack([sq_tiles(A["conv_w_out"][l]) for l in range(2)], 0))
    sh["wk"] = np.ascontiguousarray(head_tiles(A["w_kv"][:, :D]))
    wv = A["w_kv"][:, D:]
    sh["wv"] = np.ascontiguousarray(wv.reshape(8, 128, 2, 512).transpose(2, 1, 0, 3).reshape(2, 128, -1))
    sh["wq"] = np.ascontiguousarray(np.stack([head_tiles(A["attn_w_q"][l]) for l in range(2)], 0))
    sh["wao"] = np.ascontiguousarray(np.stack([A["attn_w_o"][l].reshape(8, 128, D) for l in range(2)], 0))
    lv = np.stack([np.concatenate([A["attn_lq1"][l], A["attn_lk1"][l], A["attn_lq2"][l], A["attn_lk2"][l]])
                   for l in range(2)], 0).reshape(1, 512)
    sh["lv"] = np.ascontiguousarray(np.broadcast_to(lv, (128, 512)))
    sh["subln"] = np.ascontiguousarray(A["attn_subln"].T)
    sh["relb"] = np.ascontiguousarray(A["rel_bias"])
    bt = _bucket_table()
    oh = np.zeros((32, 384), f)
    mr = np.zeros((1, 384), f)
    for npr in range(384):
        n = npr - 128
        if n < 0:
            mr[0, npr] = -1e30
        else:
            oh[bt[min(n, 255)], npr] = 1.0
    sh["onehot"] = oh
    sh["maskrow"] = mr
    return sh


def prep_core(inp, b):
    x = np.asarray(inp["x"], dtype=np.float32)[b]
    p = np.asarray(inp["p"], dtype=np.float32)[:, b]
    xT = np.ascontiguousarray(x.T.reshape(8, 128, S).transpose(1, 0, 2))
    ppT = np.ascontiguousarray(p.transpose(0, 2, 1).reshape(4, 2, 128, S).transpose(0, 2, 1, 3))
    return {"xT": xT, "ppT": ppT}


_PROG = {}


def run(inputs, stop_after=None, trace=False, debug=False):
    key = (stop_after, debug)
    if key not in _PROG:
        _PROG[key] = build_program(stop_after, debug)
    nc = _PROG[key]
    sh = prep_shared(inputs)
    in_maps = []
    for b in range(8):
        m = dict(sh)
        m.update(prep_core(inputs, b))
        in_maps.append(m)
    res = run_bass_kernel_spmd(nc, in_maps, core_ids=list(range(8)), **({"trace": True} if trace else {}))
    outs = []
    for b in range(8):
        o = np.asarray(res.results[b]["outT"])
        outs.append(o.transpose(1, 0, 2).reshape(D, S).T)
    return np.ascontiguousarray(np.stack(outs, 0).astype(np.float32)), res


def kernel(**inputs):
    out, _ = run(inputs)
    return out
```

```python
import math
from contextlib import ExitStack

import numpy as np
import concourse.bass as bass
import concourse.mybir as mybir
from concourse.bass_utils import run_bass_kernel_spmd
from concourse.alu_op_type import AluOpType as ALU

F32 = mybir.dt.float32
BF16 = mybir.dt.bfloat16
AF = mybir.ActivationFunctionType

D = 1024
S = 2048
NCH = 8
TT = 512
NT = S // TT
DFF = 4096
GH = 2
NG = DFF // (128 * GH)
CONVW = 31
PAD = CONVW - 1
NHEAD = 8
RMS_EPS = 1e-6
LN_EPS = 1e-5
NV = 18 + 37 * 2
SAME_ENG_SYNC = True
NWCH = 3
DEN_ENG1 = "dve"


class Tk:
    __slots__ = ("sem", "val", "eng")

    def __init__(self, sem, val, eng=None):
        self.sem, self.val, self.eng = sem, val, eng


class Res:
    def __init__(self, name):
        self.name = name
        self.w = None
        self.r = []
        self.sem = None
        self.cnt = 0


class Sched:
    def __init__(self, nc, stack):
        self.nc = nc
        self.stack = stack
        self.eng = {"pe": nc.tensor, "act": nc.scalar, "dve": nc.vector, "pool": nc.gpsimd, "sp": nc.sync}
        self.sem = {k: stack.enter_context(nc.semaphore("sem_" + k)) for k in self.eng}
        self.cnt = {k: 0 for k in self.eng}
        self.pending = {k: [] for k in self.eng}
        self.waited = {}
        self.nsem = len(self.eng)
        self.dma_res = []
        self.out_tks = []

    def _wait_for(self, en, tickets):
        best = {}
        for tk in tickets:
            if tk is None:
                continue
            if tk.val is None:
                assert tk.eng == en, "pending ticket consumed cross-engine"
                continue
            if tk.eng == en and (en == "pe" or not SAME_ENG_SYNC):
                continue
            key = id(tk.sem)
            if key not in best or best[key].val < tk.val:
                best[key] = tk
        for key, tk in best.items():
            if self.waited.get((en, key), 0) >= tk.val:
                continue
            self.eng[en].wait_ge(tk.sem, tk.val)
            self.waited[(en, key)] = tk.val

    def _deps(self, reads, writes):
        deps = []
        for r in reads:
            deps.append(r.w)
        for w in writes:
            deps.append(w.w)
            deps.extend(w.r)
        return deps

    def emit(self, en, fn, reads=(), writes=(), inc=True):
        self._wait_for(en, self._deps(reads, writes))
        ins = fn(self.eng[en])
        if inc:
            ins.then_inc(self.sem[en], 1)
            self.cnt[en] += 1
            tk = Tk(self.sem[en], self.cnt[en], en)
            for p in self.pending[en]:
                p.val = self.cnt[en]
            self.pending[en] = []
        else:
            tk = Tk(self.sem[en], None, en)
            self.pending[en].append(tk)
        for w in writes:
            w.w = tk
            w.r = []
        for r in reads:
            r.r.append(tk)
        return tk

    def dma(self, q, out, in_, reads, writes, semres=None):
        semres = semres or writes[0]
        if semres.sem is None:
            semres.sem = self.stack.enter_context(self.nc.semaphore("d_" + semres.name))
            self.nsem += 1
            self.dma_res.append(semres)
        self._wait_for(q, self._deps(reads, writes))
        self.eng[q].dma_start(out=out, in_=in_).then_inc(semres.sem, 16)
        semres.cnt += 16
        tk = Tk(semres.sem, semres.cnt, "dma")
        for w in writes:
            w.w = tk
            w.r = []
        for r in reads:
            r.r.append(tk)
        return tk

    def barrier(self):
        tks = [Tk(self.sem[k], self.cnt[k], k) for k in self.eng if self.cnt[k] > 0]
        dtk = [Tk(r.sem, r.cnt, "dma") for r in self.dma_res if r.cnt > 0]
        for en in self.eng:
            self._wait_for(en, [t for t in tks if t.eng != en] + dtk)


def lambda_init_of(layer):
    return 0.8 - 0.6 * math.exp(-0.3 * layer)


def build_program(stop_after=None, debug=False):
    nc = bass.Bass("TRN2", target_bir_lowering=False)

    def din(name, shape, dt=F32):
        return nc.dram_tensor(name, list(shape), dt, kind="ExternalInput").ap()

    xT = din("xT", [128, NCH, S])
    ppT = din("ppT", [4, 128, 2, S])
    vecs_d = din("vecs", [128, NV * 8])
    wgu_d = din("wgu", [8, NG, 128, 2 * 8 * 128 * GH])
    wo_d = din("wo", [8, NG, 128, GH * D])
    plg_d = din("plg", [4, 8, 128, 8 * 128])
    plp_d = din("plp", [4, 8, 128, 2 * 128])
    cwin_d = din("cwin", [2, 8, 128, 2 * 8 * 128])
    cwout_d = din("cwout", [2, 8, 128, 8 * 128])
    wk_d = din("wk", [8, 128, 8 * 128])
    wv_d = din("wv", [2, 128, 8 * 512])
    wq_d = din("wq", [2, 8, 128, 8 * 128])
    wao_d = din("wao", [2, 8, 128, D])
    lv_d = din("lv", [128, 2 * 4 * 64])
    subln_d = din("subln", [128, 2])
    relb_d = din("relb", [32, 8])
    onehot_d = din("onehot", [32, 384])
    maskrow_d = din("maskrow", [1, 384])
    outT = nc.dram_tensor("outT", [128, NCH, S], F32, kind="ExternalOutput").ap()
    dbg = nc.dram_tensor("dbg", [16, 128, NCH, 16, 32], F32, kind="ExternalOutput").ap() if debug else None
    kT_dram = nc.dram_tensor("kT_scr", [NHEAD, 128, S], BF16, kind="Internal").ap()
    v_dram = nc.dram_tensor("v_scr", [NHEAD, 128, 16, 128], BF16, kind="Internal").ap()
    g_dram = nc.dram_tensor("g_scr", [NHEAD, 128, 384], F32, kind="Internal").ap()

    with ExitStack() as stack:
        sc = Sched(nc, stack)
        E = sc.emit

        def sb(name, shape, dt):
            return stack.enter_context(nc.sbuf_tensor("s_" + name, list(shape), dt))

        h = sb("h", [128, NCH, S], F32)
        xn = sb("xn", [128, NCH, S], BF16)
        vecs = sb("vecs", [128, NV * 8], F32)
        ones_bf = sb("ones_bf", [128, 128], BF16)
        ident_bf = sb("ident_bf", [128, 128], BF16)
        wgu = [sb(f"wgu{i}", [128, 2, 8, 128 * GH], BF16) for i in range(2)]
        wo = [sb(f"wo{i}", [128, GH, D], BF16) for i in range(2)]
        actb = [sb(f"actb{i}", [128, GH, TT], BF16) for i in range(2)]
        sq = [sb(f"sq{i}", [128, NCH, TT], BF16) for i in range(2)]
        NSCR = 8
        scr = [sb(f"scr{i}", [128, TT], F32) for i in range(NSCR)]
        wch = [sb(f"wch{i}", [128, 2, 8, 128], BF16) for i in range(NWCH)]
        ps = stack.enter_context(nc.psum_tensor("ps", [128, 8, TT], F32))

        R_h = [[Res(f"h{c}_{t}") for t in range(NT)] for c in range(NCH)]
        R_xn = [Res(f"xn{t}") for t in range(NT)]
        R_vecs = Res("vecs")
        R_const = Res("const")
        R_wgu = [Res(f"wgu{i}") for i in range(2)]
        R_wo = [Res(f"wo{i}") for i in range(2)]
        R_act = [Res(f"act{i}") for i in range(2)]
        R_sq = [Res(f"sq{i}") for i in range(2)]
        R_scr = [Res(f"scr{i}") for i in range(NSCR)]
        R_wch = [Res(f"wch{i}") for i in range(NWCH)]
        R_ps = [Res(f"ps{i}") for i in range(8)]
        rot = {}

        def nxt(key, n):
            rot[key] = (rot.get(key, -1) + 1) % n
            return rot[key]

        def vcol(v, c):
            return vecs[:, v * 8 + c: v * 8 + c + 1]

        def tsl(t):
            return slice(t * TT, (t + 1) * TT)

        sc.dma("sp", vecs[:, :], vecs_d[:, :], [], [R_vecs])
        for t in range(NT):
            sc.dma("sp", h[:, :, tsl(t)], xT[:, :, tsl(t)], [], [R_h[c][t] for c in range(NCH)],
                   semres=Res(f"xload{t}"))
        E("pool", lambda e: e.memset(ones_bf[:, :], 1.0), [], [R_const])
        E("pool", lambda e: e.memset(ident_bf[:, :], 0.0), [], [R_const])
        E("pool", lambda e: e.affine_select(out=ident_bf[:, :], in_=ident_bf[:, :], pattern=[[-1, 128]],
                                            compare_op=ALU.not_equal, fill=1.0, base=0, channel_multiplier=1),
          [], [R_const])

        def stat_matmuls(bank, src_of_c, reads, nchunks=NCH):
            for c in range(nchunks):
                E("pe", lambda e, c=c: e.matmul(ps[:, bank, :], ones_bf[:, :], src_of_c(c),
                                                start=(c == 0), stop=(c == nchunks - 1)),
                  reads + [R_const], [R_ps[bank]], inc=(c == nchunks - 1))

        def rstd_from(bank_ap, bank_res, inv_n, eps, out_i):
            E("act", lambda e: e.activation(out=scr[out_i][:, :], in_=bank_ap, func=AF.Ln, bias=eps, scale=inv_n),
              [bank_res], [R_scr[out_i]])
            E("act", lambda e: e.activation(out=scr[out_i][:, :], in_=scr[out_i][:, :], func=AF.Exp, scale=-0.5),
              [R_scr[out_i]], [R_scr[out_i]])

        def rms_phase(vidx, dst_of=None):
            for t in range(NT):
                si = nxt("sq", 2)
                E("act", lambda e: e.activation(out=sq[si][:, :, :], in_=h[:, :, tsl(t)], func=AF.Square),
                  [R_h[c][t] for c in range(NCH)], [R_sq[si]])
                bank = 6 + nxt("statbank", 2)
                stat_matmuls(bank, lambda c: sq[si][:, c, :], [R_sq[si]])
                ri = nxt("rstd", 2)
                rstd_from(ps[:, bank, :], R_ps[bank], 1.0 / D, RMS_EPS, ri)
                for c in range(NCH):
                    if dst_of is None:
                        E("dve", lambda e, c=c: e.scalar_tensor_tensor(
                            out=xn[:, c, tsl(t)], in0=h[:, c, tsl(t)], scalar=vcol(vidx, c), in1=scr[ri][:, :],
                            op0=ALU.mult, op1=ALU.mult),
                          [R_h[c][t], R_scr[ri], R_vecs], [R_xn[t]])
                    else:
                        dst_of(t, c, ri)

        PRE = {"vidx": None}

        def start_rms(vidx):
            if PRE["vidx"] == vidx:
                PRE["vidx"] = None
                return
            PRE["vidx"] = None
            rms_phase(vidx)

        class RmsHook:
            def __init__(self, vidx, slots=(0, 1)):
                self.vidx, self.slots, self.prev, self.n = vidx, slots, None, 0

            def A(self, t):
                si = self.slots[self.n % len(self.slots)]
                self.n += 1
                E("act", lambda e: e.activation(out=sq[si][:, :, :], in_=h[:, :, tsl(t)], func=AF.Square),
                  [R_h[c][t] for c in range(NCH)], [R_sq[si]])
                return si

            def B(self, t, si):
                bank = 6 + nxt("statbank", 2)
                stat_matmuls(bank, lambda c: sq[si][:, c, :], [R_sq[si]])
                ri = nxt("rstd", 2)
                rstd_from(ps[:, bank, :], R_ps[bank], 1.0 / D, RMS_EPS, ri)
                for c in range(NCH):
                    E("dve", lambda e, c=c: e.scalar_tensor_tensor(
                        out=xn[:, c, tsl(t)], in0=h[:, c, tsl(t)], scalar=vcol(self.vidx, c), in1=scr[ri][:, :],
                        op0=ALU.mult, op1=ALU.mult),
                      [R_h[c][t], R_scr[ri], R_vecs], [R_xn[t]])

            def __call__(self, t):
                if len(self.slots) == 1 and self.prev is not None:
                    self.B(*self.prev)
                    self.prev = None
                si = self.A(t)
                if self.prev is not None:
                    self.B(*self.prev)
                self.prev = (t, si)

            def flush(self):
                if self.prev is not None:
                    self.B(*self.prev)
                self.prev = None
                PRE["vidx"] = self.vidx

        def ffn_phase(f, vidx, hook=None):
            start_rms(vidx)
            units = [(J, t) for J in range(NG) for t in range(NT)]
            slot_of = {}

            def load(J):
                s = nxt("ffnw", 2)
                slot_of[J] = s
                sc.dma("pool", wgu[s][:, :, :, :].rearrange("p a k c -> p (a k c)").rearrange("p (a b) -> p a b", b=2048),
                       wgu_d[f, J].rearrange("p (a b) -> p a b", b=2048), [], [R_wgu[s]])
                sc.dma("pool", wo[s][:, :, :].rearrange("p a c -> p (a c)").rearrange("p (a b) -> p a b", b=1024),
                       wo_d[f, J].rearrange("p (a b) -> p a b", b=1024), [], [R_wo[s]])

            def gu(n, jjs):
                J, t = units[n]
                s = slot_of[J]
                ab = n % 2
                for jj in jjs:
                    gb = nxt("gbank", 2)
                    ub = 2 + nxt("ubank", 2)
                    for k in range(8):
                        E("pe", lambda e, k=k: e.matmul(ps[:, gb, :], wgu[s][:, 0, k, jj * 128:(jj + 1) * 128],
                                                        xn[:, k, tsl(t)], start=(k == 0), stop=(k == 7)),
                          [R_wgu[s], R_xn[t]], [R_ps[gb]], inc=(k == 7))
                    for k in range(8):
                        E("pe", lambda e, k=k: e.matmul(ps[:, ub, :], wgu[s][:, 1, k, jj * 128:(jj + 1) * 128],
                                                        xn[:, k, tsl(t)], start=(k == 0), stop=(k == 7)),
                          [R_wgu[s], R_xn[t]], [R_ps[ub]], inc=(k == 7))
                    si = 2 + nxt("sgb", 2)
                    E("act", lambda e: e.activation(out=scr[si][:, :], in_=ps[:, gb, :], func=AF.Silu),
                      [R_ps[gb]], [R_scr[si]])
                    E("dve", lambda e: e.tensor_tensor(out=actb[ab][:, jj, :], in0=ps[:, ub, :], in1=scr[si][:, :],
                                                       op=ALU.mult),
                      [R_ps[ub], R_scr[si]], [R_act[ab]])

            def oacc(n, ms):
                J, t = units[n]
                s = slot_of[J]
                ab = n % 2
                for m in ms:
                    ob = 4 + nxt("obank4", 4)
                    for jj in range(GH):
                        E("pe", lambda e, jj=jj: e.matmul(ps[:, ob, :], wo[s][:, jj, m * 128:(m + 1) * 128],
                                                          actb[ab][:, jj, :], start=(jj == 0), stop=(jj == GH - 1)),
                          [R_wo[s], R_act[ab]], [R_ps[ob]], inc=(jj == GH - 1))
                    E("dve", lambda e: e.scalar_tensor_tensor(out=h[:, m, tsl(t)], in0=ps[:, ob, :], scalar=0.5,
                                                              in1=h[:, m, tsl(t)], op0=ALU.mult, op1=ALU.add),
                      [R_ps[ob], R_h[m][t]], [R_h[m][t]])

            load(0)
            mper = NCH // GH
            for n in range(len(units) + 1):
                for jj in range(GH):
                    if n < len(units):
                        gu(n, [jj])
                    if n >= 1:
                        oacc(n - 1, range(jj * mper, (jj + 1) * mper))
                        if hook is not None and jj == GH - 1 and units[n - 1][0] == NG - 1:
                            hook(units[n - 1][1])
                if n < len(units):
                    J, t = units[n]
                    if t == 0 and J + 1 < NG:
                        load(J + 1)
            if hook is not None:
                hook.flush()

        def ple_phase(l, hook=None):
            start_rms(4 * l + 3)
            ppb = sq[1][:, :, :].rearrange("p a b -> p (a b)").rearrange("p (k t) -> p k t", k=2)
            R_ppb = R_sq[1]
            sc.dma("pool", sq[1][:, :, :].rearrange("p a b -> p (a b)").rearrange("p (a b) -> p a b", b=1024),
                   ppT[l].rearrange("p k t -> p (k t)").rearrange("p (a b) -> p a b", b=1024), [], [R_ppb])
            for m in range(NCH):
                wi = nxt("wch", NWCH)
                sc.dma("pool", wch[wi][:, 0, :, :].rearrange("p k c -> p (k c)"), plg_d[l, m], [], [R_wch[wi]])
                sc.dma("pool", wch[wi][:, 1, 0:2, :].rearrange("p k c -> p (k c)"), plp_d[l, m], [], [R_wch[wi]])
                for t in range(NT):
                    gb = nxt("gbank", 2)
                    ub = 2 + nxt("ubank", 2)
                    for k in range(8):
                        E("pe", lambda e, k=k: e.matmul(ps[:, gb, :], wch[wi][:, 0, k, :], xn[:, k, tsl(t)],
                                                        start=(k == 0), stop=(k == 7)),
                          [R_wch[wi], R_xn[t]], [R_ps[gb]], inc=(k == 7))
                    for k in range(2):
                        E("pe", lambda e, k=k: e.matmul(ps[:, ub, :], wch[wi][:, 1, k, :], ppb[:, k, tsl(t)],
                                                        start=(k == 0), stop=(k == 1)),
                          [R_wch[wi], R_ppb], [R_ps[ub]], inc=(k == 1))
                    si = 2 + nxt("sgb", 2)
                    E("act", lambda e: e.activation(out=scr[si][:, :], in_=ps[:, gb, :], func=AF.Sigmoid),
                      [R_ps[gb]], [R_scr[si]])
                    E("dve", lambda e: e.tensor_tensor(out=scr[si][:, :], in0=ps[:, ub, :], in1=scr[si][:, :],
                                                       op=ALU.mult),
                      [R_ps[ub], R_scr[si]], [R_scr[si]])
                    E("dve", lambda e: e.tensor_tensor(out=h[:, m, tsl(t)], in0=h[:, m, tsl(t)],
                                                       in1=scr[si][:, :], op=ALU.add),
                      [R_scr[si], R_h[m][t]], [R_h[m][t]])
                    if hook is not None and m == NCH - 1:
                        hook(t)
            if hook is not None:
                hook.flush()

        def conv_phase(l, hook=None):
            vb = 18 + 37 * l
            start_rms(4 * l + 1)
            with ExitStack() as st2:
                ubuf = st2.enter_context(nc.sbuf_tensor("s_" + f"ubuf{l}", [128, NCH, PAD + TT], BF16))
                yb = st2.enter_context(nc.sbuf_tensor("s_" + f"yb{l}", [128, NCH, TT], F32))
                dg0 = st2.enter_context(nc.sbuf_tensor("s_" + f"dg{l}_0", [128, CONVW, 128], BF16))
                dg1 = wgu[1][:, :, :, :].rearrange("p a k c -> p (a k c)")[:, 0:CONVW * 128].rearrange(
                    "p (j q) -> p j q", q=128)
                dg = [dg0, dg1]
                R_dg = [Res("dg0"), R_wgu[1]]
                R_ub = [Res(f"ubuf{c}") for c in range(NCH)]
                R_yb = [Res(f"yb{c}") for c in range(NCH)]
                E("pool", lambda e: e.memset(ubuf[:, :, 0:PAD], 0.0), [], R_ub)
                MU, MSQ, RS, NM = 4, 5, 6, 7

                def ag_chunk(t, c):
                    wi = nxt("wch", NWCH)
                    sc.dma("pool", wch[wi][:, :, :, :].rearrange("p a k c -> p (a k c)"), cwin_d[l, c],
                           [], [R_wch[wi]])
                    ab_ = nxt("gbank", 2)
                    gb_ = 2 + nxt("ubank", 2)
                    for k in range(8):
                        E("pe", lambda e, k=k: e.matmul(ps[:, ab_, :], wch[wi][:, 0, k, :], xn[:, k, tsl(t)],
                                                        start=(k == 0), stop=(k == 7)),
                          [R_wch[wi], R_xn[t]], [R_ps[ab_]], inc=(k == 7))
                    for k in range(8):
                        E("pe", lambda e, k=k: e.matmul(ps[:, gb_, :], wch[wi][:, 1, k, :], xn[:, k, tsl(t)],
                                                        start=(k == 0), stop=(k == 7)),
                          [R_wch[wi], R_xn[t]], [R_ps[gb_]], inc=(k == 7))
                    si = 2 + nxt("sgb", 2)
                    E("act", lambda e: e.activation(out=scr[si][:, :], in_=ps[:, gb_, :], func=AF.Sigmoid,
                                                    bias=vcol(vb + 1, c)),
                      [R_ps[gb_], R_vecs], [R_scr[si]])
                    if t > 0:
                        E("pool", lambda e: e.tensor_copy(out=ubuf[:, c, 0:PAD], in_=ubuf[:, c, TT:TT + PAD]),
                          [R_ub[c]], [R_ub[c]])
                    E("dve", lambda e: e.scalar_tensor_tensor(out=ubuf[:, c, PAD:PAD + TT], in0=ps[:, ab_, :],
                                                              scalar=vcol(vb, c), in1=scr[si][:, :],
                                                              op0=ALU.add, op1=ALU.mult),
                      [R_ps[ab_], R_scr[si], R_vecs], [R_ub[c]])

                def build_dg(c):
                    di = c % 2
                    for j in range(CONVW):
                        E("dve", lambda e, j=j: e.tensor_scalar(out=dg[di][:, j, :], in0=ident_bf[:, :],
                                                                scalar1=vcol(vb + 6 + j, c), scalar2=None,
                                                                op0=ALU.mult),
                          [R_const, R_vecs], [R_dg[di]], inc=(j == CONVW - 1))

                def taps(t):
                    build_dg(0)
                    for c in range(NCH):
                        di = c % 2
                        if c + 1 < NCH:
                            build_dg(c + 1)
                        yb_ = 4 + nxt("obank", 2)
                        for j in range(CONVW):
                            E("pe", lambda e, j=j: e.matmul(ps[:, yb_, :], dg[di][:, j, :], ubuf[:, c, j:j + TT],
                                                            start=(j == 0), stop=(j == CONVW - 1)),
                              [R_dg[di], R_ub[c]], [R_ps[yb_]], inc=(j == CONVW - 1))
                        E("act", lambda e: e.activation(out=yb[:, c, :], in_=ps[:, yb_, :], func=AF.Identity,
                                                        bias=vcol(vb + 2, c)),
                          [R_ps[yb_], R_vecs], [R_yb[c]])

                def lnstats(t):
                    si_ = nxt("sq", 2)
                    E("act", lambda e: e.activation(out=sq[si_][:, :, :], in_=yb[:, :, :], func=AF.Square),
                      R_yb, [R_sq[si_]])
                    E("dve", lambda e: e.tensor_copy(out=xn[:, :, tsl(t)], in_=yb[:, :, :]), R_yb, [R_xn[t]])
                    b1 = 6 + nxt("statbank", 2)
                    stat_matmuls(b1, lambda c: xn[:, c, tsl(t)], [R_xn[t]])
                    b2 = 6 + nxt("statbank", 2)
                    stat_matmuls(b2, lambda c: sq[si_][:, c, :], [R_sq[si_]])
                    E("dve", lambda e: e.tensor_scalar(out=scr[MU][:, :], in0=ps[:, b1, :], scalar1=1.0 / D,
                                                       scalar2=None, op0=ALU.mult), [R_ps[b1]], [R_scr[MU]])
                    E("dve", lambda e: e.tensor_tensor(out=scr[MSQ][:, :], in0=scr[MU][:, :], in1=scr[MU][:, :],
                                                       op=ALU.mult), [R_scr[MU]], [R_scr[MSQ]])
                    E("dve", lambda e: e.scalar_tensor_tensor(out=scr[MSQ][:, :], in0=ps[:, b2, :], scalar=1.0 / D,
                                                              in1=scr[MSQ][:, :], op0=ALU.mult, op1=ALU.subtract),
                      [R_ps[b2], R_scr[MSQ]], [R_scr[MSQ]])
                    rstd_from(scr[MSQ][:, :], R_scr[MSQ], 1.0, LN_EPS, RS)
                    E("dve", lambda e: e.scalar_tensor_tensor(out=scr[NM][:, :], in0=scr[MU][:, :], scalar=-1.0,
                                                              in1=scr[RS][:, :], op0=ALU.mult, op1=ALU.mult),
                      [R_scr[MU], R_scr[RS]], [R_scr[NM]])

                def lnnorm(t, c):
                    zi = nxt("z", 2)
                    E("dve", lambda e: e.tensor_tensor(out=scr[zi][:, :], in0=yb[:, c, :], in1=scr[RS][:, :],
                                                       op=ALU.mult), [R_yb[c], R_scr[RS]], [R_scr[zi]])
                    E("dve", lambda e: e.tensor_tensor(out=scr[zi][:, :], in0=scr[zi][:, :], in1=scr[NM][:, :],
                                                       op=ALU.add), [R_scr[zi], R_scr[NM]], [R_scr[zi]])
                    E("act", lambda e: e.activation(out=xn[:, c, tsl(t)], in_=scr[zi][:, :], func=AF.Silu,
                                                    bias=vcol(vb + 4, c), scale=vcol(vb + 3, c)),
                      [R_scr[zi], R_vecs], [R_xn[t]])

                def outproj(t):
                    for m in range(NCH):
                        wi = nxt("wch", NWCH)
                        sc.dma("pool", wch[wi][:, 0, :, :].rearrange("p k c -> p (k c)"), cwout_d[l, m],
                               [], [R_wch[wi]])
                        ob = 4 + nxt("obank", 2)
                        for k in range(8):
                            E("pe", lambda e, k=k: e.matmul(ps[:, ob, :], wch[wi][:, 0, k, :], xn[:, k, tsl(t)],
                                                            start=(k == 0), stop=(k == 7)),
                              [R_wch[wi], R_xn[t]], [R_ps[ob]], inc=(k == 7))
                        E("dve", lambda e: e.scalar_tensor_tensor(out=h[:, m, tsl(t)], in0=ps[:, ob, :],
                                                                  scalar=vcol(vb + 5, m), in1=h[:, m, tsl(t)],
                                                                  op0=ALU.add, op1=ALU.add),
                          [R_ps[ob], R_h[m][t], R_vecs], [R_h[m][t]])

                for c in range(NCH):
                    ag_chunk(0, c)
                for t in range(NT):
                    taps(t)
                    lnstats(t)
                    for c in range(NCH):
                        lnnorm(t, c)
                        if t + 1 < NT:
                            ag_chunk(t + 1, c)
                    outproj(t)
                    if hook is not None:
                        hook(t)
                if hook is not None:
                    hook.flush()
                sc.barrier()

        def kv_phase():
            start_rms(16)
            with ExitStack() as st2:
                kst = [st2.enter_context(nc.sbuf_tensor("s_" + f"kst{i}", [128, S], BF16)) for i in range(2)]
                vst = [st2.enter_context(nc.sbuf_tensor("s_" + f"vst{i}", [128, D], BF16)) for i in range(2)]
                wvb = [st2.enter_context(nc.sbuf_tensor("s_" + f"wvb{i}", [128, 8, 512], BF16)) for i in range(2)]
                R_kst = [Res("kst0"), Res("kst1")]
                R_vst = [Res("vst0"), Res("vst1")]
                R_wvb = [Res("wvb0"), Res("wvb1")]
                for hh in range(NHEAD):
                    wi = nxt("wch", NWCH)
                    sc.dma("pool", wch[wi][:, 0, :, :].rearrange("p k c -> p (k c)"), wk_d[hh], [], [R_wch[wi]])
                    ks = nxt("kst", 2)
                    for t in range(NT):
                        bk = nxt("gbank", 2)
                        for k in range(8):
                            E("pe", lambda e, k=k: e.matmul(ps[:, bk, :], wch[wi][:, 0, k, :], xn[:, k, tsl(t)],
                                                            start=(k == 0), stop=(k == 7)),
                              [R_wch[wi], R_xn[t]], [R_ps[bk]], inc=(k == 7))
                        E("act", lambda e: e.activation(out=kst[ks][:, tsl(t)], in_=ps[:, bk, :], func=AF.Identity),
                          [R_ps[bk]], [R_kst[ks]])
                    sc.dma("sp", kT_dram[hh], kst[ks][:, :], [R_kst[ks]], [R_kv], semres=R_kstout[ks])
                for half in range(2):
                    sc.dma("pool", wvb[half][:, :, :].rearrange("p k c -> p (k c)").rearrange("p (a b) -> p a b", b=2048),
                           wv_d[half].rearrange("p (a b) -> p a b", b=2048), [], [R_wvb[half]])
                for kt in range(16):
                    vs = nxt("vst", 2)
                    for half in range(2):
                        bk = 2 + nxt("ubank", 2)
                        for k in range(8):
                            E("pe", lambda e, k=k: e.matmul(ps[:, bk, :], xn[:, k, kt * 128:(kt + 1) * 128],
                                                            wvb[half][:, k, :], start=(k == 0), stop=(k == 7)),
                              [R_wvb[half], R_xn[kt // 4]], [R_ps[bk]], inc=(k == 7))
                        E("dve", lambda e: e.tensor_copy(out=vst[vs][:, half * 512:(half + 1) * 512], in_=ps[:, bk, :]),
                          [R_ps[bk]], [R_vst[vs]])
                    sc.dma("sp", v_dram[:, :, kt, :].rearrange("h p e -> p h e"),
                           vst[vs][:, :].rearrange("p (h e) -> p h e", e=128), [R_vst[vs]], [R_kv],
                           semres=R_vstout[vs])
                sc.barrier()

        R_kv = Res("kvdram")
        R_kstout = [Res("kstout0"), Res("kstout1")]
        R_vstout = [Res("vstout0"), Res("vstout1")]

        def attn_phase(layer, j_, A, hook=None):
            lam_init = lambda_init_of(layer)
            start_rms(4 * layer + 1)
            kTh, vh, qT, pT, wob = A["kTh"], A["vh"], A["qT"], A["pT"], A["wob"]
            R_kTh, R_vh, R_qT, R_pT, R_wob = A["R_kTh"], A["R_vh"], A["R_qT"], A["R_pT"], A["R_wob"]
            small, R_small = A["small"], A["R_small"]
            lvb = A["lvb"]
            base = j_ * 256
            for i2 in range(2):
                E("dve", lambda e, i2=i2: e.tensor_tensor(out=A["ltmp"][:, :], in0=lvb[:, base + i2 * 128: base + i2 * 128 + 64],
                                                          in1=lvb[:, base + i2 * 128 + 64: base + i2 * 128 + 128],
                                                          op=ALU.mult), [A["R_lvb"]], [A["R_ltmp"]])
                E("dve", lambda e, i2=i2: e.tensor_reduce(out=small[:, i2:i2 + 1], in_=A["ltmp"][:, :],
                                                          axis=mybir.AxisListType.X, op=ALU.add),
                  [A["R_ltmp"]], [R_small])
            E("act", lambda e: e.activation(out=small[:, 0:2], in_=small[:, 0:2], func=AF.Exp), [R_small], [R_small])
            E("dve", lambda e: e.scalar_tensor_tensor(out=small[:, 2:3], in0=small[:, 1:2], scalar=-lam_init,
                                                      in1=small[:, 0:1], op0=ALU.add, op1=ALU.subtract),
              [R_small], [R_small])
            E("dve", lambda e: e.tensor_scalar(out=small[:, 3:4], in0=A["sublnb"][:, j_:j_ + 1],
                                               scalar1=1.0 - lam_init, scalar2=None, op0=ALU.mult),
              [A["R_sublnb"]], [R_small])
            biasd, biasn, b31 = A["biasd"], A["biasn"], A["b31"]
            R_bias = A["R_bias"]
            for hh in range(NHEAD):
                hs = 0
                ws = hh % 2
                wi = nxt("wch", NWCH)
                sc.dma("pool", wch[wi][:, 0, :, :].rearrange("p k c -> p (k c)"), wq_d[j_, hh], [], [R_wch[wi]])
                sc.dma("pool", wob[ws][:, :], wao_d[j_, hh], [], [R_wob[ws]])
                sc.dma("sp", kTh[hs][:, :], kT_dram[hh], [R_kv], [R_kTh[hs]])
                sc.dma("sp", vh[hs][:, :, :], v_dram[hh], [R_kv], [R_vh[hs]])
                for t in range(NT):
                    bk = 6 + nxt("statbank", 2)
                    for k in range(8):
                        E("pe", lambda e, k=k: e.matmul(ps[:, bk, :], wch[wi][:, 0, k, :], xn[:, k, tsl(t)],
                                                        start=(k == 0), stop=(k == 7)),
                          [R_wch[wi], R_xn[t]], [R_ps[bk]], inc=(k == 7))
                    E("dve", lambda e: e.tensor_copy(out=qT[hs][:, tsl(t)], in_=ps[:, bk, :]),
                      [R_ps[bk]], [R_qT[hs]])
                for qi in range(NT):
                    nk = 4 * (qi + 1)
                    NUM = [2, 3]
                    DEN = [4, 5]
                    for kj in range(nk):
                        r = kj - 4 * qi
                        c0 = max(r, 0)
                        qs = slice(qi * TT + c0 * 128, (qi + 1) * TT)
                        cs = slice(c0 * 128, TT)
                        pi = nxt("pT", 2)
                        for mp in range(2):
                            sbk = mp
                            rows = slice(mp * 64, (mp + 1) * 64)
                            E("pe", lambda e: e.matmul(ps[:, sbk, cs], kTh[hs][rows, kj * 128:(kj + 1) * 128],
                                                       qT[hs][rows, qs], start=True, stop=True),
                              [R_kTh[hs], R_qT[hs]], [R_ps[sbk]])
                            pr = R_pT[pi][mp]
                            pt = pT[pi][mp]
                            if r <= -2:
                                E("act", lambda e: e.activation(out=pt[:, cs], in_=ps[:, sbk, cs], func=AF.Exp,
                                                                bias=b31[:, hh:hh + 1], scale=0.125),
                                  [R_ps[sbk], R_bias], [pr])
                            else:
                                for sbl in range(c0, 4):
                                    ss = slice(sbl * 128, (sbl + 1) * 128)
                                    dd = sbl - r
                                    if dd >= 2:
                                        E("act", lambda e, ss=ss: e.activation(out=pt[:, ss], in_=ps[:, sbk, ss],
                                                                               func=AF.Exp, bias=b31[:, hh:hh + 1],
                                                                               scale=0.125),
                                          [R_ps[sbk], R_bias], [pr])
                                    else:
                                        bt = biasd if dd == 0 else biasn
                                        ti = 2 + nxt("sgb", 2)
                                        E("dve", lambda e, ss=ss, bt=bt: e.scalar_tensor_tensor(
                                            out=scr[ti][:, 0:128], in0=ps[:, sbk, ss], scalar=0.125,
                                            in1=bt[:, hh, :], op0=ALU.mult, op1=ALU.add),
                                          [R_ps[sbk], R_bias], [R_scr[ti]])
                                        E("act", lambda e, ss=ss: e.activation(out=pt[:, ss], in_=scr[ti][:, 0:128],
                                                                               func=AF.Exp),
                                          [R_scr[ti]], [pr])
                        for mp in range(2):
                            pr = R_pT[pi][mp]
                            pt = pT[pi][mp]
                            E("pe", lambda e: e.matmul(ps[:, NUM[mp], cs], vh[hs][:, kj, :], pt[:, cs],
                                                       start=(kj == 0), stop=(kj == nk - 1)),
                              [R_vh[hs], pr], [R_ps[NUM[mp]]], inc=(kj == nk - 1))
                            E("pe", lambda e: e.matmul(ps[:, DEN[mp], cs], ones_bf[:, :], pt[:, cs],
                                                       start=(kj == 0), stop=(kj == nk - 1)),
                              [R_const, pr], [R_ps[DEN[mp]]], inc=True)
                    R1, R2, O1, T2 = 4, 5, 6, 7
                    for mp, ri in ((0, R1), (1, R2)):
                        E("act", lambda e, mp=mp, ri=ri: e.activation(out=scr[ri][:, :], in_=ps[:, DEN[mp], :],
                                                                      func=AF.Ln), [R_ps[DEN[mp]]], [R_scr[ri]])
                        E("act", lambda e, ri=ri: e.activation(out=scr[ri][:, :], in_=scr[ri][:, :], func=AF.Exp,
                                                               scale=-1.0), [R_scr[ri]], [R_scr[ri]])
                    E("dve", lambda e: e.tensor_tensor(out=scr[O1][:, :], in0=ps[:, NUM[0], :], in1=scr[R1][:, :],
                                                       op=ALU.mult), [R_ps[NUM[0]], R_scr[R1]], [R_scr[O1]])
                    E("dve", lambda e: e.tensor_tensor(out=scr[T2][:, :], in0=ps[:, NUM[1], :], in1=scr[R2][:, :],
                                                       op=ALU.mult), [R_ps[NUM[1]], R_scr[R2]], [R_scr[T2]])
                    E("dve", lambda e: e.scalar_tensor_tensor(out=scr[O1][:, :], in0=scr[T2][:, :],
                                                              scalar=small[:, 2:3], in1=scr[O1][:, :],
                                                              op0=ALU.mult, op1=ALU.add),
                      [R_scr[T2], R_scr[O1], R_small], [R_scr[O1]])
                    si = nxt("sq", 2)
                    E("act", lambda e: e.activation(out=sq[si][:, 0, :], in_=scr[O1][:, :], func=AF.Square),
                      [R_scr[O1]], [R_sq[si]])
                    bk = 6 + nxt("statbank", 2)
                    stat_matmuls(bk, lambda c: sq[si][:, 0, :], [R_sq[si]], nchunks=1)
                    rstd_from(ps[:, bk, :], R_ps[bk], 1.0 / 128, RMS_EPS, T2)
                    E("dve", lambda e: e.scalar_tensor_tensor(out=sq[si][:, 1, :], in0=scr[O1][:, :],
                                                              scalar=small[:, 3:4], in1=scr[T2][:, :],
                                                              op0=ALU.mult, op1=ALU.mult),
                      [R_scr[O1], R_scr[T2], R_small, R_sq[si]], [R_sq[si]])
                    for m in range(NCH):
                        ob = 6 + nxt("statbank", 2)
                        E("pe", lambda e: e.matmul(ps[:, ob, :], wob[ws][:, m * 128:(m + 1) * 128], sq[si][:, 1, :],
                                                   start=True, stop=True), [R_wob[ws], R_sq[si]], [R_ps[ob]])
                        E("dve", lambda e: e.tensor_tensor(out=h[:, m, tsl(qi)], in0=ps[:, ob, :],
                                                           in1=h[:, m, tsl(qi)], op=ALU.add),
                          [R_ps[ob], R_h[m][qi]], [R_h[m][qi]])
                    if hook is not None and hh == NHEAD - 1:
                        hook(qi)
            if hook is not None:
                hook.flush()

        def attn_setup(st2):
            A = {}
            A["kTh"] = [st2.enter_context(nc.sbuf_tensor("s_" + f"kTh{i}", [128, S], BF16)) for i in range(1)]
            A["vh"] = [st2.enter_context(nc.sbuf_tensor("s_" + f"vh{i}", [128, 16, 128], BF16)) for i in range(1)]
            A["qT"] = [st2.enter_context(nc.sbuf_tensor("s_" + f"qT{i}", [128, S], BF16)) for i in range(1)]
            A["pT"] = [[st2.enter_context(nc.sbuf_tensor("s_" + f"pT{i}_{m}", [128, TT], BF16)) for m in range(2)]
                       for i in range(2)]
            A["wob"] = [st2.enter_context(nc.sbuf_tensor("s_" + f"wob{i}", [128, D], BF16)) for i in range(2)]
            A["small"] = st2.enter_context(nc.sbuf_tensor("s_small", [128, 8], F32))
            A["ltmp"] = st2.enter_context(nc.sbuf_tensor("s_ltmp", [128, 64], F32))
            A["lvb"] = st2.enter_context(nc.sbuf_tensor("s_lvb", [128, 512], F32))
            A["sublnb"] = st2.enter_context(nc.sbuf_tensor("s_sublnb", [128, 2], F32))
            A["biasd"] = st2.enter_context(nc.sbuf_tensor("s_biasd", [128, NHEAD, 128], F32))
            A["biasn"] = st2.enter_context(nc.sbuf_tensor("s_biasn", [128, NHEAD, 128], F32))
            A["b31"] = st2.enter_context(nc.sbuf_tensor("s_b31", [128, NHEAD], F32))
            gsb = scr[0][:, 0:384]
            oneh = scr[1][0:32, 0:384]
            maskr = scr[2][0:1, 0:384]
            ones1 = scr[3][0:1, 0:128]
            ones32 = scr[4][0:32, 0:128]
            lhs = scr[5][0:32, 0:128]
            relb = scr[6][0:32, 0:8]
            for k in ("kTh", "vh", "qT", "wob"):
                A["R_" + k] = [Res(k + "0"), Res(k + "1")]
            A["R_pT"] = [[Res(f"pT{i}_{m}") for m in range(2)] for i in range(2)]
            for k in ("small", "ltmp", "lvb", "sublnb", "bias"):
                A["R_" + k] = Res(k)
            R_gsb, R_oneh, R_maskr, R_ones1, R_ones32, R_lhs, R_relb = (R_scr[i] for i in range(7))
            R_g = Res("gdram")
            sc.dma("sp", A["lvb"][:, :], lv_d[:, :], [], [A["R_lvb"]])
            sc.dma("sp", A["sublnb"][:, :], subln_d[:, :], [], [A["R_sublnb"]])
            sc.dma("sp", relb, relb_d[:, :], [], [R_relb], semres=Res("relb"))
            sc.dma("sp", oneh, onehot_d[:, :], [], [R_oneh], semres=Res("oneh"))
            sc.dma("sp", maskr, maskrow_d[:, :], [], [R_maskr], semres=Res("maskr"))
            E("pool", lambda e: e.memset(ones1, 1.0), [], [R_ones1])
            E("pool", lambda e: e.memset(ones32, 1.0), [], [R_ones32])
            for hh in range(NHEAD):
                E("dve", lambda e: e.tensor_scalar(out=lhs, in0=ones32, scalar1=relb[:, hh:hh + 1],
                                                   scalar2=None, op0=ALU.mult), [R_ones32, R_relb], [R_lhs])
                E("pe", lambda e: e.matmul(ps[:, 0, 0:384], lhs, oneh, start=True, stop=False),
                  [R_lhs, R_oneh], [R_ps[0]], inc=False)
                E("pe", lambda e: e.matmul(ps[:, 0, 0:384], ones1, maskr, start=False, stop=True),
                  [R_ones1, R_maskr], [R_ps[0]])
                E("dve", lambda e: e.tensor_copy(out=gsb, in_=ps[:, 0, 0:384]), [R_ps[0]], [R_gsb])
                E("dve", lambda e: e.tensor_copy(out=A["b31"][:, hh:hh + 1], in_=gsb[:, 383:384]),
                  [R_gsb], [A["R_bias"]])
                sc.dma("sp", g_dram[hh], gsb, [R_gsb], [R_g], semres=Res(f"gout{hh}"))
            for hh in range(NHEAD):
                for dd, bt in ((0, A["biasd"]), (128, A["biasn"])):
                    src = bass.AP(tensor=g_dram.tensor, offset=g_dram[hh].offset + 128 + dd,
                                  ap=[[383, 128], [1, 128]])
                    sc.dma("sp", bt[:, hh, :], src, [R_g], [A["R_bias"]], semres=Res(f"bt{hh}_{dd}"))
            return A

        def dump_h():
            for t in range(NT):
                sc.out_tks.append(sc.dma("sp", outT[:, :, tsl(t)], h[:, :, tsl(t)],
                                         [R_h[c][t] for c in range(NCH)], [R_out], semres=Res(f"dump{t}")))

        R_out = Res("out")
        done = [False]

        nph = [0]

        def check(name):
            if debug:
                src = h[:, :, :].rearrange("p c (a b) -> p c a b", b=128)[:, :, :, 0:32]
                sc.out_tks.append(sc.dma("sp", dbg[nph[0]], src, [R_h[c][t] for c in range(NCH) for t in range(NT)],
                                         [R_out], semres=Res(f"dbg{nph[0]}")))
                nph[0] += 1
            if stop_after == name and not done[0]:
                dump_h()
                done[0] = True
            return done[0]

        def forward():
            for l in range(2):
                ffn_phase(2 * l, 4 * l + 0, RmsHook(4 * l + 1))
                if check(f"ffn1_{l}"):
                    return
                conv_phase(l, RmsHook(4 * l + 2))
                if check(f"mix_{l}"):
                    return
                ffn_phase(2 * l + 1, 4 * l + 2, RmsHook(4 * l + 3))
                if check(f"ffn2_{l}"):
                    return
                ple_phase(l, RmsHook(4 * (l + 1) if l == 0 else 16, slots=(0,)))
                if check(f"ple_{l}"):
                    return
            kv_phase()
            with ExitStack() as st2:
                A = attn_setup(st2)
                for l in range(2, 4):
                    ffn_phase(2 * l, 4 * l + 0, RmsHook(4 * l + 1))
                    if check(f"ffn1_{l}"):
                        return
                    attn_phase(l, l - 2, A, RmsHook(4 * l + 2))
                    if check(f"mix_{l}"):
                        return
                    ffn_phase(2 * l + 1, 4 * l + 2, RmsHook(4 * l + 3))
                    if check(f"ffn2_{l}"):
                        return
                    ple_phase(l, RmsHook(4 * (l + 1), slots=(0,)) if l == 2 else None)
                    if check(f"ple_{l}"):
                        return
                sc.barrier()
            with ExitStack() as st2:
                ob = [st2.enter_context(nc.sbuf_tensor("s_" + f"outb{i}", [128, NCH, TT], F32)) for i in range(2)]
                R_ob = [Res("outb0"), Res("outb1")]
                cur = {}

                def dst(t, c, ri):
                    if c == 0:
                        cur["i"] = nxt("outb", 2)
                    oi = cur["i"]
                    E("dve", lambda e: e.scalar_tensor_tensor(out=ob[oi][:, c, :], in0=h[:, c, tsl(t)],
                                                              scalar=vcol(17, c), in1=scr[ri][:, :],
                                                              op0=ALU.mult, op1=ALU.mult),
                      [R_h[c][t], R_scr[ri], R_vecs], [R_ob[oi]])
                    if c == NCH - 1:
                        sc.out_tks.append(sc.dma("sp", outT[:, :, tsl(t)], ob[oi][:, :, :], [R_ob[oi]], [R_out],
                                                 semres=Res(f"fin{t}")))

                rms_phase(17, dst_of=dst)
                sc.barrier()

        forward()
        sc._wait_for("sp", sc.out_tks)
        sc.barrier()
        print("sched counts:", sc.cnt, "nsem", sc.nsem)
    return nc


def _rel_bucket_np(n):
    n = np.maximum(n, 0)
    max_exact = 16
    nf = np.maximum(n, 1).astype(np.float32)
    large = max_exact + (np.log(nf / max_exact) / math.log(128 / max_exact) * (32 - max_exact)).astype(np.int32)
    large = np.minimum(large, 31)
    return np.where(n < max_exact, n, large)


def _bucket_table():
    return _rel_bucket_np(np.arange(0, 256))


def prep_shared(inp):
    f = np.float32
    A = {k: np.asarray(v, dtype=f) for k, v in inp.items()}
    sh = {}
    vl = []
    for l in range(4):
        vl += [A["ffn1_norm"][l], A["mix_norm"][l], A["ffn2_norm"][l], A["ple_norm"][l]]
    vl += [A["kv_norm"], A["final_norm"]]
    for l in range(2):
        vl += [A["conv_b_in"][l][:D], A["conv_b_in"][l][D:], A["conv_b_dw"][l], A["conv_ln_g"][l],
               A["conv_ln_b"][l], A["conv_b_out"][l]]
        vl += [A["conv_w_dw"][l][j] for j in range(CONVW)]
    V = np.stack(vl, 0)
    assert V.shape[0] == NV
    sh["vecs"] = np.ascontiguousarray(V.reshape(NV, 8, 128).transpose(2, 0, 1).reshape(128, NV * 8))
    wgu, wo = [], []
    for l in range(4):
        for nm in ("ffn1", "ffn2"):
            wi = A[nm + "_w_in"][l]
            wgu.append(wi.reshape(8, 128, 2, NG, 128 * GH).transpose(3, 1, 2, 0, 4).reshape(NG, 128, -1))
            wt = A[nm + "_w_out"][l]
            wo.append(wt.reshape(NG, GH, 128, D).transpose(0, 2, 1, 3).reshape(NG, 128, -1))
    sh["wgu"] = np.ascontiguousarray(np.stack(wgu, 0))
    sh["wo"] = np.ascontiguousarray(np.stack(wo, 0))

    def sq_tiles(W):
        return W.reshape(8, 128, 8, 128).transpose(2, 1, 0, 3).reshape(8, 128, 1024)

    def head_tiles(W):
        return W.reshape(8, 128, 2, 8, 64).transpose(3, 1, 0, 2, 4).reshape(8, 128, 1024)

    sh["plg"] = np.ascontiguousarray(np.stack([sq_tiles(A["ple_w_gate"][l]) for l in range(4)], 0))
    sh["plp"] = np.ascontiguousarray(np.stack(
        [A["ple_w_proj"][l].reshape(2, 128, 8, 128).transpose(2, 1, 0, 3).reshape(8, 128, 256) for l in range(4)], 0))
    sh["cwin"] = np.ascontiguousarray(np.stack(
        [A["conv_w_in"][l].reshape(8, 128, 2, 8, 128).transpose(3, 1, 2, 0, 4).reshape(8, 128, -1)
         for l in range(2)], 0))
    sh["cwout"] = np.ascontiguousarray(np.stack([sq_tiles(A["conv_w_out"][l]) for l in range(2)], 0))
    sh["wk"] = np.ascontiguousarray(head_tiles(A["w_kv"][:, :D]))
    wv = A["w_kv"][:, D:]
    sh["wv"] = np.ascontiguousarray(wv.reshape(8, 128, 2, 512).transpose(2, 1, 0, 3).reshape(2, 128, -1))
    sh["wq"] = np.ascontiguousarray(np.stack([head_tiles(A["attn_w_q"][l]) for l in range(2)], 0))
    sh["wao"] = np.ascontiguousarray(np.stack([A["attn_w_o"][l].reshape(8, 128, D) for l in range(2)], 0))
    lv = np.stack([np.concatenate([A["attn_lq1"][l], A["attn_lk1"][l], A["attn_lq2"][l], A["attn_lk2"][l]])
                   for l in range(2)], 0).reshape(1, 512)
    sh["lv"] = np.ascontiguousarray(np.broadcast_to(lv, (128, 512)))
    sh["subln"] = np.ascontiguousarray(A["attn_subln"].T)
    sh["relb"] = np.ascontiguousarray(A["rel_bias"])
    bt = _bucket_table()
    oh = np.zeros((32, 384), f)
    mr = np.zeros((1, 384), f)
    for npr in range(384):
        n = npr - 128
        if n < 0:
            mr[0, npr] = -1e30
        else:
            oh[bt[min(n, 255)], npr] = 1.0
    sh["onehot"] = oh
    sh["maskrow"] = mr
    return sh


def prep_core(inp, b):
    x = np.asarray(inp["x"], dtype=np.float32)[b]
    p = np.asarray(inp["p"], dtype=np.float32)[:, b]
    xT = np.ascontiguousarray(x.T.reshape(8, 128, S).transpose(1, 0, 2))
    ppT = np.ascontiguousarray(p.transpose(0, 2, 1).reshape(4, 2, 128, S).transpose(0, 2, 1, 3))
    return {"xT": xT, "ppT": ppT}


_PROG = {}


def run(inputs, stop_after=None, trace=False, debug=False):
    key = (stop_after, debug)
    if key not in _PROG:
        _PROG[key] = build_program(stop_after, debug)
    nc = _PROG[key]
    sh = prep_shared(inputs)
    in_maps = []
    for b in range(8):
        m = dict(sh)
        m.update(prep_core(inputs, b))
        in_maps.append(m)
    res = run_bass_kernel_spmd(nc, in_maps, core_ids=list(range(8)), **({"trace": True} if trace else {}))
    outs = []
    for b in range(8):
        o = np.asarray(res.results[b]["outT"])
        outs.append(o.transpose(1, 0, 2).reshape(D, S).T)
    return np.ascontiguousarray(np.stack(outs, 0).astype(np.float32)), res


def kernel(**inputs):
    out, _ = run(inputs)
    return out
```

```python
import math
from contextlib import ExitStack

import numpy as np
import concourse.bass as bass
import concourse.mybir as mybir
from concourse.bass_utils import run_bass_kernel_spmd
from concourse.alu_op_type import AluOpType as ALU

F32 = mybir.dt.float32
BF16 = mybir.dt.bfloat16
AF = mybir.ActivationFunctionType

D = 1024
S = 2048
NCH = 8
TT = 512
NT = S // TT
DFF = 4096
GH = 2
NG = DFF // (128 * GH)
CONVW = 31
PAD = CONVW - 1
NHEAD = 8
RMS_EPS = 1e-6
LN_EPS = 1e-5
NV = 18 + 37 * 2
SAME_ENG_SYNC = True
NWCH = 3
PIPELINE_ATTN = True
DEFER_FIN = True
WARM_JUNK = 0
DEN_ENG1 = "dve"


class Tk:
    __slots__ = ("sem", "val", "eng")

    def __init__(self, sem, val, eng=None):
        self.sem, self.val, self.eng = sem, val, eng


class Res:
    def __init__(self, name, excl=False):
        self.name = name
        self.w = None
        self.r = []
        self.sem = None
        self.cnt = 0
        self.excl = excl


class Sched:
    def __init__(self, nc, stack):
        self.nc = nc
        self.stack = stack
        self.eng = {"pe": nc.tensor, "act": nc.scalar, "dve": nc.vector, "pool": nc.gpsimd, "sp": nc.sync}
        self.sem = {k: stack.enter_context(nc.semaphore("sem_" + k)) for k in self.eng}
        self.cnt = {k: 0 for k in self.eng}
        self.pending = {k: [] for k in self.eng}
        self.waited = {}
        self.nsem = len(self.eng)
        self.dma_res = []
        self.out_tks = []

    def _wait_for(self, en, tickets):
        best = {}
        for tk in tickets:
            if tk is None:
                continue
            if tk.val is None:
                assert tk.eng == en, "pending ticket consumed cross-engine"
                continue
            if tk.eng == en and (en == "pe" or not SAME_ENG_SYNC):
                continue
            key = id(tk.sem)
            if key not in best or best[key].val < tk.val:
                best[key] = tk
        for key, tk in best.items():
            if self.waited.get((en, key), 0) >= tk.val:
                continue
            self.eng[en].wait_ge(tk.sem, tk.val)
            self.waited[(en, key)] = tk.val

    def _deps(self, reads, writes):
        deps = []
        for r in reads:
            deps.append(r.w)
            if r.excl:
                deps.extend(r.r)
        for w in writes:
            deps.append(w.w)
            deps.extend(w.r)
        return deps

    def emit(self, en, fn, reads=(), writes=(), inc=True):
        self._wait_for(en, self._deps(reads, writes))
        ins = fn(self.eng[en])
        if inc:
            ins.then_inc(self.sem[en], 1)
            self.cnt[en] += 1
            tk = Tk(self.sem[en], self.cnt[en], en)
            for p in self.pending[en]:
                p.val = self.cnt[en]
            self.pending[en] = []
        else:
            tk = Tk(self.sem[en], None, en)
            self.pending[en].append(tk)
        for w in writes:
            w.w = tk
            w.r = []
        for r in reads:
            r.r.append(tk)
        return tk

    def dma(self, q, out, in_, reads, writes, semres=None):
        semres = semres or writes[0]
        if semres.sem is None:
            semres.sem = self.stack.enter_context(self.nc.semaphore("d_" + semres.name))
            self.nsem += 1
            self.dma_res.append(semres)
        self._wait_for(q, self._deps(reads, writes))
        self.eng[q].dma_start(out=out, in_=in_).then_inc(semres.sem, 16)
        semres.cnt += 16
        tk = Tk(semres.sem, semres.cnt, "dma")
        for w in writes:
            w.w = tk
            w.r = []
        for r in reads:
            r.r.append(tk)
        return tk

    def barrier(self):
        tks = [Tk(self.sem[k], self.cnt[k], k) for k in self.eng if self.cnt[k] > 0]
        dtk = [Tk(r.sem, r.cnt, "dma") for r in self.dma_res if r.cnt > 0]
        for en in self.eng:
            self._wait_for(en, [t for t in tks if t.eng != en] + dtk)


def lambda_init_of(layer):
    return 0.8 - 0.6 * math.exp(-0.3 * layer)


def build_program(stop_after=None, debug=False):
    nc = bass.Bass("TRN2", target_bir_lowering=False)

    def din(name, shape, dt=F32):
        return nc.dram_tensor(name, list(shape), dt, kind="ExternalInput").ap()

    xT = din("xT", [128, NCH, S])
    ppT = din("ppT", [4, 128, 2, S])
    vecs_d = din("vecs", [128, NV * 8])
    wgu_d = din("wgu", [8, NG, 128, 2 * 8 * 128 * GH])
    wo_d = din("wo", [8, NG, 128, GH * D])
    plg_d = din("plg", [4, 8, 128, 8 * 128])
    plp_d = din("plp", [4, 8, 128, 2 * 128])
    cwin_d = din("cwin", [2, 8, 128, 2 * 8 * 128])
    cwout_d = din("cwout", [2, 8, 128, 8 * 128])
    wk_d = din("wk", [8, 128, 8 * 128])
    wv_d = din("wv", [2, 128, 8 * 512])
    wq_d = din("wq", [2, 8, 128, 8 * 128])
    wao_d = din("wao", [2, 8, 128, D])
    lv_d = din("lv", [128, 2 * 4 * 64])
    subln_d = din("subln", [128, 2])
    relb_d = din("relb", [32, 8])
    onehot_d = din("onehot", [32, 384])
    maskrow_d = din("maskrow", [1, 384])
    outT = nc.dram_tensor("outT", [128, NCH, S], F32, kind="ExternalOutput").ap()
    dbg = nc.dram_tensor("dbg", [16, 128, NCH, 16, 32], F32, kind="ExternalOutput").ap() if debug else None
    kT_dram = nc.dram_tensor("kT_scr", [NHEAD, 128, S], BF16, kind="Internal").ap()
    v_dram = nc.dram_tensor("v_scr", [NHEAD, 128, 16, 128], BF16, kind="Internal").ap()
    g_dram = nc.dram_tensor("g_scr", [NHEAD, 128, 384], F32, kind="Internal").ap()

    with ExitStack() as stack:
        sc = Sched(nc, stack)
        E = sc.emit

        def sb(name, shape, dt):
            return stack.enter_context(nc.sbuf_tensor("s_" + name, list(shape), dt))

        h = sb("h", [128, NCH, S], F32)
        xn = sb("xn", [128, NCH, S], BF16)
        vecs = sb("vecs", [128, NV * 8], F32)
        ones_bf = sb("ones_bf", [128, 128], BF16)
        ident_bf = sb("ident_bf", [128, 128], BF16)
        wgu = [sb(f"wgu{i}", [128, 2, 8, 128 * GH], BF16) for i in range(2)]
        wo = [sb(f"wo{i}", [128, GH, D], BF16) for i in range(2)]
        actb = [sb(f"actb{i}", [128, GH, TT], BF16) for i in range(2)]
        sq = [sb(f"sq{i}", [128, NCH, TT], BF16) for i in range(2)]
        NSCR = 8
        scr = [sb(f"scr{i}", [128, TT], F32) for i in range(NSCR)]
        wch = [sb(f"wch{i}", [128, 2, 8, 128], BF16) for i in range(NWCH)]
        ps = stack.enter_context(nc.psum_tensor("ps", [128, 8, TT], F32))

        R_h = [[Res(f"h{c}_{t}") for t in range(NT)] for c in range(NCH)]
        R_xn = [Res(f"xn{t}") for t in range(NT)]
        R_vecs = Res("vecs")
        R_const = Res("const")
        R_wgu = [Res(f"wgu{i}") for i in range(2)]
        R_wo = [Res(f"wo{i}") for i in range(2)]
        R_act = [Res(f"act{i}") for i in range(2)]
        R_sq = [Res(f"sq{i}") for i in range(2)]
        R_scr = [Res(f"scr{i}") for i in range(NSCR)]
        R_wch = [Res(f"wch{i}") for i in range(NWCH)]
        R_ps = [Res(f"ps{i}", excl=True) for i in range(8)]
        rot = {}

        def nxt(key, n):
            rot[key] = (rot.get(key, -1) + 1) % n
            return rot[key]

        STAT = {"banks": [6, 7]}

        def statbank():
            b = STAT["banks"]
            return b[nxt("statbank", 2) % len(b)]

        def vcol(v, c):
            return vecs[:, v * 8 + c: v * 8 + c + 1]

        def tsl(t):
            return slice(t * TT, (t + 1) * TT)

        sc.dma("sp", vecs[:, :], vecs_d[:, :], [], [R_vecs])
        for t in range(NT):
            sc.dma("sp", h[:, :, tsl(t)], xT[:, :, tsl(t)], [], [R_h[c][t] for c in range(NCH)],
                   semres=Res(f"xload{t}"))
        E("pool", lambda e: e.memset(ones_bf[:, :], 1.0), [], [R_const])
        E("pool", lambda e: e.memset(ident_bf[:, :], 0.0), [], [R_const])
        E("pool", lambda e: e.affine_select(out=ident_bf[:, :], in_=ident_bf[:, :], pattern=[[-1, 128]],
                                            compare_op=ALU.not_equal, fill=1.0, base=0, channel_multiplier=1),
          [], [R_const])

        def stat_matmuls(bank, src_of_c, reads, nchunks=NCH):
            for c in range(nchunks):
                E("pe", lambda e, c=c: e.matmul(ps[:, bank, :], ones_bf[:, :], src_of_c(c),
                                                start=(c == 0), stop=(c == nchunks - 1)),
                  reads + [R_const], [R_ps[bank]], inc=(c == nchunks - 1))

        def rstd_from(bank_ap, bank_res, inv_n, eps, out_i):
            E("act", lambda e: e.activation(out=scr[out_i][:, :], in_=bank_ap, func=AF.Ln, bias=eps, scale=inv_n),
              [bank_res], [R_scr[out_i]])
            E("act", lambda e: e.activation(out=scr[out_i][:, :], in_=scr[out_i][:, :], func=AF.Exp, scale=-0.5),
              [R_scr[out_i]], [R_scr[out_i]])

        def rms_phase(vidx, dst_of=None):
            for t in range(NT):
                si = nxt("sq", 2)
                E("act", lambda e: e.activation(out=sq[si][:, :, :], in_=h[:, :, tsl(t)], func=AF.Square),
                  [R_h[c][t] for c in range(NCH)], [R_sq[si]])
                bank = statbank()
                stat_matmuls(bank, lambda c: sq[si][:, c, :], [R_sq[si]])
                ri = nxt("rstd", 2)
                rstd_from(ps[:, bank, :], R_ps[bank], 1.0 / D, RMS_EPS, ri)
                for c in range(NCH):
                    if dst_of is None:
                        E("dve", lambda e, c=c: e.scalar_tensor_tensor(
                            out=xn[:, c, tsl(t)], in0=h[:, c, tsl(t)], scalar=vcol(vidx, c), in1=scr[ri][:, :],
                            op0=ALU.mult, op1=ALU.mult),
                          [R_h[c][t], R_scr[ri], R_vecs], [R_xn[t]])
                    else:
                        dst_of(t, c, ri)

        PRE = {"vidx": None}

        def start_rms(vidx):
            if PRE["vidx"] == vidx:
                PRE["vidx"] = None
                return
            PRE["vidx"] = None
            rms_phase(vidx)

        class RmsHook:
            def __init__(self, vidx, slots=(0, 1)):
                self.vidx, self.slots, self.prev, self.n = vidx, slots, None, 0

            def A(self, t):
                si = self.slots[self.n % len(self.slots)]
                self.n += 1
                E("act", lambda e: e.activation(out=sq[si][:, :, :], in_=h[:, :, tsl(t)], func=AF.Square),
                  [R_h[c][t] for c in range(NCH)], [R_sq[si]])
                return si

            def B(self, t, si):
                bank = statbank()
                stat_matmuls(bank, lambda c: sq[si][:, c, :], [R_sq[si]])
                ri = nxt("rstd", 2)
                rstd_from(ps[:, bank, :], R_ps[bank], 1.0 / D, RMS_EPS, ri)
                for c in range(NCH):
                    E("dve", lambda e, c=c: e.scalar_tensor_tensor(
                        out=xn[:, c, tsl(t)], in0=h[:, c, tsl(t)], scalar=vcol(self.vidx, c), in1=scr[ri][:, :],
                        op0=ALU.mult, op1=ALU.mult),
                      [R_h[c][t], R_scr[ri], R_vecs], [R_xn[t]])

            def __call__(self, t):
                if len(self.slots) == 1 and self.prev is not None:
                    self.B(*self.prev)
                    self.prev = None
                si = self.A(t)
                if self.prev is not None:
                    self.B(*self.prev)
                self.prev = (t, si)

            def flush(self):
                if self.prev is not None:
                    self.B(*self.prev)
                self.prev = None
                PRE["vidx"] = self.vidx

        def ffn_phase(f, vidx, hook=None):
            start_rms(vidx)
            units = [(J, t) for J in range(NG) for t in range(NT)]
            slot_of = {}

            def load(J):
                s = nxt("ffnw", 2)
                slot_of[J] = s
                sc.dma("pool", wgu[s][:, :, :, :].rearrange("p a k c -> p (a k c)").rearrange("p (a b) -> p a b", b=2048),
                       wgu_d[f, J].rearrange("p (a b) -> p a b", b=2048), [], [R_wgu[s]])
                sc.dma("pool", wo[s][:, :, :].rearrange("p a c -> p (a c)").rearrange("p (a b) -> p a b", b=1024),
                       wo_d[f, J].rearrange("p (a b) -> p a b", b=1024), [], [R_wo[s]])

            def gu(n, jjs):
                J, t = units[n]
                s = slot_of[J]
                ab = n % 2
                for jj in jjs:
                    gb = nxt("gbank", 2)
                    ub = 2 + nxt("ubank", 2)
                    for k in range(8):
                        E("pe", lambda e, k=k: e.matmul(ps[:, gb, :], wgu[s][:, 0, k, jj * 128:(jj + 1) * 128],
                                                        xn[:, k, tsl(t)], start=(k == 0), stop=(k == 7)),
                          [R_wgu[s], R_xn[t]], [R_ps[gb]], inc=(k == 7))
                    for k in range(8):
                        E("pe", lambda e, k=k: e.matmul(ps[:, ub, :], wgu[s][:, 1, k, jj * 128:(jj + 1) * 128],
                                                        xn[:, k, tsl(t)], start=(k == 0), stop=(k == 7)),
                          [R_wgu[s], R_xn[t]], [R_ps[ub]], inc=(k == 7))
                    si = 2 + nxt("sgb", 2)
                    E("act", lambda e: e.activation(out=scr[si][:, :], in_=ps[:, gb, :], func=AF.Silu),
                      [R_ps[gb]], [R_scr[si]])
                    E("dve", lambda e: e.tensor_tensor(out=actb[ab][:, jj, :], in0=ps[:, ub, :], in1=scr[si][:, :],
                                                       op=ALU.mult),
                      [R_ps[ub], R_scr[si]], [R_act[ab]])

            def oacc(n, ms):
                J, t = units[n]
                s = slot_of[J]
                ab = n % 2
                for m in ms:
                    ob = 4 + nxt("obank4", 4)
                    for jj in range(GH):
                        E("pe", lambda e, jj=jj: e.matmul(ps[:, ob, :], wo[s][:, jj, m * 128:(m + 1) * 128],
                                                          actb[ab][:, jj, :], start=(jj == 0), stop=(jj == GH - 1)),
                          [R_wo[s], R_act[ab]], [R_ps[ob]], inc=(jj == GH - 1))
                    E("dve", lambda e: e.scalar_tensor_tensor(out=h[:, m, tsl(t)], in0=ps[:, ob, :], scalar=0.5,
                                                              in1=h[:, m, tsl(t)], op0=ALU.mult, op1=ALU.add),
                      [R_ps[ob], R_h[m][t]], [R_h[m][t]])

            load(0)
            mper = NCH // GH
            for n in range(len(units) + 1):
                for jj in range(GH):
                    if n < len(units):
                        gu(n, [jj])
                    if n >= 1:
                        oacc(n - 1, range(jj * mper, (jj + 1) * mper))
                        if hook is not None and jj == GH - 1 and units[n - 1][0] == NG - 1:
                            hook(units[n - 1][1])
                if n < len(units):
                    J, t = units[n]
                    if t == 0 and J + 1 < NG:
                        load(J + 1)
            if hook is not None:
                hook.flush()

        def ple_phase(l, hook=None):
            start_rms(4 * l + 3)
            ppb = sq[1][:, :, :].rearrange("p a b -> p (a b)").rearrange("p (k t) -> p k t", k=2)
            R_ppb = R_sq[1]
            sc.dma("pool", sq[1][:, :, :].rearrange("p a b -> p (a b)").rearrange("p (a b) -> p a b", b=1024),
                   ppT[l].rearrange("p k t -> p (k t)").rearrange("p (a b) -> p a b", b=1024), [], [R_ppb])
            for m in range(NCH):
                wi = nxt("wch", NWCH)
                sc.dma("pool", wch[wi][:, 0, :, :].rearrange("p k c -> p (k c)"), plg_d[l, m], [], [R_wch[wi]])
                sc.dma("pool", wch[wi][:, 1, 0:2, :].rearrange("p k c -> p (k c)"), plp_d[l, m], [], [R_wch[wi]])
                for t in range(NT):
                    gb = nxt("gbank", 2)
                    ub = 2 + nxt("ubank", 2)
                    for k in range(8):
                        E("pe", lambda e, k=k: e.matmul(ps[:, gb, :], wch[wi][:, 0, k, :], xn[:, k, tsl(t)],
                                                        start=(k == 0), stop=(k == 7)),
                          [R_wch[wi], R_xn[t]], [R_ps[gb]], inc=(k == 7))
                    for k in range(2):
                        E("pe", lambda e, k=k: e.matmul(ps[:, ub, :], wch[wi][:, 1, k, :], ppb[:, k, tsl(t)],
                                                        start=(k == 0), stop=(k == 1)),
                          [R_wch[wi], R_ppb], [R_ps[ub]], inc=(k == 1))
                    si = 2 + nxt("sgb", 2)
                    E("act", lambda e: e.activation(out=scr[si][:, :], in_=ps[:, gb, :], func=AF.Sigmoid),
                      [R_ps[gb]], [R_scr[si]])
                    E("dve", lambda e: e.tensor_tensor(out=scr[si][:, :], in0=ps[:, ub, :], in1=scr[si][:, :],
                                                       op=ALU.mult),
                      [R_ps[ub], R_scr[si]], [R_scr[si]])
                    E("dve", lambda e: e.tensor_tensor(out=h[:, m, tsl(t)], in0=h[:, m, tsl(t)],
                                                       in1=scr[si][:, :], op=ALU.add),
                      [R_scr[si], R_h[m][t]], [R_h[m][t]])
                    if hook is not None and m == NCH - 1:
                        hook(t)
            if hook is not None:
                hook.flush()

        def conv_phase(l, hook=None):
            vb = 18 + 37 * l
            start_rms(4 * l + 1)
            with ExitStack() as st2:
                ubuf = st2.enter_context(nc.sbuf_tensor("s_" + f"ubuf{l}", [128, NCH, PAD + TT], BF16))
                yb = st2.enter_context(nc.sbuf_tensor("s_" + f"yb{l}", [128, NCH, TT], F32))
                dg0 = st2.enter_context(nc.sbuf_tensor("s_" + f"dg{l}_0", [128, CONVW, 128], BF16))
                dg1 = wgu[1][:, :, :, :].rearrange("p a k c -> p (a k c)")[:, 0:CONVW * 128].rearrange(
                    "p (j q) -> p j q", q=128)
                dg = [dg0, dg1]
                R_dg = [Res("dg0"), R_wgu[1]]
                R_ub = [Res(f"ubuf{c}") for c in range(NCH)]
                R_yb = [Res(f"yb{c}") for c in range(NCH)]
                E("pool", lambda e: e.memset(ubuf[:, :, 0:PAD], 0.0), [], R_ub)
                MU, MSQ, RS, NM = 4, 5, 6, 7

                def ag_chunk(t, c):
                    wi = nxt("wch", NWCH)
                    sc.dma("pool", wch[wi][:, :, :, :].rearrange("p a k c -> p (a k c)"), cwin_d[l, c],
                           [], [R_wch[wi]])
                    ab_ = nxt("gbank", 2)
                    gb_ = 2 + nxt("ubank", 2)
                    for k in range(8):
                        E("pe", lambda e, k=k: e.matmul(ps[:, ab_, :], wch[wi][:, 0, k, :], xn[:, k, tsl(t)],
                                                        start=(k == 0), stop=(k == 7)),
                          [R_wch[wi], R_xn[t]], [R_ps[ab_]], inc=(k == 7))
                    for k in range(8):
                        E("pe", lambda e, k=k: e.matmul(ps[:, gb_, :], wch[wi][:, 1, k, :], xn[:, k, tsl(t)],
                                                        start=(k == 0), stop=(k == 7)),
                          [R_wch[wi], R_xn[t]], [R_ps[gb_]], inc=(k == 7))
                    si = 2 + nxt("sgb", 2)
                    E("act", lambda e: e.activation(out=scr[si][:, :], in_=ps[:, gb_, :], func=AF.Sigmoid,
                                                    bias=vcol(vb + 1, c)),
                      [R_ps[gb_], R_vecs], [R_scr[si]])
                    if t > 0:
                        E("pool", lambda e: e.tensor_copy(out=ubuf[:, c, 0:PAD], in_=ubuf[:, c, TT:TT + PAD]),
                          [R_ub[c]], [R_ub[c]])
                    E("dve", lambda e: e.scalar_tensor_tensor(out=ubuf[:, c, PAD:PAD + TT], in0=ps[:, ab_, :],
                                                              scalar=vcol(vb, c), in1=scr[si][:, :],
                                                              op0=ALU.add, op1=ALU.mult),
                      [R_ps[ab_], R_scr[si], R_vecs], [R_ub[c]])

                def build_dg(c):
                    di = c % 2
                    for j in range(CONVW):
                        E("dve", lambda e, j=j: e.tensor_scalar(out=dg[di][:, j, :], in0=ident_bf[:, :],
                                                                scalar1=vcol(vb + 6 + j, c), scalar2=None,
                                                                op0=ALU.mult),
                          [R_const, R_vecs], [R_dg[di]], inc=(j == CONVW - 1))

                def taps(t):
                    build_dg(0)
                    for c in range(NCH):
                        di = c % 2
                        if c + 1 < NCH:
                            build_dg(c + 1)
                        yb_ = 4 + nxt("obank", 2)
                        for j in range(CONVW):
                            E("pe", lambda e, j=j: e.matmul(ps[:, yb_, :], dg[di][:, j, :], ubuf[:, c, j:j + TT],
                                                            start=(j == 0), stop=(j == CONVW - 1)),
                              [R_dg[di], R_ub[c]], [R_ps[yb_]], inc=(j == CONVW - 1))
                        E("act", lambda e: e.activation(out=yb[:, c, :], in_=ps[:, yb_, :], func=AF.Identity,
                                                        bias=vcol(vb + 2, c)),
                          [R_ps[yb_], R_vecs], [R_yb[c]])

                def lnstats(t):
                    si_ = nxt("sq", 2)
                    E("act", lambda e: e.activation(out=sq[si_][:, :, :], in_=yb[:, :, :], func=AF.Square),
                      R_yb, [R_sq[si_]])
                    E("dve", lambda e: e.tensor_copy(out=xn[:, :, tsl(t)], in_=yb[:, :, :]), R_yb, [R_xn[t]])
                    b1 = statbank()
                    stat_matmuls(b1, lambda c: xn[:, c, tsl(t)], [R_xn[t]])
                    b2 = statbank()
                    stat_matmuls(b2, lambda c: sq[si_][:, c, :], [R_sq[si_]])
                    E("dve", lambda e: e.tensor_scalar(out=scr[MU][:, :], in0=ps[:, b1, :], scalar1=1.0 / D,
                                                       scalar2=None, op0=ALU.mult), [R_ps[b1]], [R_scr[MU]])
                    E("dve", lambda e: e.tensor_tensor(out=scr[MSQ][:, :], in0=scr[MU][:, :], in1=scr[MU][:, :],
                                                       op=ALU.mult), [R_scr[MU]], [R_scr[MSQ]])
                    E("dve", lambda e: e.scalar_tensor_tensor(out=scr[MSQ][:, :], in0=ps[:, b2, :], scalar=1.0 / D,
                                                              in1=scr[MSQ][:, :], op0=ALU.mult, op1=ALU.subtract),
                      [R_ps[b2], R_scr[MSQ]], [R_scr[MSQ]])
                    rstd_from(scr[MSQ][:, :], R_scr[MSQ], 1.0, LN_EPS, RS)
                    E("dve", lambda e: e.scalar_tensor_tensor(out=scr[NM][:, :], in0=scr[MU][:, :], scalar=-1.0,
                                                              in1=scr[RS][:, :], op0=ALU.mult, op1=ALU.mult),
                      [R_scr[MU], R_scr[RS]], [R_scr[NM]])

                def lnnorm(t, c):
                    zi = nxt("z", 2)
                    E("dve", lambda e: e.tensor_tensor(out=scr[zi][:, :], in0=yb[:, c, :], in1=scr[RS][:, :],
                                                       op=ALU.mult), [R_yb[c], R_scr[RS]], [R_scr[zi]])
                    E("dve", lambda e: e.tensor_tensor(out=scr[zi][:, :], in0=scr[zi][:, :], in1=scr[NM][:, :],
                                                       op=ALU.add), [R_scr[zi], R_scr[NM]], [R_scr[zi]])
                    E("act", lambda e: e.activation(out=xn[:, c, tsl(t)], in_=scr[zi][:, :], func=AF.Silu,
                                                    bias=vcol(vb + 4, c), scale=vcol(vb + 3, c)),
                      [R_scr[zi], R_vecs], [R_xn[t]])

                def outproj(t):
                    for m in range(NCH):
                        wi = nxt("wch", NWCH)
                        sc.dma("pool", wch[wi][:, 0, :, :].rearrange("p k c -> p (k c)"), cwout_d[l, m],
                               [], [R_wch[wi]])
                        ob = 4 + nxt("obank", 2)
                        for k in range(8):
                            E("pe", lambda e, k=k: e.matmul(ps[:, ob, :], wch[wi][:, 0, k, :], xn[:, k, tsl(t)],
                                                            start=(k == 0), stop=(k == 7)),
                              [R_wch[wi], R_xn[t]], [R_ps[ob]], inc=(k == 7))
                        E("dve", lambda e: e.scalar_tensor_tensor(out=h[:, m, tsl(t)], in0=ps[:, ob, :],
                                                                  scalar=vcol(vb + 5, m), in1=h[:, m, tsl(t)],
                                                                  op0=ALU.add, op1=ALU.add),
                          [R_ps[ob], R_h[m][t], R_vecs], [R_h[m][t]])

                for c in range(NCH):
                    ag_chunk(0, c)
                for t in range(NT):
                    taps(t)
                    lnstats(t)
                    for c in range(NCH):
                        lnnorm(t, c)
                        if t + 1 < NT:
                            ag_chunk(t + 1, c)
                    outproj(t)
                    if hook is not None:
                        hook(t)
                if hook is not None:
                    hook.flush()
                sc.barrier()

        def kv_phase():
            start_rms(16)
            with ExitStack() as st2:
                kst = [st2.enter_context(nc.sbuf_tensor("s_" + f"kst{i}", [128, S], BF16)) for i in range(2)]
                vst = [st2.enter_context(nc.sbuf_tensor("s_" + f"vst{i}", [128, D], BF16)) for i in range(2)]
                wvb = [st2.enter_context(nc.sbuf_tensor("s_" + f"wvb{i}", [128, 8, 512], BF16)) for i in range(2)]
                R_kst = [Res("kst0"), Res("kst1")]
                R_vst = [Res("vst0"), Res("vst1")]
                R_wvb = [Res("wvb0"), Res("wvb1")]
                for hh in range(NHEAD):
                    wi = nxt("wch", NWCH)
                    sc.dma("pool", wch[wi][:, 0, :, :].rearrange("p k c -> p (k c)"), wk_d[hh], [], [R_wch[wi]])
                    ks = nxt("kst", 2)
                    for t in range(NT):
                        bk = nxt("gbank", 2)
                        for k in range(8):
                            E("pe", lambda e, k=k: e.matmul(ps[:, bk, :], wch[wi][:, 0, k, :], xn[:, k, tsl(t)],
                                                            start=(k == 0), stop=(k == 7)),
                              [R_wch[wi], R_xn[t]], [R_ps[bk]], inc=(k == 7))
                        E("act", lambda e: e.activation(out=kst[ks][:, tsl(t)], in_=ps[:, bk, :], func=AF.Identity),
                          [R_ps[bk]], [R_kst[ks]])
                    sc.dma("sp", kT_dram[hh], kst[ks][:, :], [R_kst[ks]], [R_kv], semres=R_kstout[ks])
                for half in range(2):
                    sc.dma("pool", wvb[half][:, :, :].rearrange("p k c -> p (k c)").rearrange("p (a b) -> p a b", b=2048),
                           wv_d[half].rearrange("p (a b) -> p a b", b=2048), [], [R_wvb[half]])
                for kt in range(16):
                    vs = nxt("vst", 2)
                    for half in range(2):
                        bk = 2 + nxt("ubank", 2)
                        for k in range(8):
                            E("pe", lambda e, k=k: e.matmul(ps[:, bk, :], xn[:, k, kt * 128:(kt + 1) * 128],
                                                            wvb[half][:, k, :], start=(k == 0), stop=(k == 7)),
                              [R_wvb[half], R_xn[kt // 4]], [R_ps[bk]], inc=(k == 7))
                        E("dve", lambda e: e.tensor_copy(out=vst[vs][:, half * 512:(half + 1) * 512], in_=ps[:, bk, :]),
                          [R_ps[bk]], [R_vst[vs]])
                    sc.dma("sp", v_dram[:, :, kt, :].rearrange("h p e -> p h e"),
                           vst[vs][:, :].rearrange("p (h e) -> p h e", e=128), [R_vst[vs]], [R_kv],
                           semres=R_vstout[vs])
                sc.barrier()

        R_kv = Res("kvdram")
        R_kstout = [Res("kstout0"), Res("kstout1")]
        R_vstout = [Res("vstout0"), Res("vstout1")]

        def attn_phase(layer, j_, A, hook=None):
            lam_init = lambda_init_of(layer)
            start_rms(4 * layer + 1)
            if WARM_JUNK:
                STAT["banks"] = [6]
                E("pe", lambda e: e.matmul(ps[:, 7, :], ones_bf[:, :], xn[:, 0, 0:TT], start=True, stop=True),
                  [R_const], [R_ps[7]])
            kTh, vh, qT, pT, wob = A["kTh"], A["vh"], A["qT"], A["pT"], A["wob"]
            R_kTh, R_vh, R_qT, R_pT, R_wob = A["R_kTh"], A["R_vh"], A["R_qT"], A["R_pT"], A["R_wob"]
            small, R_small = A["small"], A["R_small"]
            lvb = A["lvb"]
            base = j_ * 256
            for i2 in range(2):
                E("dve", lambda e, i2=i2: e.tensor_tensor(out=A["ltmp"][:, :], in0=lvb[:, base + i2 * 128: base + i2 * 128 + 64],
                                                          in1=lvb[:, base + i2 * 128 + 64: base + i2 * 128 + 128],
                                                          op=ALU.mult), [A["R_lvb"]], [A["R_ltmp"]])
                E("dve", lambda e, i2=i2: e.tensor_reduce(out=small[:, i2:i2 + 1], in_=A["ltmp"][:, :],
                                                          axis=mybir.AxisListType.X, op=ALU.add),
                  [A["R_ltmp"]], [R_small])
            E("act", lambda e: e.activation(out=small[:, 0:2], in_=small[:, 0:2], func=AF.Exp), [R_small], [R_small])
            E("dve", lambda e: e.scalar_tensor_tensor(out=small[:, 2:3], in0=small[:, 1:2], scalar=-lam_init,
                                                      in1=small[:, 0:1], op0=ALU.add, op1=ALU.subtract),
              [R_small], [R_small])
            E("dve", lambda e: e.tensor_scalar(out=small[:, 3:4], in0=A["sublnb"][:, j_:j_ + 1],
                                               scalar1=1.0 - lam_init, scalar2=None, op0=ALU.mult),
              [A["R_sublnb"]], [R_small])
            biasT, b31 = A["biasT"], A["b31"]
            R_bias = A["R_bias"]
            deferred = []
            for hh in range(NHEAD):
                hs = 0
                ws = hh % 2
                wi = nxt("wch", NWCH)
                sc.dma("pool", wch[wi][:, 0, :, :].rearrange("p k c -> p (k c)"), wq_d[j_, hh], [], [R_wch[wi]])
                sc.dma("pool", wob[ws][:, :], wao_d[j_, hh], [], [R_wob[ws]])
                sc.dma("sp", kTh[hs][:, :], kT_dram[hh], [R_kv], [R_kTh[hs]])
                sc.dma("sp", vh[hs][:, :, :], v_dram[hh], [R_kv], [R_vh[hs]])
                for t in range(NT):
                    bk = statbank()
                    for k in range(8):
                        E("pe", lambda e, k=k: e.matmul(ps[:, bk, :], wch[wi][:, 0, k, :], xn[:, k, tsl(t)],
                                                        start=(k == 0), stop=(k == 7)),
                          [R_wch[wi], R_xn[t]], [R_ps[bk]], inc=(k == 7))
                    E("dve", lambda e: e.tensor_copy(out=qT[hs][:, tsl(t)], in_=ps[:, bk, :]),
                      [R_ps[bk]], [R_qT[hs]])
                for qi in range(NT):
                    nk = 4 * (qi + 1)
                    NUM = [2, 3]
                    DEN = [4, 5]
                    SB = [0, 6]

                    def S(kj, qi=qi, hh=hh):
                        r = kj - 4 * qi
                        c0 = max(r, 0)
                        qs = slice(qi * TT + c0 * 128, (qi + 1) * TT)
                        cs = slice(c0 * 128, TT)
                        pi = kj % 2
                        for mp in range(2):
                            sbk = SB[pi] + mp
                            rows = slice(mp * 64, (mp + 1) * 64)
                            E("pe", lambda e: e.matmul(ps[:, sbk, cs], kTh[hs][rows, kj * 128:(kj + 1) * 128],
                                                       qT[hs][rows, qs], start=True, stop=True),
                              [R_kTh[hs], R_qT[hs]], [R_ps[sbk]])
                            pr = R_pT[pi][mp]
                            pt = pT[pi][mp]
                            if r >= -1:
                                lo, hi_ = c0, min(r + 1, 3)
                                ncol = (hi_ - lo + 1) * 128
                                d0 = lo - r
                                ti = 2 + nxt("sgb", 2)
                                E("dve", lambda e: e.scalar_tensor_tensor(
                                    out=scr[ti][:, 0:ncol], in0=ps[:, sbk, lo * 128:lo * 128 + ncol], scalar=0.125,
                                    in1=biasT[:, hh, d0 * 128:d0 * 128 + ncol], op0=ALU.mult, op1=ALU.add),
                                  [R_ps[sbk], R_bias], [R_scr[ti]])
                            cst = max(r + 2, 0)
                            if cst < 4:
                                cc = slice(cst * 128, TT)
                                E("act", lambda e: e.activation(out=pt[:, cc], in_=ps[:, sbk, cc], func=AF.Exp,
                                                                bias=b31[:, hh:hh + 1], scale=0.125),
                                  [R_ps[sbk], R_bias], [pr])
                            if r >= -1:
                                E("act", lambda e: e.activation(out=pt[:, lo * 128:lo * 128 + ncol],
                                                                in_=scr[ti][:, 0:ncol], func=AF.Exp),
                                  [R_scr[ti]], [pr])

                    def PV(kj, qi=qi, nk=nk):
                        r = kj - 4 * qi
                        c0 = max(r, 0)
                        cs = slice(c0 * 128, TT)
                        pi = kj % 2
                        for mp in range(2):
                            pr = R_pT[pi][mp]
                            pt = pT[pi][mp]
                            E("pe", lambda e: e.matmul(ps[:, NUM[mp], cs], vh[hs][:, kj, :], pt[:, cs],
                                                       start=(kj == 0), stop=(kj == nk - 1)),
                              [R_vh[hs], pr], [R_ps[NUM[mp]]], inc=(kj == nk - 1))
                            E("pe", lambda e: e.matmul(ps[:, DEN[mp], cs], ones_bf[:, :], pt[:, cs],
                                                       start=(kj == 0), stop=(kj == nk - 1)),
                              [R_const, pr], [R_ps[DEN[mp]]], inc=True)

                    def make_fin(hh=hh, qi=qi, ws=ws):
                        R1, R2, O1, T2 = 4, 5, 6, 7
                        si = nxt("sq", 2)

                        def fin1():
                            for mp, ri in ((0, R1), (1, R2)):
                                E("act", lambda e, mp=mp, ri=ri: e.activation(out=scr[ri][:, :], in_=ps[:, DEN[mp], :],
                                                                              func=AF.Ln), [R_ps[DEN[mp]]], [R_scr[ri]])
                                E("act", lambda e, ri=ri: e.activation(out=scr[ri][:, :], in_=scr[ri][:, :], func=AF.Exp,
                                                                       scale=-1.0), [R_scr[ri]], [R_scr[ri]])
                            E("dve", lambda e: e.tensor_tensor(out=scr[O1][:, :], in0=ps[:, NUM[0], :], in1=scr[R1][:, :],
                                                               op=ALU.mult), [R_ps[NUM[0]], R_scr[R1]], [R_scr[O1]])
                            E("dve", lambda e: e.tensor_tensor(out=scr[T2][:, :], in0=ps[:, NUM[1], :], in1=scr[R2][:, :],
                                                               op=ALU.mult), [R_ps[NUM[1]], R_scr[R2]], [R_scr[T2]])
                            E("dve", lambda e: e.scalar_tensor_tensor(out=scr[O1][:, :], in0=scr[T2][:, :],
                                                                      scalar=small[:, 2:3], in1=scr[O1][:, :],
                                                                      op0=ALU.mult, op1=ALU.add),
                              [R_scr[T2], R_scr[O1], R_small], [R_scr[O1]])
                            E("act", lambda e: e.activation(out=sq[si][:, 0, :], in_=scr[O1][:, :], func=AF.Square),
                              [R_scr[O1]], [R_sq[si]])

                        def fin2():
                            bk = statbank()
                            stat_matmuls(bk, lambda c: sq[si][:, 0, :], [R_sq[si]], nchunks=1)
                            rstd_from(ps[:, bk, :], R_ps[bk], 1.0 / 128, RMS_EPS, T2)
                            E("dve", lambda e: e.scalar_tensor_tensor(out=sq[si][:, 1, :], in0=scr[O1][:, :],
                                                                      scalar=small[:, 3:4], in1=scr[T2][:, :],
                                                                      op0=ALU.mult, op1=ALU.mult),
                              [R_scr[O1], R_scr[T2], R_small, R_sq[si]], [R_sq[si]])

                        def fin3(ms, last):
                            for m in ms:
                                ob = statbank()
                                E("pe", lambda e: e.matmul(ps[:, ob, :], wob[ws][:, m * 128:(m + 1) * 128],
                                                           sq[si][:, 1, :], start=True, stop=True),
                                  [R_wob[ws], R_sq[si]], [R_ps[ob]])
                                E("dve", lambda e: e.tensor_tensor(out=h[:, m, tsl(qi)], in0=ps[:, ob, :],
                                                                   in1=h[:, m, tsl(qi)], op=ALU.add),
                                  [R_ps[ob], R_h[m][qi]], [R_h[m][qi]])
                            if last and hook is not None and hh == NHEAD - 1:
                                hook(qi)

                        return fin1, [fin2, lambda: fin3(range(0, 4), False), lambda: fin3(range(4, 8), True)]

                    S(0)
                    for kj in range(nk):
                        if PIPELINE_ATTN and kj + 1 < nk:
                            S(kj + 1)
                        PV(kj)
                        if not PIPELINE_ATTN and kj + 1 < nk:
                            S(kj + 1)
                        if DEFER_FIN and deferred and kj >= 1:
                            deferred.pop(0)()
                    while deferred:
                        deferred.pop(0)()
                    f1, rest = make_fin()
                    f1()
                    deferred.extend(rest)
                    if not DEFER_FIN:
                        while deferred:
                            deferred.pop(0)()
            while deferred:
                deferred.pop(0)()
            if hook is not None:
                hook.flush()
            STAT["banks"] = [6, 7]

        def attn_setup(st2):
            A = {}
            A["kTh"] = [st2.enter_context(nc.sbuf_tensor("s_" + f"kTh{i}", [128, S], BF16)) for i in range(1)]
            A["vh"] = [st2.enter_context(nc.sbuf_tensor("s_" + f"vh{i}", [128, 16, 128], BF16)) for i in range(1)]
            A["qT"] = [st2.enter_context(nc.sbuf_tensor("s_" + f"qT{i}", [128, S], BF16)) for i in range(1)]
            A["pT"] = [[st2.enter_context(nc.sbuf_tensor("s_" + f"pT{i}_{m}", [128, TT], BF16)) for m in range(2)]
                       for i in range(2)]
            A["wob"] = [st2.enter_context(nc.sbuf_tensor("s_" + f"wob{i}", [128, D], BF16)) for i in range(2)]
            A["small"] = st2.enter_context(nc.sbuf_tensor("s_small", [128, 8], F32))
            A["ltmp"] = st2.enter_context(nc.sbuf_tensor("s_ltmp", [128, 64], F32))
            A["lvb"] = st2.enter_context(nc.sbuf_tensor("s_lvb", [128, 512], F32))
            A["sublnb"] = st2.enter_context(nc.sbuf_tensor("s_sublnb", [128, 2], F32))
            A["biasT"] = st2.enter_context(nc.sbuf_tensor("s_biasT", [128, NHEAD, 256], F32))
            A["b31"] = st2.enter_context(nc.sbuf_tensor("s_b31", [128, NHEAD], F32))
            gsb = scr[0][:, 0:384]
            oneh = scr[1][0:32, 0:384]
            maskr = scr[2][0:1, 0:384]
            ones1 = scr[3][0:1, 0:128]
            ones32 = scr[4][0:32, 0:128]
            lhs = scr[5][0:32, 0:128]
            relb = scr[6][0:32, 0:8]
            for k in ("kTh", "vh", "qT", "wob"):
                A["R_" + k] = [Res(k + "0"), Res(k + "1")]
            A["R_pT"] = [[Res(f"pT{i}_{m}") for m in range(2)] for i in range(2)]
            for k in ("small", "ltmp", "lvb", "sublnb", "bias"):
                A["R_" + k] = Res(k)
            R_gsb, R_oneh, R_maskr, R_ones1, R_ones32, R_lhs, R_relb = (R_scr[i] for i in range(7))
            R_g = Res("gdram")
            sc.dma("sp", A["lvb"][:, :], lv_d[:, :], [], [A["R_lvb"]])
            sc.dma("sp", A["sublnb"][:, :], subln_d[:, :], [], [A["R_sublnb"]])
            sc.dma("sp", relb, relb_d[:, :], [], [R_relb], semres=Res("relb"))
            sc.dma("sp", oneh, onehot_d[:, :], [], [R_oneh], semres=Res("oneh"))
            sc.dma("sp", maskr, maskrow_d[:, :], [], [R_maskr], semres=Res("maskr"))
            E("pool", lambda e: e.memset(ones1, 1.0), [], [R_ones1])
            E("pool", lambda e: e.memset(ones32, 1.0), [], [R_ones32])
            for hh in range(NHEAD):
                E("dve", lambda e: e.tensor_scalar(out=lhs, in0=ones32, scalar1=relb[:, hh:hh + 1],
                                                   scalar2=None, op0=ALU.mult), [R_ones32, R_relb], [R_lhs])
                E("pe", lambda e: e.matmul(ps[:, 0, 0:384], lhs, oneh, start=True, stop=False),
                  [R_lhs, R_oneh], [R_ps[0]], inc=False)
                E("pe", lambda e: e.matmul(ps[:, 0, 0:384], ones1, maskr, start=False, stop=True),
                  [R_ones1, R_maskr], [R_ps[0]])
                E("dve", lambda e: e.tensor_copy(out=gsb, in_=ps[:, 0, 0:384]), [R_ps[0]], [R_gsb])
                E("dve", lambda e: e.tensor_copy(out=A["b31"][:, hh:hh + 1], in_=gsb[:, 383:384]),
                  [R_gsb], [A["R_bias"]])
                sc.dma("sp", g_dram[hh], gsb, [R_gsb], [R_g], semres=Res(f"gout{hh}"))
            for hh in range(NHEAD):
                src = bass.AP(tensor=g_dram.tensor, offset=g_dram[hh].offset + 128, ap=[[383, 128], [1, 256]])
                sc.dma("sp", A["biasT"][:, hh, :], src, [R_g], [A["R_bias"]], semres=Res(f"bt{hh}"))
            return A

        def dump_h():
            for t in range(NT):
                sc.out_tks.append(sc.dma("sp", outT[:, :, tsl(t)], h[:, :, tsl(t)],
                                         [R_h[c][t] for c in range(NCH)], [R_out], semres=Res(f"dump{t}")))

        R_out = Res("out")
        done = [False]

        nph = [0]

        def check(name):
            if debug:
                src = h[:, :, :].rearrange("p c (a b) -> p c a b", b=128)[:, :, :, 0:32]
                sc.out_tks.append(sc.dma("sp", dbg[nph[0]], src, [R_h[c][t] for c in range(NCH) for t in range(NT)],
                                         [R_out], semres=Res(f"dbg{nph[0]}")))
                nph[0] += 1
            if stop_after == name and not done[0]:
                dump_h()
                done[0] = True
            return done[0]

        def forward():
            for l in range(2):
                ffn_phase(2 * l, 4 * l + 0, RmsHook(4 * l + 1))
                if check(f"ffn1_{l}"):
                    return
                conv_phase(l, RmsHook(4 * l + 2))
                if check(f"mix_{l}"):
                    return
                ffn_phase(2 * l + 1, 4 * l + 2, RmsHook(4 * l + 3))
                if check(f"ffn2_{l}"):
                    return
                ple_phase(l, RmsHook(4 * (l + 1) if l == 0 else 16, slots=(0,)))
                if check(f"ple_{l}"):
                    return
            kv_phase()
            with ExitStack() as st2:
                A = attn_setup(st2)
                for l in range(2, 4):
                    ffn_phase(2 * l, 4 * l + 0, RmsHook(4 * l + 1))
                    if check(f"ffn1_{l}"):
                        return
                    attn_phase(l, l - 2, A, RmsHook(4 * l + 2))
                    if check(f"mix_{l}"):
                        return
                    ffn_phase(2 * l + 1, 4 * l + 2, RmsHook(4 * l + 3))
                    if check(f"ffn2_{l}"):
                        return
                    ple_phase(l, RmsHook(4 * (l + 1), slots=(0,)) if l == 2 else None)
                    if check(f"ple_{l}"):
                        return
                sc.barrier()
            with ExitStack() as st2:
                ob = [st2.enter_context(nc.sbuf_tensor("s_" + f"outb{i}", [128, NCH, TT], F32)) for i in range(2)]
                R_ob = [Res("outb0"), Res("outb1")]
                cur = {}

                def dst(t, c, ri):
                    if c == 0:
                        cur["i"] = nxt("outb", 2)
                    oi = cur["i"]
                    E("dve", lambda e: e.scalar_tensor_tensor(out=ob[oi][:, c, :], in0=h[:, c, tsl(t)],
                                                              scalar=vcol(17, c), in1=scr[ri][:, :],
                                                              op0=ALU.mult, op1=ALU.mult),
                      [R_h[c][t], R_scr[ri], R_vecs], [R_ob[oi]])
                    if c == NCH - 1:
                        sc.out_tks.append(sc.dma("sp", outT[:, :, tsl(t)], ob[oi][:, :, :], [R_ob[oi]], [R_out],
                                                 semres=Res(f"fin{t}")))

                rms_phase(17, dst_of=dst)
                sc.barrier()

        forward()
        sc._wait_for("sp", sc.out_tks)
        sc.barrier()
        print("sched counts:", sc.cnt, "nsem", sc.nsem)
    return nc


def _rel_bucket_np(n):
    n = np.maximum(n, 0)
    max_exact = 16
    nf = np.maximum(n, 1).astype(np.float32)
    large = max_exact + (np.log(nf / max_exact) / math.log(128 / max_exact) * (32 - max_exact)).astype(np.int32)
    large = np.minimum(large, 31)
    return np.where(n < max_exact, n, large)


def _bucket_table():
    return _rel_bucket_np(np.arange(0, 256))


def prep_shared(inp):
    f = np.float32
    A = {k: np.asarray(v, dtype=f) for k, v in inp.items()}
    sh = {}
    vl = []
    for l in range(4):
        vl += [A["ffn1_norm"][l], A["mix_norm"][l], A["ffn2_norm"][l], A["ple_norm"][l]]
    vl += [A["kv_norm"], A["final_norm"]]
    for l in range(2):
        vl += [A["conv_b_in"][l][:D], A["conv_b_in"][l][D:], A["conv_b_dw"][l], A["conv_ln_g"][l],
               A["conv_ln_b"][l], A["conv_b_out"][l]]
        vl += [A["conv_w_dw"][l][j] for j in range(CONVW)]
    V = np.stack(vl, 0)
    assert V.shape[0] == NV
    sh["vecs"] = np.ascontiguousarray(V.reshape(NV, 8, 128).transpose(2, 0, 1).reshape(128, NV * 8))
    wgu, wo = [], []
    for l in range(4):
        for nm in ("ffn1", "ffn2"):
            wi = A[nm + "_w_in"][l]
            wgu.append(wi.reshape(8, 128, 2, NG, 128 * GH).transpose(3, 1, 2, 0, 4).reshape(NG, 128, -1))
            wt = A[nm + "_w_out"][l]
            wo.append(wt.reshape(NG, GH, 128, D).transpose(0, 2, 1, 3).reshape(NG, 128, -1))
    sh["wgu"] = np.ascontiguousarray(np.stack(wgu, 0))
    sh["wo"] = np.ascontiguousarray(np.stack(wo, 0))

    def sq_tiles(W):
        return W.reshape(8, 128, 8, 128).transpose(2, 1, 0, 3).reshape(8, 128, 1024)

    def head_tiles(W):
        return W.reshape(8, 128, 2, 8, 64).transpose(3, 1, 0, 2, 4).reshape(8, 128, 1024)

    sh["plg"] = np.ascontiguousarray(np.stack([sq_tiles(A["ple_w_gate"][l]) for l in range(4)], 0))
    sh["plp"] = np.ascontiguousarray(np.stack(
        [A["ple_w_proj"][l].reshape(2, 128, 8, 128).transpose(2, 1, 0, 3).reshape(8, 128, 256) for l in range(4)], 0))
    sh["cwin"] = np.ascontiguousarray(np.stack(
        [A["conv_w_in"][l].reshape(8, 128, 2, 8, 128).transpose(3, 1, 2, 0, 4).reshape(8, 128, -1)
         for l in range(2)], 0))
    sh["cwout"] = np.ascontiguousarray(np.stack([sq_tiles(A["conv_w_out"][l]) for l in range(2)], 0))
    sh["wk"] = np.ascontiguousarray(head_tiles(A["w_kv"][:, :D]))
    wv = A["w_kv"][:, D:]
    sh["wv"] = np.ascontiguousarray(wv.reshape(8, 128, 2, 512).transpose(2, 1, 0, 3).reshape(2, 128, -1))
    sh["wq"] = np.ascontiguousarray(np.stack([head_tiles(A["attn_w_q"][l]) for l in range(2)], 0))
    sh["wao"] = np.ascontiguousarray(np.stack([A["attn_w_o"][l].reshape(8, 128, D) for l in range(2)], 0))
    lv = np.stack([np.concatenate([A["attn_lq1"][l], A["attn_lk1"][l], A["attn_lq2"][l], A["attn_lk2"][l]])
                   for l in range(2)], 0).reshape(1, 512)
    sh["lv"] = np.ascontiguousarray(np.broadcast_to(lv, (128, 512)))
    sh["subln"] = np.ascontiguousarray(A["attn_subln"].T)
    sh["relb"] = np.ascontiguousarray(A["rel_bias"])
    bt = _bucket_table()
    oh = np.zeros((32, 384), f)
    mr = np.zeros((1, 384), f)
    for npr in range(384):
        n = npr - 128
        if n < 0:
            mr[0, npr] = -1e30
        else:
            oh[bt[min(n, 255)], npr] = 1.0
    sh["onehot"] = oh
    sh["maskrow"] = mr
    return sh


def prep_core(inp, b):
    x = np.asarray(inp["x"], dtype=np.float32)[b]
    p = np.asarray(inp["p"], dtype=np.float32)[:, b]
    xT = np.ascontiguousarray(x.T.reshape(8, 128, S).transpose(1, 0, 2))
    ppT = np.ascontiguousarray(p.transpose(0, 2, 1).reshape(4, 2, 128, S).transpose(0, 2, 1, 3))
    return {"xT": xT, "ppT": ppT}


_PROG = {}


def run(inputs, stop_after=None, trace=False, debug=False):
    key = (stop_after, debug)
    if key not in _PROG:
        _PROG[key] = build_program(stop_after, debug)
    nc = _PROG[key]
    sh = prep_shared(inputs)
    in_maps = []
    for b in range(8):
        m = dict(sh)
        m.update(prep_core(inputs, b))
        in_maps.append(m)
    res = run_bass_kernel_spmd(nc, in_maps, core_ids=list(range(8)), **({"trace": True} if trace else {}))
    outs = []
    for b in range(8):
        o = np.asarray(res.results[b]["outT"])
        outs.append(o.transpose(1, 0, 2).reshape(D, S).T)
    return np.ascontiguousarray(np.stack(outs, 0).astype(np.float32)), res


def kernel(**inputs):
    out, _ = run(inputs)
    return out
```
